# Optimizing a Trainium2 kernel written in Bass

```python
import math
import jax, jax.numpy as jnp
from jax import lax
import numpy as np

D_MODEL = 2048
BATCH = 4
SEQ = 2048
DEPTH = 2
DEC_BATCH = 128
DEC_SEQ = 1
PAST_LEN = 16384
PAGE_SIZE = 128

N_MIXERS = 2
N_CONV_LAYERS = (DEPTH + 1) // 2
N_MLSTM_LAYERS = DEPTH // 2
D_CONV = D_MODEL
CONV_WIDTH = 31
N_HEADS = 8
D_QK = D_MODEL // 16
D_V = D_MODEL // 8
MLSTM_CHUNK = 128
MLSTM_SPLITS = [N_HEADS * D_QK, 2 * N_HEADS * D_QK, 2 * N_HEADS * D_QK + N_HEADS * D_V,
                2 * N_HEADS * D_QK + 2 * N_HEADS * D_V, 2 * N_HEADS * D_QK + 2 * N_HEADS * D_V + N_HEADS]
MLSTM_PROJ = 2 * N_HEADS * D_QK + 2 * N_HEADS * D_V + 2 * N_HEADS
D_FF = ((8 * D_MODEL // 3 + 255) // 256) * 256
FFN_CONV_WIDTH = 3
D_PLE = 256
ALPHA = (2.0 * DEPTH) ** 0.25
BETA = (8.0 * DEPTH) ** -0.25
LN_EPS = 1e-5

kernel_name = 'conformer_mlstm_convffn_deepnorm_step'


def layer_norm(x, g, b):
    xf = x.astype(jnp.float32)
    mu = xf.mean(-1, keepdims=True)
    var = jnp.mean(jnp.square(xf - mu), -1, keepdims=True)
    return ((xf - mu) * lax.rsqrt(var + LN_EPS) * g.astype(jnp.float32) + b.astype(jnp.float32)).astype(x.dtype)


def rms_norm(x, g):
    xf = x.astype(jnp.float32)
    return (xf * lax.rsqrt(jnp.mean(jnp.square(xf), -1, keepdims=True) + LN_EPS) * g.astype(jnp.float32)).astype(x.dtype)


def causal_dwconv(x, past, w, b):
    width, chans = w.shape
    xx = jnp.concatenate([past.astype(x.dtype), x], axis=1)
    y = lax.conv_general_dilated(xx, w.astype(xx.dtype)[:, None, :], window_strides=(1,), padding='VALID',
                                 dimension_numbers=('NWC', 'WIO', 'NWC'), feature_group_count=chans)
    return y + b, xx[:, xx.shape[1] - (width - 1):]


def conformer_conv(x, past, w_in, b_in, w_dw, b_dw, ln_g, ln_b, w_out, b_out):
    a, gt = jnp.split(x @ w_in + b_in, 2, axis=-1)
    u = a * jax.nn.sigmoid(gt)
    y, new_past = causal_dwconv(u, past, w_dw, b_dw)
    y = jax.nn.silu(layer_norm(y, ln_g, ln_b))
    return y @ w_out + b_out, new_past


def mlstm_chunk_scan(q, k, v, log_i, log_f, c0, n0, m0):
    bsz, slen = q.shape[0], q.shape[1]
    lch = math.gcd(slen, MLSTM_CHUNK)
    nch = slen // lch

    def to_chunks(t):
        return jnp.moveaxis(t.reshape((bsz, nch, lch) + t.shape[2:]), 1, 0)

    causal = jnp.tril(jnp.ones((lch, lch), dtype=bool))

    def step(carry, inp):
        c, n, m = carry
        qc, kc, vc, li, lf = inp
        bh = jnp.moveaxis(jnp.cumsum(lf, axis=1), 1, 2)
        lih = jnp.moveaxis(li, 1, 2)
        d = jnp.where(causal, bh[..., :, None] - bh[..., None, :] + lih[..., None, :], -jnp.inf)
        inter = bh + m[..., None]
        m_tok = jnp.maximum(inter, d.max(-1))
        w_intra = jnp.exp(d - m_tok[..., None])
        w_inter = jnp.exp(inter - m_tok)
        s = jnp.einsum('blhk,bshk->bhls', qc, kc) * w_intra
        num = (jnp.einsum('bhls,bshv->blhv', s, vc)
               + jnp.einsum('blhk,bhkv->blhv', qc, c) * jnp.moveaxis(w_inter, 1, 2)[..., None])
        den = s.sum(-1) + jnp.einsum('blhk,bhk->bhl', qc, n) * w_inter
        den = jnp.maximum(jnp.abs(den), jnp.exp(-m_tok))
        h = num / jnp.moveaxis(den, 1, 2)[..., None]
        m_new = m_tok[..., -1]
        g_state = jnp.exp(inter[..., -1] - m_new)
        g_tok = jnp.exp(bh[..., -1:] - bh + lih - m_new[..., None])
        c_new = g_state[..., None, None] * c + jnp.einsum('bhs,bshk,bshv->bhkv', g_tok, kc, vc)
        n_new = g_state[..., None] * n + jnp.einsum('bhs,bshk->bhk', g_tok, kc)
        return (c_new, n_new, m_new), h

    xs = (to_chunks(q), to_chunks(k), to_chunks(v), to_chunks(log_i), to_chunks(log_f))
    (c, n, m), hs = lax.scan(step, (c0, n0, m0), xs)
    h = jnp.moveaxis(hs, 0, 1).reshape(bsz, slen, N_HEADS, D_V)
    return h, c, n, m


def mlstm_mixer(x, c0, n0, m0, w_in, b_gates, hn_g, w_out, b_out):
    bsz, slen, _ = x.shape
    z = (x @ w_in).astype(jnp.float32)
    q, k, v, o, ig, fg = jnp.split(z, MLSTM_SPLITS, axis=-1)
    q = q.reshape(bsz, slen, N_HEADS, D_QK) * (D_QK ** -0.5)
    k = k.reshape(bsz, slen, N_HEADS, D_QK)
    v = v.reshape(bsz, slen, N_HEADS, D_V)
    bg = b_gates.astype(jnp.float32)
    log_i = ig + bg[0]
    log_f = jax.nn.log_sigmoid(fg + bg[1])
    h, c, n, m = mlstm_chunk_scan(q, k, v, log_i, log_f, c0.astype(jnp.float32),
                                  n0.astype(jnp.float32), m0.astype(jnp.float32))
    mu = h.mean(-1, keepdims=True)
    var = jnp.mean(jnp.square(h - mu), -1, keepdims=True)
    hn = (h - mu) * lax.rsqrt(var + LN_EPS) * hn_g.astype(jnp.float32).reshape(N_HEADS, D_V)
    out = (jax.nn.sigmoid(o) * hn.reshape(bsz, slen, N_HEADS * D_V)).astype(x.dtype)
    return out @ w_out + b_out, c, n, m


def conv_ffn(x, past, w_gate, w_up, w_dw, b_dw, w_down, b_down):
    g, new_past = causal_dwconv(x @ w_gate, past, w_dw, b_dw)
    return (jax.nn.silu(g) * (x @ w_up)) @ w_down + b_down, new_past


def trunk(x, p, conv_buf, c_st, n_st, m_st, ffn_buf,
          cv_w_in, cv_b_in, cv_w_dw, cv_b_dw, cv_ln_g, cv_ln_b, cv_w_out, cv_b_out,
          ml_w_in, ml_b_gates, ml_hn_g, ml_w_out, ml_b_out,
          ln_mix_g, ln_mix_b, ln_ffn_g, ln_ffn_b,
          ff_w_gate, ff_w_up, ff_w_dw, ff_b_dw, ff_w_down, ff_b_down,
          pl_w_proj, pl_g, pl_w_gate):
    new_conv, new_c, new_n, new_m, new_ffn = [], [], [], [], []
    for i in range(DEPTH):
        j = i // N_MIXERS
        if i % N_MIXERS == 0:
            mix, buf = conformer_conv(x, conv_buf[j], cv_w_in[j], cv_b_in[j], cv_w_dw[j], cv_b_dw[j],
                                      cv_ln_g[j], cv_ln_b[j], cv_w_out[j], cv_b_out[j])
            new_conv.append(buf.astype(conv_buf.dtype))
        else:
            mix, c, n, m = mlstm_mixer(x, c_st[j], n_st[j], m_st[j], ml_w_in[j], ml_b_gates[j],
                                       ml_hn_g[j], ml_w_out[j], ml_b_out[j])
            new_c.append(c.astype(c_st.dtype))
            new_n.append(n.astype(n_st.dtype))
            new_m.append(m.astype(m_st.dtype))
        x = layer_norm(ALPHA * x + mix.astype(x.dtype), ln_mix_g[i], ln_mix_b[i])
        f, fbuf = conv_ffn(x, ffn_buf[i], ff_w_gate[i], ff_w_up[i], ff_w_dw[i], ff_b_dw[i],
                           ff_w_down[i], ff_b_down[i])
        new_ffn.append(fbuf.astype(ffn_buf.dtype))
        x = layer_norm(ALPHA * x + f.astype(x.dtype), ln_ffn_g[i], ln_ffn_b[i])
        x = x + jax.nn.sigmoid(x @ pl_w_gate[i]) * rms_norm(p[i] @ pl_w_proj[i], pl_g[i])
    return (x, jnp.stack(new_conv), jnp.stack(new_c), jnp.stack(new_n), jnp.stack(new_m), jnp.stack(new_ffn))


def setup_inputs(seed: int = 0) -> dict:
    key = jax.random.key(seed)
    ks = iter(jax.random.split(key, 48))

    def nrm(shape, scale):
        return jax.random.normal(next(ks), shape, jnp.float32) * scale

    na, nb = N_CONV_LAYERS, N_MLSTM_LAYERS
    f_bias = jnp.broadcast_to(jnp.linspace(3.0, 6.0, N_HEADS, dtype=jnp.float32), (nb, N_HEADS))
    return {
        'x_prompt': nrm((BATCH, SEQ, D_MODEL), 1.0),
        'x_sample': nrm((DEC_BATCH, DEC_SEQ, D_MODEL), 1.0),
        'p_prompt': nrm((DEPTH, BATCH, SEQ, D_PLE), 1.0),
        'p_sample': nrm((DEPTH, DEC_BATCH, DEC_SEQ, D_PLE), 1.0),
        'state_conv': nrm((na, DEC_BATCH, CONV_WIDTH - 1, D_CONV), 0.5),
        'state_mlstm_c': nrm((nb, DEC_BATCH, N_HEADS, D_QK, D_V), 1.0),
        'state_mlstm_n': nrm((nb, DEC_BATCH, N_HEADS, D_QK), 1.0),
        'state_mlstm_m': nrm((nb, DEC_BATCH, N_HEADS), 1.0),
        'state_ffn_conv': nrm((DEPTH, DEC_BATCH, FFN_CONV_WIDTH - 1, D_FF), 1.0),
        'cv_w_in': nrm((na, D_MODEL, 2 * D_CONV), D_MODEL ** -0.5),
        'cv_b_in': nrm((na, 2 * D_CONV), 0.02),
        'cv_w_dw': nrm((na, CONV_WIDTH, D_CONV), CONV_WIDTH ** -0.5),
        'cv_b_dw': nrm((na, D_CONV), 0.02),
        'cv_ln_g': 1.0 + nrm((na, D_CONV), 0.02),
        'cv_ln_b': nrm((na, D_CONV), 0.02),
        'cv_w_out': nrm((na, D_CONV, D_MODEL), BETA * D_CONV ** -0.5),
        'cv_b_out': nrm((na, D_MODEL), 0.02),
        'ml_w_in': nrm((nb, D_MODEL, MLSTM_PROJ), D_MODEL ** -0.5),
        'ml_b_gates': jnp.stack([nrm((nb, N_HEADS), 0.1), f_bias + nrm((nb, N_HEADS), 0.1)], axis=1),
        'ml_hn_g': 1.0 + nrm((nb, N_HEADS * D_V), 0.02),
        'ml_w_out': nrm((nb, N_HEADS * D_V, D_MODEL), BETA * (N_HEADS * D_V) ** -0.5),
        'ml_b_out': nrm((nb, D_MODEL), 0.02),
        'ln_mix_g': 1.0 + nrm((DEPTH, D_MODEL), 0.02),
        'ln_mix_b': nrm((DEPTH, D_MODEL), 0.02),
        'ln_ffn_g': 1.0 + nrm((DEPTH, D_MODEL), 0.02),
        'ln_ffn_b': nrm((DEPTH, D_MODEL), 0.02),
        'ff_w_gate': nrm((DEPTH, D_MODEL, D_FF), D_MODEL ** -0.5),
        'ff_w_up': nrm((DEPTH, D_MODEL, D_FF), D_MODEL ** -0.5),
        'ff_w_dw': nrm((DEPTH, FFN_CONV_WIDTH, D_FF), FFN_CONV_WIDTH ** -0.5),
        'ff_b_dw': nrm((DEPTH, D_FF), 0.02),
        'ff_w_down': nrm((DEPTH, D_FF, D_MODEL), BETA * D_FF ** -0.5),
        'ff_b_down': nrm((DEPTH, D_MODEL), 0.02),
        'pl_w_proj': nrm((DEPTH, D_PLE, D_MODEL), D_PLE ** -0.5),
        'pl_g': 1.0 + nrm((DEPTH, D_MODEL), 0.02),
        'pl_w_gate': nrm((DEPTH, D_MODEL, D_MODEL), D_MODEL ** -0.5),
    }


def reference(x_prompt, x_sample, p_prompt, p_sample, state_conv, state_mlstm_c, state_mlstm_n,
              state_mlstm_m, state_ffn_conv,
              cv_w_in, cv_b_in, cv_w_dw, cv_b_dw, cv_ln_g, cv_ln_b, cv_w_out, cv_b_out,
              ml_w_in, ml_b_gates, ml_hn_g, ml_w_out, ml_b_out,
              ln_mix_g, ln_mix_b, ln_ffn_g, ln_ffn_b,
              ff_w_gate, ff_w_up, ff_w_dw, ff_b_dw, ff_w_down, ff_b_down,
              pl_w_proj, pl_g, pl_w_gate):
    weights = (cv_w_in, cv_b_in, cv_w_dw, cv_b_dw, cv_ln_g, cv_ln_b, cv_w_out, cv_b_out,
               ml_w_in, ml_b_gates, ml_hn_g, ml_w_out, ml_b_out,
               ln_mix_g, ln_mix_b, ln_ffn_g, ln_ffn_b,
               ff_w_gate, ff_w_up, ff_w_dw, ff_b_dw, ff_w_down, ff_b_down,
               pl_w_proj, pl_g, pl_w_gate)
    bp = x_prompt.shape[0]
    dt = x_prompt.dtype
    z_conv = jnp.zeros((N_CONV_LAYERS, bp, CONV_WIDTH - 1, D_CONV), dt)
    z_c = jnp.zeros((N_MLSTM_LAYERS, bp, N_HEADS, D_QK, D_V), dt)
    z_n = jnp.zeros((N_MLSTM_LAYERS, bp, N_HEADS, D_QK), dt)
    z_m = jnp.zeros((N_MLSTM_LAYERS, bp, N_HEADS), dt)
    z_ffn = jnp.zeros((DEPTH, bp, FFN_CONV_WIDTH - 1, D_FF), dt)
    y_prompt, conv_p, c_p, n_p, m_p, ffn_p = trunk(x_prompt, p_prompt, z_conv, z_c, z_n, z_m, z_ffn, *weights)
    y_sample, conv_s, c_s, n_s, m_s, ffn_s = trunk(x_sample, p_sample, state_conv, state_mlstm_c, state_mlstm_n,
                                                   state_mlstm_m, state_ffn_conv, *weights)
    return (y_prompt, y_sample, conv_p, conv_s, c_p, n_p, m_p, c_s, n_s, m_s, ffn_p, ffn_s)
```

```python
import contextlib
import numpy as np
import concourse.bass as bass
import concourse.mybir as mybir
from concourse.bass_utils import run_bass_kernel_spmd

F32 = mybir.dt.float32
BF16 = mybir.dt.bfloat16
AF = mybir.ActivationFunctionType
ALU = mybir.AluOpType
AX = mybir.AxisListType

D = 2048
DFF = 5632
NT16 = 16
NFT = 44
HALO = 32
NP = 1024
NS = 16
NC = HALO + NP + NS
C_P0 = HALO
C_S0 = HALO + NP
CBS = [(0, 512), (512, 512), (1024, 48)]
ALPHA = 4.0 ** 0.25
LN_EPS = 1e-5
CW = 31
NH = 8
DK = 128
DV = 256
MLP = 6160

SAME_ENGINE_SYNC = True
N_DMA_SEMS = 40
WS_ELEMS = 8192
N_WS = 2


class Buf:
    __slots__ = ("name", "w", "re", "rd", "excl")
    registry = []
    fence_ins = None

    def __init__(self, name=""):
        self.name = name
        self.w = Buf.fence_ins
        self.re = {}
        self.rd = []
        self.excl = False
        Buf.registry.append(self)


class Ins:
    __slots__ = ("eng", "emit", "deps", "signal", "ticket", "is_dma", "sem_i", "prev_dma")


class Sched:
    ENGS = ("pe", "act", "dve", "pool", "sp")

    def __init__(self, nc):
        self.nc = nc
        self.lists = {e: [] for e in self.ENGS}
        self.dma_rr = 0
        self.dma_last = [None] * N_DMA_SEMS
        self.dma_cnt = [0] * N_DMA_SEMS

    def op(self, eng, emit, reads=(), writes=(), dma=False):
        ins = Ins()
        ins.eng = eng
        ins.emit = emit
        ins.signal = False
        ins.ticket = None
        ins.is_dma = dma
        ins.prev_dma = None
        ins.sem_i = -1
        deps = set()
        raw = set()
        ex = [b for b in reads if b.excl and b not in writes]
        if ex:
            writes = list(writes) + ex
        for b in reads:
            if b.w is not None:
                deps.add(b.w)
                raw.add(b.w)
        for b in writes:
            if b.w is not None:
                deps.add(b.w)
            deps.update(b.re.values())
            deps.update(b.rd)
        for b in reads:
            if dma:
                b.rd.append(ins)
            else:
                b.re[eng] = ins
        for b in writes:
            b.w = ins
            b.re = {}
            b.rd = []
        deps.discard(ins)
        if dma:
            i = self.dma_rr
            self.dma_rr = (self.dma_rr + 1) % N_DMA_SEMS
            ins.sem_i = i
            ins.prev_dma = self.dma_last[i]
            self.dma_last[i] = ins
            self.dma_cnt[i] += 16
            ins.ticket = self.dma_cnt[i]
            ins.signal = True
        fin = []
        for d in deps:
            if d.is_dma:
                fin.append(d)
            elif d.eng == eng and not dma:
                if SAME_ENGINE_SYNC and eng != "pe" and d in raw:
                    d.signal = True
                    fin.append(d)
            else:
                d.signal = True
                fin.append(d)
        ins.deps = fin
        self.lists[eng].append(ins)
        return ins

    def finalize(self):
        nc = self.nc
        for e in self.ENGS:
            c = 0
            for ins in self.lists[e]:
                if ins.is_dma:
                    continue
                if ins.signal:
                    c += 1
                    ins.ticket = c
        with contextlib.ExitStack() as st:
            esem = {e: st.enter_context(nc.semaphore("s_" + e)) for e in self.ENGS}
            dsem = [st.enter_context(nc.semaphore("d%d" % i)) for i in range(N_DMA_SEMS)]
            block = st.enter_context(nc.Block())
            lists = self.lists
            dma_cnt = self.dma_cnt

            def run(ename, eobj):
                waited = {}
                for ins in lists[ename]:
                    need = {}
                    for d in ins.deps:
                        key = ("d", d.sem_i) if d.is_dma else ("e", d.eng)
                        if d.ticket > need.get(key, 0):
                            need[key] = d.ticket
                    if ins.is_dma and ins.prev_dma is not None:
                        key = ("d", ins.sem_i)
                        if ins.prev_dma.ticket > need.get(key, 0):
                            need[key] = ins.prev_dma.ticket
                    for key, t in need.items():
                        if waited.get(key, 0) >= t:
                            continue
                        waited[key] = t
                        sem = dsem[key[1]] if key[0] == "d" else esem[key[1]]
                        eobj.wait_ge(sem, t)
                    bi = ins.emit(eobj)
                    if ins.is_dma:
                        bi.then_inc(dsem[ins.sem_i], 16)
                    elif ins.signal:
                        bi.then_inc(esem[ename], 1)
                if ename == "sp":
                    for i in range(N_DMA_SEMS):
                        if dma_cnt[i] > 0:
                            eobj.wait_ge(dsem[i], dma_cnt[i])

            @block.tensor
            def _(e):
                run("pe", e)

            @block.scalar
            def _(e):
                run("act", e)

            @block.vector
            def _(e):
                run("dve", e)

            @block.gpsimd
            def _(e):
                run("pool", e)

            @block.sync
            def _(e):
                run("sp", e)


def vec_layout():
    off = {}
    n = 0

    def add(name, ncols):
        nonlocal n
        off[name] = n
        n += ncols

    add("cv_b_in", 32)
    add("cv_w_dw", 16 * CW)
    add("cv_b_dw", 16)
    add("cv_ln_g", 16)
    add("cv_ln_b", 16)
    add("cv_b_out", 16)
    add("ml_b_out", 16)
    for l in range(2):
        add("ln_mix_g%d" % l, 16)
        add("ln_mix_b%d" % l, 16)
        add("ln_ffn_g%d" % l, 16)
        add("ln_ffn_b%d" % l, 16)
        add("ff_w_dw%d" % l, NFT * 3)
        add("ff_b_dw%d" % l, NFT)
        add("ff_b_down%d" % l, 16)
        add("pl_g%d" % l, 16)
    add("hmask", 1)
    add("ml_bg", 1)
    return off, n


VOFF, NV = vec_layout()


def fm(v):
    v = np.asarray(v, np.float32)
    return np.ascontiguousarray(v.reshape(-1, 128).T)


def pack_vecs(inp, hflag):
    V = np.zeros((128, NV), np.float32)

    def put(name, arr):
        V[:, VOFF[name]:VOFF[name] + arr.shape[1]] = arr

    put("cv_b_in", fm(inp["cv_b_in"][0]))
    w = inp["cv_w_dw"][0]
    put("cv_w_dw", np.ascontiguousarray(w.reshape(CW, 16, 128).transpose(2, 1, 0).reshape(128, 16 * CW)))
    put("cv_b_dw", fm(inp["cv_b_dw"][0]))
    put("cv_ln_g", fm(inp["cv_ln_g"][0]))
    put("cv_ln_b", fm(inp["cv_ln_b"][0]))
    put("cv_b_out", fm(inp["cv_b_out"][0]))
    put("ml_b_out", fm(inp["ml_b_out"][0]))
    for l in range(2):
        put("ln_mix_g%d" % l, fm(inp["ln_mix_g"][l]))
        put("ln_mix_b%d" % l, fm(inp["ln_mix_b"][l]))
        put("ln_ffn_g%d" % l, fm(inp["ln_ffn_g"][l]))
        put("ln_ffn_b%d" % l, fm(inp["ln_ffn_b"][l]))
        w = inp["ff_w_dw"][l]
        put("ff_w_dw%d" % l, np.ascontiguousarray(w.reshape(3, NFT, 128).transpose(2, 1, 0).reshape(128, NFT * 3)))
        put("ff_b_dw%d" % l, fm(inp["ff_b_dw"][l]))
        put("ff_b_down%d" % l, fm(inp["ff_b_down"][l]))
        put("pl_g%d" % l, fm(inp["pl_g"][l]))
    V[:, VOFF["hmask"]] = hflag
    V[0:16, VOFF["ml_bg"]] = np.asarray(inp["ml_b_gates"][0], np.float32).reshape(16)
    return V


class Prog:
    def __init__(self, n_layers=2):
        self.n_layers = n_layers
        nc = self.nc = bass.Bass("TRN2", target_bir_lowering=False)
        self.S = Sched(nc)
        dt = nc.dram_tensor
        self.xT2 = dt("xT", [2, D, NC], F32, kind="ExternalInput").ap()
        self.pT2 = dt("pT", [2, 2, 256, NC], F32, kind="ExternalInput").ap()
        self.vecs_d = dt("vecs", [128, NV], F32, kind="ExternalInput").ap()
        self.convst = dt("convst", [128, 16, NS, 30], F32, kind="ExternalInput").ap()
        self.ffnst = dt("ffnst", [2, 128, NFT, NS, 2], F32, kind="ExternalInput").ap()
        self.cv_w_in = dt("cv_w_in", [D, 2 * D], F32, kind="ExternalInput").ap()
        self.cv_w_out = dt("cv_w_out", [D, D], F32, kind="ExternalInput").ap()
        self.ff_w_gate = dt("ff_w_gate", [2, D, DFF], F32, kind="ExternalInput").ap()
        self.ff_w_up = dt("ff_w_up", [2, D, DFF], F32, kind="ExternalInput").ap()
        self.ff_w_down = dt("ff_w_down", [2, DFF, D], F32, kind="ExternalInput").ap()
        self.pl_w_proj = dt("pl_w_proj", [2, 256, D], F32, kind="ExternalInput").ap()
        self.pl_w_gate = dt("pl_w_gate", [2, D, D], F32, kind="ExternalInput").ap()
        self.ml_w_in = dt("ml_w_in", [D, MLP], F32, kind="ExternalInput").ap()
        self.ml_w_out = dt("ml_w_out", [D, D], F32, kind="ExternalInput").ap()
        self.consts_d = dt("consts", [128, 5 * 128], F32, kind="ExternalInput").ap()
        self.bg_rep = dt("bg_rep", [128, 16], F32, kind="ExternalInput").ap()
        self.hg_rep = dt("hg_rep", [128, D], F32, kind="ExternalInput").ap()
        self.xscr = dt("xscr", [128, 16, NC], F32).ap()
        self.cS = dt("cS", [NS, NH, DK, DV], F32, kind="ExternalInput").ap()
        self.nS = dt("nS", [NS, NH, DK], F32, kind="ExternalInput").ap()
        self.mS = dt("mS", [NS, NH], F32, kind="ExternalInput").ap()
        self.id16 = dt("id16", [128, NS * NS], F32, kind="ExternalInput").ap()
        self.cSo = dt("cSo", [NS, NH, DK, DV], F32, kind="ExternalOutput").ap()
        self.nSo = dt("nSo", [NS, NH, DK], F32, kind="ExternalOutput").ap()
        self.mSo = dt("mSo", [NS, NH], F32, kind="ExternalOutput").ap()
        self.cP = dt("cP", [128, NH, DV + 1], F32, kind="ExternalOutput").ap()
        self.mP = dt("mP", [128, NH], F32, kind="ExternalOutput").ap()
        self.yT = dt("yT", [128, 16, NC], F32, kind="ExternalOutput").ap()
        self.convP = dt("convP", [128, 16, 30], F32, kind="ExternalOutput").ap()
        self.convS = dt("convS", [128, 16, NS, 30], F32, kind="ExternalOutput").ap()
        self.ffnP = dt("ffnP", [2, 128, NFT, 2], F32, kind="ExternalOutput").ap()
        self.ffnS = dt("ffnS", [2, 128, NFT, NS, 2], F32, kind="ExternalOutput").ap()

        sb = nc.alloc_sbuf_tensor
        self.XB = sb("XB", [128, 16, NC], BF16)
        self.R = sb("R", [128, 16, NC], F32)
        self.bXB = [Buf("XB%d" % i) for i in range(16)]
        self.bR = [Buf("R%d" % i) for i in range(16)]
        self.WS = [sb("WS%d" % i, [128, WS_ELEMS], BF16) for i in range(N_WS)]
        self.bWS = [Buf("WS%d" % i) for i in range(N_WS)]
        self.ws_rr = 0
        self.VEC = sb("VEC", [128, NV], F32)
        self.bVEC = Buf("VEC")
        self.ONES = sb("ONES", [128, 128], BF16)
        self.bONES = Buf("ONES")
        self.bT1, self.bT2 = Buf("T1"), Buf("T2")
        self.bYB = [Buf() for _ in range(2)]
        self.bYQ = [Buf() for _ in range(2)]
        self.UH = sb("UH", [128, 16, 30], F32)
        self.bUH = [Buf() for _ in range(16)]
        self.XH = [sb("XH%d" % l, [128, 16, 2], BF16) for l in range(2)]
        self.bXH = [Buf() for _ in range(2)]
        self.CSTP = sb("CSTP", [128, NH, DV + 1], F32)
        self.bCSTP = [Buf() for _ in range(NH)]
        self.MPP = sb("MPP", [128, NH], F32)
        self.bMPP = Buf()
        self.tmp_base = None
        self.TMPB = 54 * 1024
        self.LNB = 2 * NC * 4 + 4 * NC * 2
        self.TMP = sb("TMP", [128, self.TMPB // 4], F32)
        lo = (self.TMPB - self.LNB) // 4
        self.T1 = self.TMP[:, lo:lo + NC]
        self.T2 = self.TMP[:, lo + NC:lo + 2 * NC]
        yb = self.TMP[:, lo + 2 * NC:lo + 4 * NC].bitcast(BF16)
        self.YB = [yb[:, i * NC:(i + 1) * NC] for i in range(2)]
        self.YQ = [yb[:, (2 + i) * NC:(3 + i) * NC] for i in range(2)]
        self.PG = [nc.alloc_psum_tensor("pg%d" % i, [128, 1536], F32) for i in range(2)]
        self.bPG = [Buf("pg%d" % i) for i in range(2)]
        self.PX = [nc.alloc_psum_tensor("px%d" % i, [128, 512], F32) for i in range(2)]
        self.bPX = [Buf("px%d" % i) for i in range(2)]
        for b_ in self.bPG + self.bPX:
            b_.excl = True
        self.FEN0 = sb("FEN0", [128, 8], F32)
        self.IDF = sb("IDF", [128, 128], F32)
        self.IDB16 = sb("IDB16", [128, 128], BF16)
        self.bIDB16 = Buf("IDB16")
        Buf.registry = []
        Buf.fence_ins = None

    def phase_fence(self, extra=(), clear=True):
        old = list(Buf.registry)
        if clear:
            Buf.registry = []
        ins = self.S.op("dve", lambda e: e.memset(self.FEN0[:], 0.0), writes=list(old) + list(extra))
        Buf.fence_ins = ins

    def V(self, name, col):
        c = VOFF[name] + col
        return self.VEC[:, c:c + 1]

    def tmp_alloc(self, base=None, cap=None):
        if base is None:
            base = self.TMP
            cap = self.TMPB - self.LNB if cap is None else cap
        state = {"off": 0}

        def alloc(shape, dtype):
            esz = 4 if dtype == F32 else 2
            n = int(np.prod(shape[1:]))
            nbytes = (n * esz + 31) // 32 * 32
            o = state["off"]
            assert o + nbytes <= cap, ("TMP overflow", o + nbytes)
            state["off"] = o + nbytes
            ap = base[:, o // 4:(o + nbytes) // 4]
            if dtype != F32:
                ap = ap.bitcast(BF16)
            ap = ap[:, 0:n]
            if len(shape) == 3:
                ap = ap.rearrange("p (a b) -> p a b", a=shape[1])
            elif len(shape) == 4:
                ap = ap.rearrange("p (a b c) -> p a b c", a=shape[1], b=shape[2])
            if shape[0] != 128:
                ap = ap[0:shape[0]]
            return ap
        return alloc

    def load_w(self, parts, nbytes_check=None):
        S = self.S
        slot = self.ws_rr
        self.ws_rr = (self.ws_rr + 1) % N_WS
        for dstf, src in parts:
            dst = dstf(self.WS[slot])
            S.op("pool", lambda e, dst=dst, src=src: e.dma_start(out=dst, in_=src),
                 writes=[self.bWS[slot]], dma=True)
        return slot

    def mm_group(self, pgk, lhsT_fn, rhs_fn, nk, reads, cbs=None, reads_k=None):
        S = self.S
        pg = self.PG[pgk]
        cbs = CBS if cbs is None else cbs
        for k in range(nk):
            for (c0, cn) in cbs:
                S.op("pe", lambda e, k=k, c0=c0, cn=cn: e.matmul(
                    pg[:, c0:c0 + cn], lhsT=lhsT_fn(k), rhs=rhs_fn(k, c0, cn),
                    start=(k == 0), stop=(k == nk - 1)),
                    reads=(reads if reads_k is None else list(reads) + [reads_k[k]]), writes=[self.bPG[pgk]])

    def setup(self):
        S = self.S
        ps = self.ps
        if ps == 0:
            S.op("sp", lambda e: e.dma_start(out=self.VEC[:], in_=self.vecs_d), writes=[self.bVEC], dma=True)
            S.op("dve", lambda e: e.memset(self.ONES[:], 1.0 / D), writes=[self.bONES])
            S.op("sp", lambda e: e.dma_start(out=self.IDF[:], in_=self.consts_d[:, 0:128]), writes=[self.bIDB16], dma=True)
            S.op("act", lambda e: e.activation(out=self.IDB16[:], in_=self.IDF[:], func=AF.Copy),
                 reads=[self.bIDB16], writes=[self.bIDB16])
        xv = self.xT2[ps].rearrange("(t p) n -> p t n", p=128)
        for i in range(0, 16, 4):
            S.op("pool", lambda e, i=i: e.dma_start(out=self.XB[:, i:i + 4, :], in_=xv[:, i:i + 4, :]),
                 writes=self.bXB[i:i + 4], dma=True)

    def layer_norm(self, gname, bname, outs, center=True, src_fn=None):
        S = self.S
        R, bR = self.R, self.bR
        if getattr(self, "ln_needs_fence", False):
            self.phase_fence(extra=[self.bT1, self.bT2] + self.bYB + self.bYQ, clear=False)
            self.ln_needs_fence = False
        for i in range(16):
            p = i % 2
            if center:
                S.op("dve", lambda e, i=i, p=p: e.tensor_copy(out=self.YB[p], in_=R[:, i, :]),
                     reads=[bR[i]], writes=[self.bYB[p]])
            S.op("act", lambda e, i=i, p=p: e.activation(out=self.YQ[p], in_=R[:, i, :], func=AF.Square),
                 reads=[bR[i]], writes=[self.bYQ[p]])
            for (c0, cn) in CBS:
                if center:
                    S.op("pe", lambda e, i=i, p=p, c0=c0, cn=cn: e.matmul(
                        self.PG[0][:, c0:c0 + cn], lhsT=self.ONES[:], rhs=self.YB[p][:, c0:c0 + cn],
                        start=(i == 0), stop=(i == 15)), reads=[self.bYB[p], self.bONES], writes=[self.bPG[0]])
                S.op("pe", lambda e, i=i, p=p, c0=c0, cn=cn: e.matmul(
                    self.PG[1][:, c0:c0 + cn], lhsT=self.ONES[:], rhs=self.YQ[p][:, c0:c0 + cn],
                    start=(i == 0), stop=(i == 15)), reads=[self.bYQ[p], self.bONES], writes=[self.bPG[1]])
        self.finish_stats(center)
        for i in range(16):
            S.op("dve", lambda e, i=i: e.tensor_tensor(out=R[:, i, :], in0=R[:, i, :], in1=self.T2, op=ALU.mult),
                 reads=[bR[i], self.bT2], writes=[bR[i]])
            if center:
                S.op("dve", lambda e, i=i: e.tensor_tensor(out=R[:, i, :], in0=R[:, i, :], in1=self.T1, op=ALU.add),
                     reads=[bR[i], self.bT1], writes=[bR[i]])
            for (dst_fn, buf_fn, func, is_R) in sorted(outs, key=lambda o: o[3]):
                S.op("act", lambda e, i=i, dst_fn=dst_fn, func=func: e.activation(
                    out=dst_fn(i), in_=R[:, i, :], func=func,
                    scale=self.V(gname, i), bias=self.V(bname, i)),
                    reads=[bR[i], self.bVEC], writes=[buf_fn(i)])

    def rsqrt_T2(self):
        S = self.S
        T2 = self.T2
        S.op("act", lambda e: e.activation(out=T2, in_=T2, func=AF.Sqrt), reads=[self.bT2], writes=[self.bT2])
        S.op("dve", lambda e: e.reciprocal(out=T2, in_=T2), reads=[self.bT2], writes=[self.bT2])

    def finish_stats(self, center):
        S = self.S
        T1, T2 = self.T1, self.T2
        if center:
            S.op("act", lambda e: e.activation(out=T1, in_=self.PG[0][:, 0:NC], func=AF.Copy),
                 reads=[self.bPG[0]], writes=[self.bT1])
            S.op("dve", lambda e: e.tensor_tensor(out=T2, in0=T1, in1=T1, op=ALU.mult),
                 reads=[self.bT1], writes=[self.bT2])
            S.op("dve", lambda e: e.tensor_tensor(out=T2, in0=self.PG[1][:, 0:NC], in1=T2, op=ALU.subtract),
                 reads=[self.bPG[1], self.bT2], writes=[self.bT2])
            S.op("dve", lambda e: e.tensor_scalar(out=T2, in0=T2, scalar1=LN_EPS, scalar2=None, op0=ALU.add),
                 reads=[self.bT2], writes=[self.bT2])
            self.rsqrt_T2()
            S.op("dve", lambda e: e.scalar_tensor_tensor(out=T1, in0=T1, scalar=-1.0, in1=T2,
                                                         op0=ALU.mult, op1=ALU.mult),
                 reads=[self.bT1, self.bT2], writes=[self.bT1])
        else:
            S.op("dve", lambda e: e.tensor_scalar(out=T2, in0=self.PG[1][:, 0:NC], scalar1=LN_EPS, scalar2=None,
                                                  op0=ALU.add),
                 reads=[self.bPG[1]], writes=[self.bT2])
            self.rsqrt_T2()

    def conformer(self):
        S = self.S
        self.phase_fence()
        R, bR, XB, bXB = self.R, self.bR, self.XB, self.bXB
        al = self.tmp_alloc()
        UT = [al([128, NC], F32) for _ in range(2)]
        UB = [al([128, NC], BF16) for _ in range(2)]
        SGT = [al([128, NC], F32) for _ in range(1)]
        XT = UT
        CS = [al([128, NS, 30], F32) for _ in range(1)]
        NCS = [al([128, NS, 30], F32) for _ in range(1)]
        RED = al([128, NS], F32)
        DGW = [al([128, CW, 128], BF16) for _ in range(2)]
        bUT, bUB, bSGT = [Buf() for _ in range(2)], [Buf() for _ in range(2)], [Buf() for _ in range(1)]
        bXT = bUT
        bCS, bNCS, bRED = [Buf() for _ in range(1)], [Buf() for _ in range(1)], Buf()
        bDGW = [Buf() for _ in range(2)]
        wv = self.cv_w_in.rearrange("(kt p) c -> p kt c", p=128)
        hm = self.V("hmask", 0)
        st = {}

        def proj(i):
            p = i % 2
            if i % 2 == 0:
                slot = self.load_w([
                    (lambda t: t[:, :].rearrange("p (kt g c) -> p kt g c", kt=16, g=2)[:, :, 0, :], wv[:, :, i * 128:i * 128 + 256]),
                    (lambda t: t[:, :].rearrange("p (kt g c) -> p kt g c", kt=16, g=2)[:, :, 1, :], wv[:, :, D + i * 128:D + i * 128 + 256]),
                ])
                st["wt"] = self.WS[slot][:, :].rearrange("p (kt g c) -> p kt g c", kt=16, g=2)
                st["bw"] = self.bWS[slot]
            wt, bw = st["wt"], st["bw"]
            j = i % 2
            self.mm_group(0, lambda k, wt=wt, j=j: wt[:, k, 0, j * 128:(j + 1) * 128],
                          lambda k, c0, cn: XB[:, k, c0:c0 + cn], 16, [bw], reads_k=bXB)
            self.mm_group(1, lambda k, wt=wt, j=j: wt[:, k, 1, j * 128:(j + 1) * 128],
                          lambda k, c0, cn: XB[:, k, c0:c0 + cn], 16, [bw], reads_k=bXB)
            S.op("act", lambda e, i=i: e.activation(out=SGT[0][:], in_=self.PG[1][:, 0:NC], func=AF.Sigmoid,
                                                    bias=self.V("cv_b_in", 16 + i), scale=1.0),
                 reads=[self.bPG[1], self.bVEC], writes=[bSGT[0]])
            S.op("dve", lambda e, i=i, p=p: e.scalar_tensor_tensor(
                out=UT[p][:], in0=self.PG[0][:, 0:NC], scalar=self.V("cv_b_in", i), in1=SGT[0][:],
                op0=ALU.add, op1=ALU.mult), reads=[self.bPG[0], bSGT[0], self.bVEC], writes=[bUT[p]])
            if self.ps == 0:
                S.op("dve", lambda e, p=p: e.memset(UT[p][:, 0:HALO], 0.0), writes=[bUT[p]])
            else:
                S.op("dve", lambda e, p=p: e.memset(UT[p][:, 0:2], 0.0), writes=[bUT[p]])
                S.op("dve", lambda e, p=p, i=i: e.tensor_scalar(out=UT[p][:, 2:HALO], in0=self.UH[:, i, :], scalar1=hm,
                                                              scalar2=None, op0=ALU.mult),
                     reads=[self.bUH[i], self.bVEC], writes=[bUT[p]])
            S.op("act", lambda e, p=p: e.activation(out=UB[p][:], in_=UT[p][:], func=AF.Copy), reads=[bUT[p]], writes=[bUB[p]])
            wc = VOFF["cv_w_dw"] + i * CW
            S.op("dve", lambda e, p=p, wc=wc: e.tensor_tensor(
                out=DGW[p][:], in0=self.IDB16[:][:, None, :].broadcast_to([128, CW, 128]),
                in1=self.VEC[:, wc:wc + CW][:, :, None].broadcast_to([128, CW, 128]), op=ALU.mult),
                reads=[self.bIDB16, self.bVEC], writes=[bDGW[p]])

        def conv(i):
            p = i % 2
            wc = VOFF["cv_w_dw"] + i * CW
            NDVE = 10
            for j2 in range(NDVE):
                wcol = self.VEC[:, wc + j2:wc + j2 + 1]
                if j2 == 0:
                    S.op("dve", lambda e, i=i, p=p, wcol=wcol: e.tensor_scalar(
                        out=R[:, i, C_P0:C_S0], in0=UT[p][:, C_P0 - 30:C_S0 - 30], scalar1=wcol,
                        scalar2=self.V("cv_b_dw", i), op0=ALU.mult, op1=ALU.add),
                        reads=[bUT[p], self.bVEC], writes=[bR[i]])
                else:
                    S.op("dve", lambda e, i=i, p=p, wcol=wcol, j2=j2: e.scalar_tensor_tensor(
                        out=R[:, i, C_P0:C_S0], in0=UT[p][:, C_P0 - 30 + j2:C_S0 - 30 + j2], scalar=wcol,
                        in1=R[:, i, C_P0:C_S0], op0=ALU.mult, op1=ALU.add),
                        reads=[bUT[p], bR[i], self.bVEC], writes=[bR[i]])
            for j2 in range(NDVE, CW):
                for q, c0 in enumerate((C_P0, C_P0 + 512)):
                    S.op("pe", lambda e, p=p, j2=j2, q=q, c0=c0: e.matmul(
                        self.PX[q][:, :], lhsT=DGW[p][:, j2, :], rhs=UB[p][:, c0 - 30 + j2:c0 - 30 + j2 + 512],
                        start=(j2 == NDVE), stop=(j2 == CW - 1)), reads=[bDGW[p], bUB[p]], writes=[self.bPX[q]])
            for q, c0 in enumerate((C_P0, C_P0 + 512)):
                S.op("dve", lambda e, i=i, q=q, c0=c0: e.tensor_tensor(out=R[:, i, c0:c0 + 512], in0=self.PX[q][:, :],
                                                                     in1=R[:, i, c0:c0 + 512], op=ALU.add),
                     reads=[self.bPX[q], bR[i]], writes=[bR[i]])
            S.op("dve", lambda e, i=i: e.memset(R[:, i, 0:C_P0], 0.0), writes=[bR[i]])
            S.op("sp", lambda e, i=i: e.dma_start(out=CS[0][:], in_=self.convst[:, i]), writes=[bCS[0]], dma=True)
            if self.ps == 1:
                S.op("act", lambda e: e.activation(out=NCS[0][:, :, 0:29], in_=CS[0][:, :, 1:30], func=AF.Copy),
                     reads=[bCS[0]], writes=[bNCS[0]])
            wrow = self.VEC[:, wc:wc + 30]
            S.op("dve", lambda e, wrow=wrow: e.tensor_tensor(
                out=CS[0][:], in0=CS[0][:], in1=wrow[:, None, :].broadcast_to([128, NS, 30]), op=ALU.mult),
                reads=[bCS[0], self.bVEC], writes=[bCS[0]])
            S.op("dve", lambda e: e.tensor_reduce(out=RED[:], in_=CS[0][:], axis=AX.X, op=ALU.add),
                 reads=[bCS[0]], writes=[bRED])
            S.op("dve", lambda e, i=i, p=p, wc=wc: e.scalar_tensor_tensor(
                out=R[:, i, C_S0:NC], in0=UT[p][:, C_S0:NC], scalar=self.VEC[:, wc + 30:wc + 31], in1=RED[:],
                op0=ALU.mult, op1=ALU.add), reads=[bUT[p], bRED, self.bVEC], writes=[bR[i]])
            S.op("dve", lambda e, i=i: e.tensor_scalar(
                out=R[:, i, C_S0:NC], in0=R[:, i, C_S0:NC], scalar1=self.V("cv_b_dw", i), scalar2=None, op0=ALU.add),
                reads=[bR[i], self.bVEC], writes=[bR[i]])
            if self.ps == 1:
                S.op("act", lambda e, p=p: e.activation(out=NCS[0][:, :, 29], in_=UT[p][:, C_S0:NC], func=AF.Copy),
                     reads=[bUT[p]], writes=[bNCS[0]])
                S.op("sp", lambda e, i=i: e.dma_start(out=self.convS[:, i], in_=NCS[0][:]), reads=[bNCS[0]], dma=True)
                S.op("sp", lambda e, i=i, p=p: e.dma_start(out=self.convP[:, i, :], in_=UT[p][:, C_S0 - 30:C_S0]),
                     reads=[bUT[p]], dma=True)
            else:
                S.op("act", lambda e, i=i, p=p: e.activation(out=self.UH[:, i, :], in_=UT[p][:, C_S0 - 30:C_S0], func=AF.Copy),
                     reads=[bUT[p]], writes=[self.bUH[i]])

        proj(0)
        for i in range(16):
            if i + 1 < 16:
                proj(i + 1)
            conv(i)
        self.layer_norm("cv_ln_g", "cv_ln_b", [(lambda i: XB[:, i, :], lambda i: bXB[i], AF.Silu, 0)])
        wo = self.cv_w_out.rearrange("(kt p) c -> p kt c", p=128)
        xv = self.xT2[self.ps].rearrange("(t p) n -> p t n", p=128)
        for m in range(16):
            p = m % 2
            if m % 2 == 0:
                slot = self.load_w([(lambda t: t[:, 0:4096].rearrange("p (kt c) -> p kt c", kt=16),
                                     wo[:, :, m * 128:m * 128 + 256])])
                wt = self.WS[slot][:, 0:4096].rearrange("p (kt c) -> p kt c", kt=16)
                bw = self.bWS[slot]
            j = m % 2
            S.op("sp", lambda e, m=m, p=p: e.dma_start(out=XT[p][:], in_=xv[:, m, :]), writes=[bXT[p]], dma=True)
            S.op("act", lambda e, m=m, p=p: e.activation(out=XT[p][:], in_=XT[p][:], func=AF.Identity,
                                                         scale=ALPHA, bias=self.V("cv_b_out", m)),
                 reads=[bXT[p], self.bVEC], writes=[bXT[p]])
            self.mm_group(p, lambda k, wt=wt, j=j: wt[:, k, j * 128:(j + 1) * 128],
                          lambda k, c0, cn: XB[:, k, c0:c0 + cn], 16, [bw], reads_k=bXB)
            S.op("dve", lambda e, m=m, p=p: e.tensor_tensor(out=R[:, m, :], in0=self.PG[p][:, 0:NC], in1=XT[p][:],
                                                            op=ALU.add),
                 reads=[self.bPG[p], bXT[p]], writes=[bR[m]])

    def post_mix_ln(self, l, mask_halo):
        S = self.S
        R, bR, XB, bXB = self.R, self.bR, self.XB, self.bXB
        self.layer_norm("ln_mix_g%d" % l, "ln_mix_b%d" % l,
                        [(lambda i: XB[:, i, :], lambda i: bXB[i], AF.Identity, 0),
                         (lambda i: R[:, i, :], lambda i: bR[i], AF.Identity, 1)])
        hm = self.V("hmask", 0)
        if self.ps == 0:
            S.op("act", lambda e: e.activation(out=self.XH[l][:], in_=XB[:, :, C_S0 - 2:C_S0], func=AF.Copy),
                 reads=list(bXB), writes=[self.bXH[l]])
            S.op("dve", lambda e: e.memset(XB[:, :, 0:HALO], 0.0), writes=list(bXB))
        else:
            S.op("dve", lambda e: e.tensor_scalar(out=XB[:, :, HALO - 2:HALO], in0=self.XH[l][:], scalar1=hm,
                                                  scalar2=None, op0=ALU.mult),
                 reads=[self.bXH[l], self.bVEC], writes=list(bXB))

    def ffn(self, l):
        S = self.S
        self.phase_fence()
        R, bR, XB, bXB = self.R, self.bR, self.XB, self.bXB
        al = self.tmp_alloc()
        GMAX = 8
        HB = al([128, GMAX, NC], BF16)
        GS = [al([128, NC], F32) for _ in range(2)]
        GC = [al([128, NC], F32) for _ in range(2)]
        FS = al([128, GMAX, NS, 2], F32)
        NFS = al([128, GMAX, NS, 2], F32)
        FP = al([128, GMAX, 2], F32)
        bHB = [Buf() for _ in range(GMAX)]
        bGS, bGC = [Buf() for _ in range(2)], [Buf() for _ in range(2)]
        bFS, bNFS, bFP = Buf(), Buf(), Buf()
        wg = self.ff_w_gate[l].rearrange("(kt p) c -> p kt c", p=128)
        wu = self.ff_w_up[l].rearrange("(kt p) c -> p kt c", p=128)
        wd = self.ff_w_down[l].rearrange("(kt p) c -> p kt c", p=128)
        for m in range(16):
            S.op("act", lambda e, m=m: e.activation(out=R[:, m, :], in_=R[:, m, :], func=AF.Identity, scale=ALPHA,
                                                    bias=self.V("ff_b_down%d" % l, m)),
                 reads=[bR[m], self.bVEC], writes=[bR[m]])
        S.op("dve", lambda e: e.memset(HB[:, :, 0:HALO], 0.0), writes=bHB)
        for p in range(2):
            S.op("dve", lambda e, p=p: e.memset(GC[p][:, 0:HALO], 0.0), writes=[bGC[p]])
        groups = [(0, 8), (8, 8), (16, 8), (24, 8), (32, 8), (40, 4)]
        wdw = VOFF["ff_w_dw%d" % l]
        for (f0, gn) in groups:
            S.op("sp", lambda e, f0=f0, gn=gn: e.dma_start(out=FS[:, 0:gn], in_=self.ffnst[l, :, f0:f0 + gn]),
                 writes=[bFS], dma=True)
            for fl in range(gn):
                f = f0 + fl
                p = f % 2
                if fl % 2 == 0:
                    slot = self.load_w([
                        (lambda t: t[:, :].rearrange("p (kt g c) -> p kt g c", kt=16, g=2)[:, :, 0, :], wg[:, :, f * 128:f * 128 + 256]),
                        (lambda t: t[:, :].rearrange("p (kt g c) -> p kt g c", kt=16, g=2)[:, :, 1, :], wu[:, :, f * 128:f * 128 + 256]),
                    ])
                    wt = self.WS[slot][:, :].rearrange("p (kt g c) -> p kt g c", kt=16, g=2)
                    bw = self.bWS[slot]
                j = fl % 2
                self.mm_group(0, lambda k, wt=wt, j=j: wt[:, k, 0, j * 128:(j + 1) * 128],
                              lambda k, c0, cn: XB[:, k, c0:c0 + cn], 16, [bw], reads_k=bXB)
                S.op("act", lambda e, p=p: e.activation(out=GS[p][:], in_=self.PG[0][:, 0:NC], func=AF.Copy),
                     reads=[self.bPG[0]], writes=[bGS[p]])
                self.mm_group(1, lambda k, wt=wt, j=j: wt[:, k, 1, j * 128:(j + 1) * 128],
                              lambda k, c0, cn: XB[:, k, c0:c0 + cn], 16, [bw], reads_k=bXB)
                w0 = self.VEC[:, wdw + f * 3 + 0:wdw + f * 3 + 1]
                w1 = self.VEC[:, wdw + f * 3 + 1:wdw + f * 3 + 2]
                w2 = self.VEC[:, wdw + f * 3 + 2:wdw + f * 3 + 3]
                bb = self.V("ff_b_dw%d" % l, f)
                S.op("dve", lambda e, p=p, w0=w0, bb=bb: e.tensor_scalar(
                    out=GC[p][:, C_P0:C_S0], in0=GS[p][:, C_P0 - 2:C_S0 - 2], scalar1=w0, scalar2=bb,
                    op0=ALU.mult, op1=ALU.add), reads=[bGS[p], self.bVEC], writes=[bGC[p]])
                S.op("dve", lambda e, p=p, w1=w1: e.scalar_tensor_tensor(
                    out=GC[p][:, C_P0:C_S0], in0=GS[p][:, C_P0 - 1:C_S0 - 1], scalar=w1, in1=GC[p][:, C_P0:C_S0],
                    op0=ALU.mult, op1=ALU.add), reads=[bGS[p], bGC[p], self.bVEC], writes=[bGC[p]])
                S.op("dve", lambda e, p=p, w2=w2: e.scalar_tensor_tensor(
                    out=GC[p][:, C_P0:C_S0], in0=GS[p][:, C_P0:C_S0], scalar=w2, in1=GC[p][:, C_P0:C_S0],
                    op0=ALU.mult, op1=ALU.add), reads=[bGS[p], bGC[p], self.bVEC], writes=[bGC[p]])
                S.op("dve", lambda e, p=p, fl=fl, w0=w0, bb=bb: e.tensor_scalar(
                    out=GC[p][:, C_S0:NC], in0=FS[:, fl, :, 0], scalar1=w0, scalar2=bb,
                    op0=ALU.mult, op1=ALU.add), reads=[bFS, self.bVEC], writes=[bGC[p]])
                S.op("dve", lambda e, p=p, fl=fl, w1=w1: e.scalar_tensor_tensor(
                    out=GC[p][:, C_S0:NC], in0=FS[:, fl, :, 1], scalar=w1, in1=GC[p][:, C_S0:NC],
                    op0=ALU.mult, op1=ALU.add), reads=[bFS, bGC[p], self.bVEC], writes=[bGC[p]])
                S.op("dve", lambda e, p=p, w2=w2: e.scalar_tensor_tensor(
                    out=GC[p][:, C_S0:NC], in0=GS[p][:, C_S0:NC], scalar=w2, in1=GC[p][:, C_S0:NC],
                    op0=ALU.mult, op1=ALU.add), reads=[bGS[p], bGC[p], self.bVEC], writes=[bGC[p]])
                S.op("act", lambda e, fl=fl: e.activation(out=NFS[:, fl, :, 0], in_=FS[:, fl, :, 1], func=AF.Copy),
                     reads=[bFS], writes=[bNFS])
                S.op("act", lambda e, p=p, fl=fl: e.activation(out=NFS[:, fl, :, 1], in_=GS[p][:, C_S0:NC], func=AF.Copy),
                     reads=[bGS[p]], writes=[bNFS])
                S.op("act", lambda e, p=p, fl=fl: e.activation(out=FP[:, fl, :], in_=GS[p][:, C_S0 - 2:C_S0], func=AF.Copy),
                     reads=[bGS[p]], writes=[bFP])
                S.op("act", lambda e, p=p: e.activation(out=GC[p][:, C_P0:NC], in_=GC[p][:, C_P0:NC], func=AF.Silu),
                     reads=[bGC[p]], writes=[bGC[p]])
                S.op("dve", lambda e, p=p, fl=fl: e.tensor_tensor(
                    out=HB[:, fl, C_P0:NC], in0=GC[p][:, C_P0:NC], in1=self.PG[1][:, C_P0:NC], op=ALU.mult),
                    reads=[bGC[p], self.bPG[1]], writes=[bHB[fl]])
            if self.ps == 1:
                S.op("sp", lambda e, f0=f0, gn=gn: e.dma_start(out=self.ffnS[l, :, f0:f0 + gn], in_=NFS[:, 0:gn]),
                     reads=[bNFS], dma=True)
                S.op("sp", lambda e, f0=f0, gn=gn: e.dma_start(out=self.ffnP[l, :, f0:f0 + gn], in_=FP[:, 0:gn]),
                     reads=[bFP], dma=True)
            for m in range(16):
                p = m % 2
                if m % 2 == 0:
                    slot = self.load_w([(lambda t, gn=gn: t[:, 0:gn * 256].rearrange("p (kt c) -> p kt c", kt=gn),
                                         wd[:, f0:f0 + gn, m * 128:m * 128 + 256])])
                    wt = self.WS[slot][:, 0:gn * 256].rearrange("p (kt c) -> p kt c", kt=gn)
                    bw = self.bWS[slot]
                j = m % 2
                self.mm_group(p, lambda k, wt=wt, j=j: wt[:, k, j * 128:(j + 1) * 128],
                              lambda k, c0, cn: HB[:, k, c0:c0 + cn], gn, [bw], reads_k=bHB)
                S.op("dve", lambda e, m=m, p=p: e.tensor_tensor(out=R[:, m, :], in0=self.PG[p][:, 0:NC], in1=R[:, m, :],
                                                                op=ALU.add),
                     reads=[self.bPG[p], bR[m]], writes=[bR[m]])

    def post_ffn_ln(self, l):
        R, bR, XB, bXB = self.R, self.bR, self.XB, self.bXB
        self.layer_norm("ln_ffn_g%d" % l, "ln_ffn_b%d" % l,
                        [(lambda i: XB[:, i, :], lambda i: bXB[i], AF.Identity, 0),
                         (lambda i: R[:, i, :], lambda i: bR[i], AF.Identity, 1)])

    def ple(self, l, last):
        S = self.S
        self.phase_fence()
        R, bR, XB, bXB = self.R, self.bR, self.XB, self.bXB
        al = self.tmp_alloc()
        PT = al([128, 2, NC], BF16)
        SG = [al([128, NC], F32) for _ in range(2)]
        TT = [al([128, NC], F32) for _ in range(2)]
        bPT, bSG, bTT = Buf(), [Buf() for _ in range(2)], [Buf() for _ in range(2)]
        pv = self.pT2[self.ps, l].rearrange("(kt p) n -> p kt n", p=128)
        S.op("pool", lambda e: e.dma_start(out=PT[:], in_=pv), writes=[bPT], dma=True)
        wp = self.pl_w_proj[l].rearrange("(kt p) c -> p kt c", p=128)
        wpt = al([128, 2, D], BF16)
        bwp = Buf()
        S.op("pool", lambda e: e.dma_start(out=wpt[:], in_=wp), writes=[bwp], dma=True)
        for m in range(16):
            p = m % 2
            self.mm_group(0, lambda k, m=m: wpt[:, k, m * 128:(m + 1) * 128],
                          lambda k, c0, cn: PT[:, k, c0:c0 + cn], 2, [bwp, bPT])
            S.op("act", lambda e, p=p: e.activation(out=self.YQ[p], in_=self.PG[0][:, 0:NC], func=AF.Square),
                 reads=[self.bPG[0]], writes=[self.bYQ[p]])
            for (c0, cn) in CBS:
                S.op("pe", lambda e, m=m, p=p, c0=c0, cn=cn: e.matmul(
                    self.PG[1][:, c0:c0 + cn], lhsT=self.ONES[:], rhs=self.YQ[p][:, c0:c0 + cn],
                    start=(m == 0), stop=(m == 15)), reads=[self.bYQ[p], self.bONES], writes=[self.bPG[1]])
        self.finish_stats(False)
        wgv = self.pl_w_gate[l].rearrange("(kt p) c -> p kt c", p=128)
        for m in range(16):
            p = m % 2
            if m % 2 == 0:
                slot = self.load_w([(lambda t: t[:, 0:4096].rearrange("p (kt c) -> p kt c", kt=16),
                                     wgv[:, :, m * 128:m * 128 + 256])])
                wt = self.WS[slot][:, 0:4096].rearrange("p (kt c) -> p kt c", kt=16)
                bw = self.bWS[slot]
            j = m % 2
            self.mm_group(0, lambda k, wt=wt, j=j: wt[:, k, j * 128:(j + 1) * 128], lambda k, c0, cn: XB[:, k, c0:c0 + cn], 16, [bw], reads_k=bXB)
            self.mm_group(1, lambda k, m=m: wpt[:, k, m * 128:(m + 1) * 128],
                          lambda k, c0, cn: PT[:, k, c0:c0 + cn], 2, [bwp, bPT])
            S.op("act", lambda e, p=p: e.activation(out=SG[p][:], in_=self.PG[0][:, 0:NC], func=AF.Sigmoid),
                 reads=[self.bPG[0]], writes=[bSG[p]])
            S.op("dve", lambda e, p=p: e.tensor_tensor(out=TT[p][:], in0=self.PG[1][:, 0:NC], in1=self.T2, op=ALU.mult),
                 reads=[self.bPG[1], self.bT2], writes=[bTT[p]])
            S.op("dve", lambda e, p=p, m=m: e.scalar_tensor_tensor(
                out=TT[p][:], in0=TT[p][:], scalar=self.V("pl_g%d" % l, m), in1=SG[p][:], op0=ALU.mult, op1=ALU.mult),
                reads=[bTT[p], bSG[p], self.bVEC], writes=[bTT[p]])
            S.op("dve", lambda e, p=p, m=m: e.tensor_tensor(out=R[:, m, :], in0=R[:, m, :], in1=TT[p][:], op=ALU.add),
                 reads=[bR[m], bTT[p]], writes=[bR[m]])
        if last:
            for m in range(0, 16, 4):
                S.op("sp", lambda e, m=m: e.dma_start(out=self.yT[:, m:m + 4, :], in_=R[:, m:m + 4, :]),
                     reads=bR[m:m + 4], dma=True)
        else:
            for m in range(16):
                S.op("act", lambda e, m=m: e.activation(out=XB[:, m, :], in_=R[:, m, :], func=AF.Copy),
                     reads=[bR[m]], writes=[bXB[m]])

    def build(self):
        for ps in (0, 1):
            self.ps = ps
            self.setup()
            self.conformer()
            self.post_mix_ln(0, True)
            self.ffn(0)
            self.post_ffn_ln(0)
            self.ple(0, last=(self.n_layers == 1 and ps == 1))
            if self.n_layers == 2:
                self.mlstm()
                self.ln_needs_fence = True
                self.post_mix_ln(1, False)
                if ps == 1:
                    self.ffn(1)
                    self.post_ffn_ln(1)
                    self.ple(1, last=True)
        self.S.finalize()
        return self.nc

    def mlstm(self):
        S = self.S
        self.phase_fence()
        op = S.op
        R, bR, XB, bXB = self.R, self.bR, self.XB, self.bXB
        b_scr = Buf("xscr")
        for m in range(0, 16, 4):
            op("sp", lambda e, m=m: e.dma_start(out=self.xscr[:, m:m + 4, :], in_=R[:, m:m + 4, :]),
               reads=bR[m:m + 4], writes=[b_scr], dma=True)
        ar_ = self.tmp_alloc(self.R[:, :, :].rearrange("p a b -> p (a b)"), 16 * NC * 4)
        al = self.tmp_alloc(cap=self.TMPB)
        arena_bufs = []

        def ar(shape, dtype):
            return ar_(shape, dtype)

        def AB(name=""):
            b = Buf(name)
            arena_bufs.append(b)
            return b
        HS = al([128, 16, NC], BF16)
        bHS = [Buf() for _ in range(16)]
        XT = [al([128, NC], F32) for _ in range(2)]
        bXT = [Buf() for _ in range(2)]
        QTs = [ar([128, NC], BF16) for _ in range(2)]
        KTs = [ar([128, NC], BF16) for _ in range(2)]
        VTs = [ar([128, 2, NC], BF16) for _ in range(2)]
        SOs = [ar([128, 2, NC], BF16) for _ in range(2)]
        bQTs, bKTs, bVTs, bSOs = [AB(), AB()], [AB(), AB()], [AB(), AB()], [AB(), AB()]
        CST = self.CSTP
        bCST = self.bCSTP
        CB = ar([128, DV + 1], BF16)
        bCB = AB()
        MP = self.MPP
        bMP = self.bMPP
        CON = ar([128, 5, 128], F32)
        bCON = AB()
        IDB = ar([128, 128], BF16)
        bIDB = AB()
        HGs = [ar([128, DV], F32) for _ in range(2)]
        bHGs = [AB(), AB()]
        BG = ar([128, 16], F32)
        bBG = AB()
        LI, LF, BH, AT = (ar([128, 8, NH], F32) for _ in range(4))
        bLI, bLF, bBH, bAT = AB(), AB(), AB(), AB()
        ZT = ar([128, NH], F32)
        bZT = AB()
        DG = ar([128, 128], F32); bDG = AB()
        DM = ar([128, 128], F32); bDM = AB()
        WI = ar([128, 128], F32); bWI = AB()
        PB = ar([128, 128], BF16); bPB = AB()
        PT_ = ar([128, 128], BF16); bPT_ = AB()
        VCH = ar([128, DV + 2], BF16); bVCH = AB()
        KG = ar([128, 128], BF16); bKG = AB()
        HN = ar([128, DV], F32); bHN = AB()
        HQ = ar([128, DV], F32); bHQ = AB()
        HNB = ar([128, DV], BF16); bHNB = AB()
        SC = ar([128, 16], F32)
        bSC = AB()
        bSCc = [AB() for _ in range(16)]
        BC = ar([128, 2], F32); bBC = AB()
        FEN = ar([128, 8], F32)
        op("dve", lambda e: e.memset(FEN[:], 0.0), writes=list(bR) + arena_bufs)
        IDENT, TRI, MASKNEG, ONESF, E127 = (CON[:, i, :] for i in range(5))
        PX0, PX1 = self.PX
        bS_ = bAB = bSM = self.bPX[0]
        bTR = self.bPX[1]
        bNI = bIN = bUP = self.bPG[0]
        PS_S = PX0[:, 0:128]
        PS_AB = PX0[:, 128:256]
        PS_SM = PX0[:, 256:272]
        PS_TR = PX1[:, 0:256].bitcast(BF16)
        PS_NI = self.PG[0][:, 0:DV]
        PS_IN = self.PG[0][:, 512:512 + DV + 1]
        PS_UP = self.PG[0][:, 1024:1024 + DV + 1]

        op("sp", lambda e: e.dma_start(out=CON[:], in_=self.consts_d.rearrange("p (a b) -> p a b", a=5)), writes=[bCON], dma=True)
        op("sp", lambda e: e.dma_start(out=BG[:], in_=self.bg_rep), writes=[bBG], dma=True)
        op("act", lambda e: e.activation(out=IDB[:], in_=IDENT, func=AF.Copy), reads=[bCON], writes=[bIDB])
        if self.ps == 0:
            op("dve", lambda e: e.memset(CST[:], 0.0), writes=bCST)
            op("dve", lambda e: e.memset(MP[:], 0.0), writes=[bMP])
        else:
            hm_ = self.V("hmask", 0)
            op("dve", lambda e: e.tensor_scalar(out=CST[:], in0=CST[:], scalar1=hm_, scalar2=None, op0=ALU.mult),
               reads=list(bCST) + [self.bVEC], writes=list(bCST))
            op("dve", lambda e: e.tensor_scalar(out=MP[:], in0=MP[:], scalar1=hm_, scalar2=None, op0=ALU.mult),
               reads=[bMP, self.bVEC], writes=[bMP])
        op("dve", lambda e: e.memset(VCH[:, DV:DV + 2], 1.0), writes=[bVCH])
        if self.ps == 0:
            for t_ in range(0, 16, 4):
                op("dve", lambda e, t_=t_: e.memset(HS[:, t_:t_ + 4, :], 0.0), writes=bHS[t_:t_ + 4])
        else:
            op("dve", lambda e: e.memset(HS[:, :, 0:C_P0], 0.0), writes=bHS)
            op("dve", lambda e: e.memset(HS[:, :, C_S0:NC], 0.0), writes=bHS)

        wv = self.ml_w_in.rearrange("(kt p) c -> p kt c", p=128)
        slot = self.load_w([(lambda t: t[:, 0:256].rearrange("p (kt c) -> p kt c", kt=16), wv[:, :, 6144:6160])])
        wg = self.WS[slot][:, 0:256].rearrange("p (kt c) -> p kt c", kt=16)
        bwg = self.bWS[slot]
        for c in range(8):
            c0 = C_P0 + 128 * c
            for kt in range(16):
                op("pe", lambda e, kt=kt, c0=c0: e.matmul(PS_SM, lhsT=XB[:, kt, c0:c0 + 128], rhs=wg[:, kt, :],
                                                          start=(kt == 0), stop=(kt == 15)),
                   reads=[bwg] + bXB, writes=[bSM])
            op("dve", lambda e, c=c: e.tensor_tensor(out=LI[:, c, :], in0=PS_SM[:, 0:8], in1=BG[:, 0:8], op=ALU.add),
               reads=[bSM, bBG], writes=[bLI])
            op("dve", lambda e: e.tensor_tensor(out=ZT[:], in0=PS_SM[:, 8:16], in1=BG[:, 8:16], op=ALU.add),
               reads=[bSM, bBG], writes=[bZT])
            op("act", lambda e: e.activation(out=ZT[:], in_=ZT[:], func=AF.Exp, scale=-1.0), reads=[bZT], writes=[bZT])
            op("dve", lambda e: e.tensor_scalar(out=ZT[:], in0=ZT[:], scalar1=1.0, scalar2=None, op0=ALU.add),
               reads=[bZT], writes=[bZT])
            op("act", lambda e: e.activation(out=ZT[:], in_=ZT[:], func=AF.Ln), reads=[bZT], writes=[bZT])
            op("dve", lambda e, c=c: e.tensor_scalar(out=LF[:, c, :], in0=ZT[:], scalar1=-1.0, scalar2=None, op0=ALU.mult),
               reads=[bZT], writes=[bLF])
            op("pe", lambda e, c=c: e.matmul(PS_SM[:, 0:8], lhsT=TRI, rhs=LF[:, c, :], start=True, stop=True),
               reads=[bCON, bLF], writes=[bSM])
            op("act", lambda e, c=c: e.activation(out=BH[:, c, :], in_=PS_SM[:, 0:8], func=AF.Copy),
               reads=[bSM], writes=[bBH])
            op("dve", lambda e, c=c: e.tensor_tensor(out=AT[:, c, :], in0=LI[:, c, :], in1=BH[:, c, :], op=ALU.subtract),
               reads=[bLI, bBH], writes=[bAT])

        ATT = ar([NH, 8, 128], F32); bATT = AB()
        AMAX = ar([NH, 8], F32); BHL = ar([NH, 8], F32); T2g = ar([NH, 8], F32); MN = ar([NH, 8], F32)
        MPV = ar([NH, 8], F32); GCOL = ar([NH, 8], F32); GST = ar([NH, 8], F32); T1g = ar([NH, 1], F32); M0 = ar([NH, 1], F32)
        bFM = AB()
        XG = ar([NH, 8, 8], F32); bXG = AB()
        GCOLB, GSTB, MPVB, MNB = (ar([128, NH, 8], F32) for _ in range(4))
        bGBC = AB()
        PSg = PX0[0:NH, 0:128]
        for c in range(8):
            op("pe", lambda e, c=c: e.transpose(out=PSg, in_=AT[:, c, :], identity=IDENT), reads=[bAT, bCON], writes=[self.bPX[0]])
            op("act", lambda e, c=c: e.activation(out=ATT[:, c, :], in_=PSg, func=AF.Copy), reads=[self.bPX[0]], writes=[bATT])
        op("dve", lambda e: e.tensor_reduce(out=AMAX[:], in_=ATT[:], axis=AX.X, op=ALU.max), reads=[bATT], writes=[bFM])
        for c in range(8):
            op("pe", lambda e, c=c: e.matmul(PX0[0:NH, 256 + c:257 + c], lhsT=LF[:, c, :], rhs=ONESF[:, 0:1], start=True, stop=True),
               reads=[bLF, bCON], writes=[self.bPX[0]])
        op("act", lambda e: e.activation(out=BHL[:], in_=PX0[0:NH, 256:264], func=AF.Copy), reads=[self.bPX[0]], writes=[bFM])
        op("pe", lambda e: e.transpose(out=PSg, in_=MP[:], identity=IDENT), reads=[bMP, bCON], writes=[self.bPX[0]])
        op("act", lambda e: e.activation(out=M0[:], in_=PX0[0:NH, 0:1], func=AF.Copy), reads=[self.bPX[0]], writes=[bFM])
        op("dve", lambda e: e.tensor_tensor(out=T2g[:], in0=BHL[:], in1=AMAX[:], op=ALU.add), reads=[bFM], writes=[bFM])
        for c in range(8):
            prev = M0[:] if c == 0 else MN[:, c - 1:c]
            op("dve", lambda e, c=c, prev=prev: e.tensor_tensor(out=T1g[:], in0=BHL[:, c:c + 1], in1=prev, op=ALU.add), reads=[bFM], writes=[bFM])
            op("dve", lambda e, c=c: e.tensor_tensor(out=MN[:, c:c + 1], in0=T1g[:], in1=T2g[:, c:c + 1], op=ALU.max), reads=[bFM], writes=[bFM])
        op("dve", lambda e: e.tensor_copy(out=MPV[:, 0:1], in_=M0[:]), reads=[bFM], writes=[bFM])
        op("dve", lambda e: e.tensor_copy(out=MPV[:, 1:8], in_=MN[:, 0:7]), reads=[bFM], writes=[bFM])
        op("dve", lambda e: e.tensor_tensor(out=GCOL[:], in0=BHL[:], in1=MN[:], op=ALU.subtract), reads=[bFM], writes=[bFM])
        op("dve", lambda e: e.tensor_tensor(out=GST[:], in0=MPV[:], in1=GCOL[:], op=ALU.add), reads=[bFM], writes=[bFM])
        op("act", lambda e: e.activation(out=GST[:], in_=GST[:], func=AF.Exp), reads=[bFM], writes=[bFM])
        for V_, VB_ in ((GCOL, GCOLB), (GST, GSTB), (MPV, MPVB), (MN, MNB)):
            op("dve", lambda e, V_=V_: e.tensor_tensor(out=XG[:], in0=V_[:][:, None, :].broadcast_to([NH, NH, 8]),
                                                       in1=IDENT[0:NH, 0:NH][:, :, None].broadcast_to([NH, NH, 8]), op=ALU.mult),
               reads=[bFM, bCON], writes=[bXG])
            op("pe", lambda e: e.matmul(PX0[:, 0:64], lhsT=ONESF[0:NH, :], rhs=XG[:].rearrange("k h c -> k (h c)"), start=True, stop=True),
               reads=[bXG, bCON], writes=[self.bPX[0]])
            op("act", lambda e, VB_=VB_: e.activation(out=VB_[:].rearrange("p h c -> p (h c)"), in_=PX0[:, 0:64], func=AF.Copy),
               reads=[self.bPX[0]], writes=[bGBC])
        op("dve", lambda e: e.tensor_copy(out=MP[:], in_=MNB[:, :, 7]), reads=[bGBC, bMP], writes=[bMP])

        if self.ps == 1:
            CSB = [ar([128, NS, DV], F32) for _ in range(1)]
            bCSB = [AB() for _ in range(1)]
            NSI = ar([NS, NH, DK], F32); bNSI = AB()
            NSO = NSI; bNSO = bNSI
            MS0 = ar([NS, NH], F32); bMS0 = AB()
            SG8 = ar([NS, 6, NH], F32); bSG8 = AB()
            ID16 = al([128, NS, NS], F32); bID16 = Buf()
            ZQ = al([128, NS, NS], F32); bZQ = Buf()
            QS = DG[0:NS, :]; bQS = bDG
            KS = WI[0:NS, :]; bKS = bWI
            VS = ar([NS, DV], F32); bVS = AB()
            KM = al([NS, NS, DK], F32); bKM = Buf()
            DW = ar([NS, NS], F32); bDW = AB()
            GB = ar([128, NS], F32); bGB = AB()
            SS = ar([NS, 16], F32); bSS = AB()
            TQ = DM[0:NS, :]; bTQ = bDM
            HNs = HN[0:NS, :]; bHNs = bHN
            HQs = HQ[0:NS, :]; bHQs = bHQ
            HNBs = HNB[0:NS, :]; bHNBs = bHNB
            op("sp", lambda e: e.dma_start(out=NSI[:], in_=self.nS), writes=[bNSI], dma=True)
            op("sp", lambda e: e.dma_start(out=MS0[:], in_=self.mS), writes=[bMS0], dma=True)
            op("sp", lambda e: e.dma_start(out=ID16[:], in_=self.id16.rearrange("p (a b) -> p a b", a=NS)), writes=[bID16], dma=True)
            for kt in range(16):
                op("pe", lambda e, kt=kt: e.matmul(PS_SM[0:NS, :], lhsT=XB[:, kt, C_S0:NC], rhs=wg[:, kt, :],
                                                   start=(kt == 0), stop=(kt == 15)), reads=[bwg] + bXB, writes=[bSM])
            op("dve", lambda e: e.tensor_tensor(out=SG8[:, 0, :], in0=PS_SM[0:NS, 0:8], in1=BG[0:NS, 0:8], op=ALU.add),
               reads=[bSM, bBG], writes=[bSG8])
            op("dve", lambda e: e.tensor_tensor(out=ZT[0:NS, :], in0=PS_SM[0:NS, 8:16], in1=BG[0:NS, 8:16], op=ALU.add),
               reads=[bSM, bBG], writes=[bZT])
            op("act", lambda e: e.activation(out=ZT[0:NS, :], in_=ZT[0:NS, :], func=AF.Exp, scale=-1.0), reads=[bZT], writes=[bZT])
            op("dve", lambda e: e.tensor_scalar(out=ZT[0:NS, :], in0=ZT[0:NS, :], scalar1=1.0, scalar2=None, op0=ALU.add),
               reads=[bZT], writes=[bZT])
            op("act", lambda e: e.activation(out=ZT[0:NS, :], in_=ZT[0:NS, :], func=AF.Ln), reads=[bZT], writes=[bZT])
            op("dve", lambda e: e.tensor_tensor(out=SG8[:, 1, :], in0=MS0[:], in1=ZT[0:NS, :], op=ALU.subtract),
               reads=[bMS0, bZT], writes=[bSG8])
            op("dve", lambda e: e.tensor_tensor(out=SG8[:, 2, :], in0=SG8[:, 1, :], in1=SG8[:, 0, :], op=ALU.max), reads=[bSG8], writes=[bSG8])
            op("dve", lambda e: e.tensor_tensor(out=SG8[:, 3, :], in0=SG8[:, 0, :], in1=SG8[:, 2, :], op=ALU.subtract), reads=[bSG8], writes=[bSG8])
            op("dve", lambda e: e.tensor_tensor(out=SG8[:, 4, :], in0=SG8[:, 1, :], in1=SG8[:, 2, :], op=ALU.subtract), reads=[bSG8], writes=[bSG8])
            op("act", lambda e: e.activation(out=SG8[:, 3:5, :], in_=SG8[:, 3:5, :], func=AF.Exp), reads=[bSG8], writes=[bSG8])
            op("act", lambda e: e.activation(out=SG8[:, 5, :], in_=SG8[:, 2, :], func=AF.Exp, scale=-1.0), reads=[bSG8], writes=[bSG8])
            op("sp", lambda e: e.dma_start(out=self.mSo, in_=SG8[:, 2, :]), reads=[bSG8], dma=True)

        scale_q = float(DK) ** -0.5

        def proj_items(h):
            hb = h % 2
            QT, KT, VTt, SO = QTs[hb], KTs[hb], VTs[hb], SOs[hb]
            bQT, bKT, bVT, bSO = bQTs[hb], bKTs[hb], bVTs[hb], bSOs[hb]
            items = []
            stt = {}

            def loads():
                s1 = self.load_w([
                    (lambda t: t[:, 0:8192].rearrange("p (kt c) -> p kt c", kt=16)[:, :, 0:128], wv[:, :, h * 128:(h + 1) * 128]),
                    (lambda t: t[:, 0:8192].rearrange("p (kt c) -> p kt c", kt=16)[:, :, 128:256], wv[:, :, 1024 + h * 128:1024 + (h + 1) * 128]),
                    (lambda t: t[:, 0:8192].rearrange("p (kt c) -> p kt c", kt=16)[:, :, 256:512], wv[:, :, 2048 + h * 256:2048 + (h + 1) * 256]),
                ])
                stt["w1"] = self.WS[s1][:, 0:8192].rearrange("p (kt c) -> p kt c", kt=16)
                stt["bw1"] = self.bWS[s1]
                s2 = self.load_w([(lambda t: t[:, 0:4096].rearrange("p (kt c) -> p kt c", kt=16),
                                   wv[:, :, 4096 + h * 256:4096 + (h + 1) * 256])])
                stt["w2"] = self.WS[s2][:, 0:4096].rearrange("p (kt c) -> p kt c", kt=16)
                stt["bw2"] = self.bWS[s2]
                op("sp", lambda e: e.dma_start(out=HGs[hb][:], in_=self.hg_rep[:, h * DV:(h + 1) * DV]), writes=[bHGs[hb]], dma=True)
            items.append(loads)
            cb_last = [(C_S0 - 128, 1024 - (C_S0 - 128)), (1024, C_S0 - 1024)] if self.ps == 0 else None

            def group(wkey, col0, evac, cbs):
                cbl = CBS if cbs is None else cbs
                for k in range(16):
                    for (c0, cn) in cbl:
                        def mm(k=k, c0=c0, cn=cn):
                            w = stt[wkey]
                            bw = stt["b" + wkey]
                            op("pe", lambda e: e.matmul(self.PG[1][:, c0:c0 + cn], lhsT=w[:, k, col0:col0 + 128],
                                                        rhs=XB[:, k, c0:c0 + cn], start=(k == 0), stop=(k == 15)),
                               reads=[bw, bXB[k]], writes=[self.bPG[1]])
                        items.append(mm)
                items.append(evac)
            group("w1", 0, lambda: op("act", lambda e: e.activation(out=QT[:], in_=self.PG[1][:, 0:NC], func=AF.Copy, scale=scale_q),
                                      reads=[self.bPG[1]], writes=[bQT]), cb_last)
            group("w1", 128, lambda: op("act", lambda e: e.activation(out=KT[:], in_=self.PG[1][:, 0:NC], func=AF.Copy),
                                        reads=[self.bPG[1]], writes=[bKT]), None)
            for j in range(2):
                group("w1", 256 + j * 128, lambda j=j: op("act", lambda e: e.activation(out=VTt[:, j, :], in_=self.PG[1][:, 0:NC], func=AF.Copy),
                                                         reads=[self.bPG[1]], writes=[bVT]), None)
            for j in range(2):
                group("w2", j * 128, lambda j=j: op("act", lambda e: e.activation(out=SO[:, j, :], in_=self.PG[1][:, 0:NC], func=AF.Sigmoid),
                                                    reads=[self.bPG[1]], writes=[bSO]), cb_last)
            return items

        def head_body(h, QT, KT, VTt, SO, bQT, bKT, bVT, bSO, HGc, bHGc, pending):
            def emit_pending(n):
                for _ in range(min(n, len(pending))):
                    pending.pop(0)()
            emit_pending(1)
            op("act", lambda e, h=h: e.activation(out=CB[:], in_=CST[:, h, :], func=AF.Copy), reads=[bCST[h]], writes=[bCB])
            for c in range(8):
                c0 = C_P0 + 128 * c
                mp = MPVB[:, h, c:c + 1]
                full = (self.ps == 1) or (c == 7)
                if full:
                    op("pe", lambda e, c0=c0: e.matmul(PS_S, lhsT=QT[:, c0:c0 + 128], rhs=KT[:, c0:c0 + 128], start=True, stop=True),
                       reads=[bQT, bKT], writes=[bS_])
                if full:
                    op("act", lambda e, c=c, h=h: e.activation(out=DG[:], in_=IDENT, func=AF.Copy, scale=AT[:, c, h:h + 1]),
                       reads=[bCON, bAT], writes=[bDG])
                    op("pe", lambda e: e.matmul(PS_AB, lhsT=ONESF, rhs=DG[:], start=True, stop=True),
                       reads=[bCON, bDG], writes=[bAB])
                emit_pending(0 if c == 0 else 14)
                if full:
                    op("dve", lambda e, c=c, h=h: e.scalar_tensor_tensor(out=DM[:], in0=PS_AB, scalar=BH[:, c, h:h + 1], in1=MASKNEG,
                                                                      op0=ALU.add, op1=ALU.add),
                       reads=[bAB, bBH, bCON], writes=[bDM])
                    op("dve", lambda e: e.tensor_reduce(out=SC[:, 0:1], in_=DM[:], axis=AX.X, op=ALU.max), reads=[bDM], writes=[bSCc[0]])
                    op("dve", lambda e, c=c, h=h, mp=mp: e.tensor_tensor(out=SC[:, 1:2], in0=BH[:, c, h:h + 1], in1=mp, op=ALU.add),
                       reads=[bBH, bGBC], writes=[bSCc[1]])
                    op("dve", lambda e: e.tensor_tensor(out=SC[:, 2:3], in0=SC[:, 0:1], in1=SC[:, 1:2], op=ALU.max), reads=[bSCc[0], bSCc[1]], writes=[bSCc[2]])
                    op("dve", lambda e: e.tensor_scalar(out=SC[:, 3:4], in0=SC[:, 2:3], scalar1=-1.0, scalar2=None, op0=ALU.mult),
                       reads=[bSCc[2]], writes=[bSCc[3]])
                if full:
                    op("act", lambda e: e.activation(out=WI[:], in_=DM[:], func=AF.Exp, bias=SC[:, 3:4], scale=1.0),
                       reads=[bDM, bSCc[3]], writes=[bWI])
                    op("act", lambda e: e.activation(out=SC[:, 4:5], in_=SC[:, 1:2], func=AF.Exp, bias=SC[:, 3:4], scale=1.0),
                       reads=[bSCc[1], bSCc[3]], writes=[bSCc[4]])
                    op("act", lambda e: e.activation(out=SC[:, 5:6], in_=SC[:, 3:4], func=AF.Exp), reads=[bSCc[3]], writes=[bSCc[5]])
                    op("dve", lambda e: e.tensor_tensor(out=PB[:], in0=PS_S, in1=WI[:], op=ALU.mult), reads=[bS_, bWI], writes=[bPB])
                    op("act", lambda e: e.activation(out=WI[:], in_=PB[:], func=AF.Copy, accum_out=SC[:, 6:7]), reads=[bPB], writes=[bWI, bSCc[6]])
                    op("pe", lambda e: e.transpose(out=PS_TR[:, 0:128], in_=PB[:], identity=IDB[:]), reads=[bPB, bIDB], writes=[bTR])
                    op("act", lambda e: e.activation(out=PT_[:], in_=PS_TR[:, 0:128], func=AF.Copy), reads=[bTR], writes=[bPT_])
                for j in range(2):
                    op("pe", lambda e, j=j, c0=c0: e.transpose(out=PS_TR[:, 128 + j * 128:256 + j * 128], in_=VTt[:, j, c0:c0 + 128], identity=IDB[:]),
                       reads=[bVT, bIDB], writes=[bTR])
                op("act", lambda e: e.activation(out=VCH[:, 0:DV], in_=PS_TR[:, 128:384], func=AF.Copy), reads=[bTR], writes=[bVCH])
                op("pe", lambda e, c0=c0: e.transpose(out=PS_TR[:, 384:512], in_=KT[:, c0:c0 + 128], identity=IDB[:]),
                   reads=[bKT, bIDB], writes=[bTR])
                if not full:
                    emit_pending(0 if c == 0 else 30)
                if full:
                    op("pe", lambda e: e.matmul(PS_NI, lhsT=PT_[:], rhs=VCH[:, 0:DV], start=True, stop=True),
                       reads=[bPT_, bVCH], writes=[bNI])
                    op("pe", lambda e, c0=c0: e.matmul(PS_IN, lhsT=QT[:, c0:c0 + 128], rhs=CB[:], start=True, stop=True),
                       reads=[bQT, bCB], writes=[bIN])
                    emit_pending(0 if c == 0 else 30)
                    op("dve", lambda e: e.tensor_scalar(out=HN[:], in0=PS_IN[:, 0:DV], scalar1=SC[:, 4:5], scalar2=None, op0=ALU.mult),
                       reads=[bIN, bSCc[4]], writes=[bHN])
                    op("dve", lambda e: e.tensor_tensor(out=HN[:], in0=HN[:], in1=PS_NI, op=ALU.add), reads=[bHN, bNI], writes=[bHN])
                    op("dve", lambda e: e.scalar_tensor_tensor(out=SC[:, 7:8], in0=PS_IN[:, DV:DV + 1], scalar=SC[:, 4:5], in1=SC[:, 6:7],
                                                               op0=ALU.mult, op1=ALU.add), reads=[bIN, bSCc[4], bSCc[6]], writes=[bSCc[7]])
                    op("dve", lambda e: e.tensor_scalar(out=SC[:, 14:15], in0=SC[:, 7:8], scalar1=-1.0, scalar2=None, op0=ALU.mult), reads=[bSCc[7]], writes=[bSCc[14]])
                    op("dve", lambda e: e.tensor_tensor(out=SC[:, 7:8], in0=SC[:, 7:8], in1=SC[:, 14:15], op=ALU.max), reads=[bSCc[7], bSCc[14]], writes=[bSCc[7]])
                    op("dve", lambda e: e.tensor_tensor(out=SC[:, 7:8], in0=SC[:, 7:8], in1=SC[:, 5:6], op=ALU.max), reads=[bSCc[5], bSCc[7]], writes=[bSCc[7]])
                    op("act", lambda e: e.activation(out=HQ[:], in_=HN[:], func=AF.Identity, accum_out=SC[:, 12:13]),
                       reads=[bHN], writes=[bHQ, bSCc[12]])
                    op("act", lambda e: e.activation(out=HQ[:], in_=HN[:], func=AF.Square, accum_out=SC[:, 13:14]),
                       reads=[bHN], writes=[bHQ, bSCc[13]])
                    op("dve", lambda e: e.tensor_scalar(out=SC[:, 12:13], in0=SC[:, 12:13], scalar1=1.0 / DV, scalar2=None, op0=ALU.mult),
                       reads=[bSCc[12]], writes=[bSCc[12]])
                    op("dve", lambda e: e.tensor_tensor(out=SC[:, 8:9], in0=SC[:, 12:13], in1=SC[:, 12:13], op=ALU.mult),
                       reads=[bSCc[12]], writes=[bSCc[8]])
                    op("dve", lambda e: e.scalar_tensor_tensor(out=SC[:, 13:14], in0=SC[:, 13:14], scalar=1.0 / DV, in1=SC[:, 8:9],
                                                               op0=ALU.mult, op1=ALU.subtract), reads=[bSCc[13], bSCc[8]], writes=[bSCc[13]])
                    op("dve", lambda e: e.tensor_tensor(out=SC[:, 8:9], in0=SC[:, 7:8], in1=SC[:, 7:8], op=ALU.mult),
                       reads=[bSCc[7]], writes=[bSCc[8]])
                    op("dve", lambda e: e.scalar_tensor_tensor(out=SC[:, 13:14], in0=SC[:, 8:9], scalar=LN_EPS, in1=SC[:, 13:14],
                                                               op0=ALU.mult, op1=ALU.add), reads=[bSCc[8], bSCc[13]], writes=[bSCc[13]])
                    op("act", lambda e: e.activation(out=SC[:, 13:14], in_=SC[:, 13:14], func=AF.Ln), reads=[bSCc[13]], writes=[bSCc[13]])
                    op("act", lambda e: e.activation(out=SC[:, 13:14], in_=SC[:, 13:14], func=AF.Exp, scale=-0.5), reads=[bSCc[13]], writes=[bSCc[13]])
                    op("dve", lambda e: e.tensor_scalar(out=HN[:], in0=HN[:], scalar1=SC[:, 12:13], scalar2=SC[:, 13:14],
                                                        op0=ALU.subtract, op1=ALU.mult), reads=[bHN, bSCc[12], bSCc[13]], writes=[bHN])
                    op("dve", lambda e, h=h: e.tensor_tensor(out=HNB[:], in0=HN[:], in1=HGc[:, :], op=ALU.mult),
                       reads=[bHN, bHGc], writes=[bHNB])
                op("act", lambda e, c=c, h=h: e.activation(out=SC[:, 10:11], in_=AT[:, c, h:h + 1], func=AF.Exp, bias=GCOLB[:, h, c:c + 1], scale=1.0),
                   reads=[bAT, bGBC], writes=[bSCc[10]])
                op("act", lambda e: e.activation(out=KG[:], in_=PS_TR[:, 384:512], func=AF.Copy, scale=SC[:, 10:11]),
                   reads=[bTR, bSCc[10]], writes=[bKG])
                op("pe", lambda e: e.matmul(PS_UP, lhsT=KG[:], rhs=VCH[:, 0:DV + 1], start=True, stop=True),
                   reads=[bKG, bVCH], writes=[bUP])
                op("dve", lambda e, h=h, c=c: e.scalar_tensor_tensor(out=CST[:, h, :], in0=CST[:, h, :], scalar=GSTB[:, h, c:c + 1], in1=PS_UP,
                                                                     op0=ALU.mult, op1=ALU.add), reads=[bCST[h], bUP, bGBC], writes=[bCST[h]])
                op("act", lambda e, h=h: e.activation(out=CB[:], in_=CST[:, h, :], func=AF.Copy), reads=[bCST[h]], writes=[bCB])
                if full:
                    for j in range(2):
                        op("pe", lambda e, j=j: e.transpose(out=PS_TR[:, j * 128:(j + 1) * 128], in_=HNB[:, j * 128:(j + 1) * 128], identity=IDB[:]),
                           reads=[bHNB, bIDB], writes=[bTR])
                        op("dve", lambda e, j=j, h=h, c0=c0: e.tensor_tensor(out=HS[:, 2 * h + j, c0:c0 + 128], in0=PS_TR[:, j * 128:(j + 1) * 128],
                                                                           in1=SO[:, j, c0:c0 + 128], op=ALU.mult),
                           reads=[bTR, bSO], writes=[bHS[2 * h + j]])

            emit_pending(len(pending))
            if self.ps == 0:
                return
            sb_ = 0
            CSh = CSB[sb_]
            op("sp", lambda e, h=h, CSh=CSh: e.dma_start(out=CSh[:], in_=self.cS[:, h].rearrange("i k v -> k i v")),
               writes=[bCSB[sb_]], dma=True)
            op("pe", lambda e: e.transpose(out=PS_TR[0:NS, 0:128], in_=QT[:, C_S0:NC], identity=IDB[:]), reads=[bQT, bIDB], writes=[bTR])
            op("pe", lambda e: e.transpose(out=PS_TR[0:NS, 128:256], in_=KT[:, C_S0:NC], identity=IDB[:]), reads=[bKT, bIDB], writes=[bTR])
            for j in range(2):
                op("pe", lambda e, j=j: e.transpose(out=PS_TR[0:NS, 256 + j * 128:384 + j * 128], in_=VTt[:, j, C_S0:NC], identity=IDB[:]),
                   reads=[bVT, bIDB], writes=[bTR])
            op("dve", lambda e: e.tensor_copy(out=QS[:], in_=PS_TR[0:NS, 0:128]), reads=[bTR], writes=[bQS])
            op("dve", lambda e: e.tensor_copy(out=VS[:], in_=PS_TR[0:NS, 256:512]), reads=[bTR], writes=[bVS])
            op("dve", lambda e, h=h: e.tensor_scalar(out=KS[:], in0=PS_TR[0:NS, 128:256], scalar1=SG8[:, 3, h:h + 1], scalar2=None, op0=ALU.mult),
               reads=[bTR, bSG8], writes=[bKS])
            op("dve", lambda e: e.tensor_tensor(out=TQ[:, 0:DK], in0=QS[:], in1=KS[:], op=ALU.mult), reads=[bQS, bKS], writes=[bTQ])
            op("dve", lambda e: e.tensor_reduce(out=SS[:, 0:1], in_=TQ[:, 0:DK], axis=AX.X, op=ALU.add), reads=[bTQ], writes=[bSS])
            op("dve", lambda e, h=h: e.tensor_tensor(out=TQ[:, 0:DK], in0=QS[:], in1=NSI[:, h, :], op=ALU.mult), reads=[bQS, bNSI], writes=[bTQ])
            op("dve", lambda e: e.tensor_reduce(out=SS[:, 1:2], in_=TQ[:, 0:DK], axis=AX.X, op=ALU.add), reads=[bTQ], writes=[bSS])
            op("dve", lambda e, h=h: e.scalar_tensor_tensor(out=NSO[:, h, :], in0=NSI[:, h, :], scalar=SG8[:, 4, h:h + 1], in1=KS[:],
                                                            op0=ALU.mult, op1=ALU.add), reads=[bNSI, bSG8, bKS], writes=[bNSO])
            op("dve", lambda e: e.tensor_tensor(out=ZQ[:], in0=QT[:, C_S0:NC][:, None, :].broadcast_to([128, NS, NS]), in1=ID16[:], op=ALU.mult),
               reads=[bQT, bID16], writes=[bZQ])
            for i in range(NS):
                op("pe", lambda e, i=i, CSh=CSh: e.matmul(PS_NI[0:NS, :], lhsT=ZQ[:, i, :], rhs=CSh[:, i, :], start=(i == 0), stop=(i == NS - 1)),
                   reads=[bZQ, bCSB[sb_]], writes=[bNI])
            op("dve", lambda e: e.tensor_scalar(out=HNs[:], in0=VS[:], scalar1=SS[:, 0:1], scalar2=None, op0=ALU.mult),
               reads=[bVS, bSS], writes=[bHNs])
            op("dve", lambda e, h=h: e.scalar_tensor_tensor(out=HNs[:], in0=PS_NI[0:NS, :], scalar=SG8[:, 4, h:h + 1], in1=HNs[:],
                                                            op0=ALU.mult, op1=ALU.add), reads=[bNI, bSG8, bHNs], writes=[bHNs])
            op("dve", lambda e, h=h: e.scalar_tensor_tensor(out=SS[:, 2:3], in0=SS[:, 1:2], scalar=SG8[:, 4, h:h + 1], in1=SS[:, 0:1],
                                                            op0=ALU.mult, op1=ALU.add), reads=[bSS, bSG8], writes=[bSS])
            op("dve", lambda e: e.tensor_scalar(out=SS[:, 3:4], in0=SS[:, 2:3], scalar1=-1.0, scalar2=None, op0=ALU.mult), reads=[bSS], writes=[bSS])
            op("dve", lambda e: e.tensor_tensor(out=SS[:, 2:3], in0=SS[:, 2:3], in1=SS[:, 3:4], op=ALU.max), reads=[bSS], writes=[bSS])
            op("dve", lambda e, h=h: e.tensor_tensor(out=SS[:, 2:3], in0=SS[:, 2:3], in1=SG8[:, 5, h:h + 1], op=ALU.max), reads=[bSS, bSG8], writes=[bSS])
            op("dve", lambda e: e.reciprocal(out=SS[:, 4:5], in_=SS[:, 2:3]), reads=[bSS], writes=[bSS])
            op("dve", lambda e: e.tensor_scalar(out=HNs[:], in0=HNs[:], scalar1=SS[:, 4:5], scalar2=None, op0=ALU.mult), reads=[bHNs, bSS], writes=[bHNs])
            op("dve", lambda e: e.tensor_reduce(out=SS[:, 5:6], in_=HNs[:], axis=AX.X, op=ALU.add), reads=[bHNs], writes=[bSS])
            op("dve", lambda e: e.tensor_scalar(out=SS[:, 5:6], in0=SS[:, 5:6], scalar1=1.0 / DV, scalar2=None, op0=ALU.mult), reads=[bSS], writes=[bSS])
            op("dve", lambda e: e.tensor_scalar(out=HNs[:], in0=HNs[:], scalar1=SS[:, 5:6], scalar2=None, op0=ALU.subtract), reads=[bHNs, bSS], writes=[bHNs])
            op("dve", lambda e: e.tensor_tensor(out=HQs[:], in0=HNs[:], in1=HNs[:], op=ALU.mult), reads=[bHNs], writes=[bHQs])
            op("dve", lambda e: e.tensor_reduce(out=SS[:, 6:7], in_=HQs[:], axis=AX.X, op=ALU.add), reads=[bHQs], writes=[bSS])
            op("dve", lambda e: e.tensor_scalar(out=SS[:, 6:7], in0=SS[:, 6:7], scalar1=1.0 / DV, scalar2=LN_EPS, op0=ALU.mult, op1=ALU.add),
               reads=[bSS], writes=[bSS])
            op("act", lambda e: e.activation(out=SS[:, 6:7], in_=SS[:, 6:7], func=AF.Ln), reads=[bSS], writes=[bSS])
            op("act", lambda e: e.activation(out=SS[:, 6:7], in_=SS[:, 6:7], func=AF.Exp, scale=-0.5), reads=[bSS], writes=[bSS])
            op("dve", lambda e, h=h: e.scalar_tensor_tensor(out=HNBs[:], in0=HNs[:], scalar=SS[:, 6:7], in1=HGc[0:NS, :],
                                                            op0=ALU.mult, op1=ALU.mult), reads=[bHNs, bSS, bHGc], writes=[bHNBs])
            for j in range(2):
                op("pe", lambda e, j=j: e.transpose(out=PS_TR[:, j * NS:(j + 1) * NS], in_=HNBs[:, j * 128:(j + 1) * 128], identity=IDB[0:NS, 0:NS]),
                   reads=[bHNBs, bIDB], writes=[bTR])
                op("dve", lambda e, j=j, h=h: e.tensor_tensor(out=HS[:, 2 * h + j, C_S0:NC], in0=PS_TR[:, j * NS:(j + 1) * NS],
                                                              in1=SO[:, j, C_S0:NC], op=ALU.mult),
                   reads=[bTR, bSO], writes=[bHS[2 * h + j]])
            op("dve", lambda e, h=h: e.tensor_scalar(out=DW[:], in0=IDENT[0:NS, 0:NS], scalar1=SG8[:, 4, h:h + 1], scalar2=None, op0=ALU.mult),
               reads=[bCON, bSG8], writes=[bDW])
            op("pe", lambda e: e.matmul(PS_SM[:, 0:NS], lhsT=ONESF[0:NS, :], rhs=DW[:], start=True, stop=True), reads=[bCON, bDW], writes=[bSM])
            op("act", lambda e: e.activation(out=GB[:], in_=PS_SM[:, 0:NS], func=AF.Copy), reads=[bSM], writes=[bGB])
            op("dve", lambda e: e.tensor_tensor(out=KM[:], in0=KS[:][:, None, :].broadcast_to([NS, NS, DK]),
                                                in1=IDENT[0:NS, 0:NS][:, :, None].broadcast_to([NS, NS, DK]), op=ALU.mult),
               reads=[bKS, bCON], writes=[bKM])
            bPSO = [Buf() for _ in range(3)]
            for b_ in bPSO:
                b_.excl = True
            for i in range(NS):
                slot_ = i % 3
                PS_O = self.PG[1][:, slot_ * 512:slot_ * 512 + DV]
                op("pe", lambda e, i=i, PS_O=PS_O: e.matmul(PS_O, lhsT=KM[:, i, :], rhs=VS[:], start=True, stop=True),
                   reads=[bKM, bVS], writes=[bPSO[slot_]] + ([self.bPG[1]] if i == 0 else []))
                op("dve", lambda e, i=i, PS_O=PS_O, CSh=CSh: e.scalar_tensor_tensor(out=CSh[:, i, :], in0=CSh[:, i, :], scalar=GB[:, i:i + 1], in1=PS_O,
                                                                                   op0=ALU.mult, op1=ALU.add),
                   reads=[bCSB[sb_], bGB, bPSO[slot_]], writes=[bCSB[sb_]])
            op("dve", lambda e: e.memset(FEN[:], 0.0), reads=bPSO, writes=[self.bPG[1]])
            op("sp", lambda e, h=h, CSh=CSh: e.dma_start(out=self.cSo[:, h].rearrange("i k v -> k i v"), in_=CSh[:]),
               reads=[bCSB[sb_]], dma=True)
        for it in proj_items(0):
            it()
        for h in range(NH):
            hb = h % 2
            pending = proj_items(h + 1) if h + 1 < NH else []
            head_body(h, QTs[hb], KTs[hb], VTs[hb], SOs[hb], bQTs[hb], bKTs[hb], bVTs[hb], bSOs[hb], HGs[hb], bHGs[hb], pending)
        if self.ps == 1:
            op("sp", lambda e: e.dma_start(out=self.nSo, in_=NSO[:]), reads=[bNSO], dma=True)
        if self.ps == 1:
            op("sp", lambda e: e.dma_start(out=self.cP, in_=CST[:]), reads=bCST, dma=True)
            op("sp", lambda e: e.dma_start(out=self.mP, in_=MP[:]), reads=[bMP], dma=True)
        wo = self.ml_w_out.rearrange("(kt p) c -> p kt c", p=128)
        for m in range(16):
            p = m % 2
            if m % 2 == 0:
                slot = self.load_w([(lambda t: t[:, 0:4096].rearrange("p (kt c) -> p kt c", kt=16),
                                     wo[:, :, m * 128:m * 128 + 256])])
                wt = self.WS[slot][:, 0:4096].rearrange("p (kt c) -> p kt c", kt=16)
                bw = self.bWS[slot]
            j = m % 2
            op("sp", lambda e, m=m, p=p: e.dma_start(out=XT[p][:], in_=self.xscr[:, m, :]), reads=[b_scr], writes=[bXT[p]], dma=True)
            op("act", lambda e, m=m, p=p: e.activation(out=XT[p][:], in_=XT[p][:], func=AF.Identity,
                                                       scale=ALPHA, bias=self.V("ml_b_out", m)),
               reads=[bXT[p], self.bVEC], writes=[bXT[p]])
            self.mm_group(p, lambda k, wt=wt, j=j: wt[:, k, j * 128:(j + 1) * 128],
                          lambda k, c0, cn: HS[:, k, c0:c0 + cn], 16, [bw], reads_k=bHS)
            op("dve", lambda e, m=m, p=p: e.tensor_tensor(out=R[:, m, :], in0=self.PG[p][:, 0:NC], in1=XT[p][:], op=ALU.add),
               reads=[self.bPG[p], bXT[p]], writes=[bR[m]] + (arena_bufs if m == 0 else []))


def make_in_maps(inp):
    xp = np.asarray(inp["x_prompt"], np.float32)
    xs = np.asarray(inp["x_sample"], np.float32)
    pp = np.asarray(inp["p_prompt"], np.float32)
    ps = np.asarray(inp["p_sample"], np.float32)
    sconv = np.asarray(inp["state_conv"], np.float32)
    sffn = np.asarray(inp["state_ffn_conv"], np.float32)
    shared = {k: np.ascontiguousarray(np.asarray(inp[k], np.float32)) for k in
              ("ff_w_gate", "ff_w_up", "ff_w_down", "pl_w_proj", "pl_w_gate")}
    shared["cv_w_in"] = np.ascontiguousarray(np.asarray(inp["cv_w_in"], np.float32)[0])
    shared["cv_w_out"] = np.ascontiguousarray(np.asarray(inp["cv_w_out"], np.float32)[0])
    shared["ml_w_in"] = np.ascontiguousarray(np.asarray(inp["ml_w_in"], np.float32)[0])
    shared["ml_w_out"] = np.ascontiguousarray(np.asarray(inp["ml_w_out"], np.float32)[0])
    ii = np.arange(128)
    consts = np.zeros((128, 5, 128), np.float32)
    consts[:, 0, :] = np.eye(128, dtype=np.float32)
    consts[:, 1, :] = (ii[:, None] <= ii[None, :]).astype(np.float32)
    consts[:, 2, :] = np.where(ii[None, :] <= ii[:, None], 0.0, -1.0e9)
    consts[:, 3, :] = 1.0
    consts[127, 4, :] = 1.0
    shared["consts"] = consts.reshape(128, 5 * 128)
    shared["id16"] = np.ascontiguousarray(np.broadcast_to(np.eye(NS, dtype=np.float32).reshape(1, NS * NS), (128, NS * NS)))
    shared["bg_rep"] = np.ascontiguousarray(np.broadcast_to(np.asarray(inp["ml_b_gates"], np.float32)[0].reshape(1, 16), (128, 16)))
    shared["hg_rep"] = np.ascontiguousarray(np.broadcast_to(np.asarray(inp["ml_hn_g"], np.float32)[0].reshape(1, D), (128, D)))
    maps = []
    for c in range(8):
        s, h = c // 2, c % 2
        t0 = h * NP
        xT = np.zeros((2, D, NC), np.float32)
        pT = np.zeros((2, 2, 256, NC), np.float32)
        if h == 1:
            xT[0][:, C_P0:C_S0] = xp[s, 0:NP].T
            for l in range(2):
                pT[0, l][:, C_P0:C_S0] = pp[l, s, 0:NP].T
        xT[1][:, C_P0:C_S0] = xp[s, t0:t0 + NP].T
        xT[1][:, C_S0:NC] = xs[c * NS:(c + 1) * NS, 0].T
        for l in range(2):
            pT[1, l][:, C_P0:C_S0] = pp[l, s, t0:t0 + NP].T
            pT[1, l][:, C_S0:NC] = ps[l, c * NS:(c + 1) * NS, 0].T
        cst = sconv[0, c * NS:(c + 1) * NS]
        convst = np.ascontiguousarray(cst.reshape(NS, 30, 16, 128).transpose(3, 2, 0, 1))
        fst = sffn[:, c * NS:(c + 1) * NS]
        ffnst = np.ascontiguousarray(fst.reshape(2, NS, 2, NFT, 128).transpose(0, 4, 3, 1, 2))
        m = dict(shared)
        m["cS"] = np.ascontiguousarray(np.asarray(inp["state_mlstm_c"], np.float32)[0, c * NS:(c + 1) * NS])
        m["nS"] = np.ascontiguousarray(np.asarray(inp["state_mlstm_n"], np.float32)[0, c * NS:(c + 1) * NS])
        m["mS"] = np.ascontiguousarray(np.asarray(inp["state_mlstm_m"], np.float32)[0, c * NS:(c + 1) * NS])
        m.update({"xT": xT, "pT": pT, "vecs": pack_vecs(inp, float(h)), "convst": convst, "ffnst": ffnst})
        maps.append(m)
    return maps


_PROG_CACHE = {}


def run_device(inp, n_layers=2):
    if n_layers not in _PROG_CACHE:
        _PROG_CACHE[n_layers] = Prog(n_layers).build()
    nc = _PROG_CACHE[n_layers]
    maps = make_in_maps(inp)
    if n_layers == 1:
        for m in maps:
            for k in ("ml_w_in", "ml_w_out", "consts", "bg_rep", "hg_rep", "id16", "cS", "nS", "mS"):
                m.pop(k, None)
    res = run_bass_kernel_spmd(nc, maps, core_ids=list(range(8)))
    return res.results


def assemble(results):
    y_prompt = np.zeros((4, 2048, D), np.float32)
    y_sample = np.zeros((128, 1, D), np.float32)
    conv_p = np.zeros((1, 4, 30, D), np.float32)
    conv_s = np.zeros((1, 128, 30, D), np.float32)
    ffn_p = np.zeros((2, 4, 2, DFF), np.float32)
    ffn_s = np.zeros((2, 128, 2, DFF), np.float32)
    for c in range(8):
        s, h = c // 2, c % 2
        r = results[c]
        yT = r["yT"].transpose(1, 0, 2).reshape(D, NC)
        y_prompt[s, h * NP:(h + 1) * NP] = yT[:, C_P0:C_S0].T
        y_sample[c * NS:(c + 1) * NS, 0] = yT[:, C_S0:NC].T
        conv_s[0, c * NS:(c + 1) * NS] = r["convS"].transpose(2, 3, 1, 0).reshape(NS, 30, D)
        ffn_s[:, c * NS:(c + 1) * NS] = r["ffnS"].transpose(0, 3, 4, 2, 1).reshape(2, NS, 2, DFF)
        if h == 1:
            conv_p[0, s] = r["convP"].transpose(2, 1, 0).reshape(30, D)
            ffn_p[:, s] = r["ffnP"].transpose(0, 3, 2, 1).reshape(2, 2, DFF)
    return y_prompt, y_sample, conv_p, conv_s, ffn_p, ffn_s


def kernel(**inputs):
    results = run_device(inputs, 2)
    y_prompt, y_sample, conv_p, conv_s, ffn_p, ffn_s = assemble(results)
    c_p = np.zeros((1, 4, NH, DK, DV), np.float32)
    n_p = np.zeros((1, 4, NH, DK), np.float32)
    m_p = np.zeros((1, 4, NH), np.float32)
    for c in range(8):
        s_, h = c // 2, c % 2
        if h == 1:
            cp = results[c]["cP"]
            c_p[0, s_] = cp[:, :, 0:DV].transpose(1, 0, 2)
            n_p[0, s_] = cp[:, :, DV].T
            m_p[0, s_] = results[c]["mP"][0]
    c_s = np.zeros((1, 128, NH, DK, DV), np.float32)
    n_s = np.zeros((1, 128, NH, DK), np.float32)
    m_s = np.zeros((1, 128, NH), np.float32)
    for c in range(8):
        c_s[0, c * NS:(c + 1) * NS] = results[c]["cSo"]
        n_s[0, c * NS:(c + 1) * NS] = results[c]["nSo"]
        m_s[0, c * NS:(c + 1) * NS] = results[c]["mSo"]
    return (y_prompt, y_sample, conv_p, conv_s, c_p, n_p, m_p, c_s, n_s, m_s, ffn_p, ffn_s)
```

```python
import contextlib
import numpy as np
import concourse.bass as bass
import concourse.mybir as mybir
from concourse.bass_utils import run_bass_kernel_spmd

F32 = mybir.dt.float32
BF16 = mybir.dt.bfloat16
AF = mybir.ActivationFunctionType
ALU = mybir.AluOpType
AX = mybir.AxisListType

D = 2048
DFF = 5632
NT16 = 16
NFT = 44
HALO = 32
NP = 1024
NS = 16
NC = HALO + NP + NS
C_P0 = HALO
C_S0 = HALO + NP
CBS = [(0, 512), (512, 512), (1024, 48)]
ALPHA = 4.0 ** 0.25
LN_EPS = 1e-5
CW = 31
NH = 8
DK = 128
DV = 256
MLP = 6160

SAME_ENGINE_SYNC = True
N_DMA_SEMS = 40
WS_ELEMS = 8192
N_WS = 2


class Buf:
    __slots__ = ("name", "w", "re", "rd", "excl")
    registry = []
    fence_ins = None

    def __init__(self, name=""):
        self.name = name
        self.w = Buf.fence_ins
        self.re = {}
        self.rd = []
        self.excl = False
        Buf.registry.append(self)


class Ins:
    __slots__ = ("eng", "emit", "deps", "signal", "ticket", "is_dma", "sem_i", "prev_dma")


class Sched:
    ENGS = ("pe", "act", "dve", "pool", "sp")

    def __init__(self, nc):
        self.nc = nc
        self.lists = {e: [] for e in self.ENGS}
        self.dma_rr = 0
        self.dma_last = [None] * N_DMA_SEMS
        self.dma_cnt = [0] * N_DMA_SEMS

    def op(self, eng, emit, reads=(), writes=(), dma=False):
        ins = Ins()
        ins.eng = eng
        ins.emit = emit
        ins.signal = False
        ins.ticket = None
        ins.is_dma = dma
        ins.prev_dma = None
        ins.sem_i = -1
        deps = set()
        raw = set()
        ex = [b for b in reads if b.excl and b not in writes]
        if ex:
            writes = list(writes) + ex
        for b in reads:
            if b.w is not None:
                deps.add(b.w)
                raw.add(b.w)
        for b in writes:
            if b.w is not None:
                deps.add(b.w)
            deps.update(b.re.values())
            deps.update(b.rd)
        for b in reads:
            if dma:
                b.rd.append(ins)
            else:
                b.re[eng] = ins
        for b in writes:
            b.w = ins
            b.re = {}
            b.rd = []
        deps.discard(ins)
        if dma:
            i = self.dma_rr
            self.dma_rr = (self.dma_rr + 1) % N_DMA_SEMS
            ins.sem_i = i
            ins.prev_dma = self.dma_last[i]
            self.dma_last[i] = ins
            self.dma_cnt[i] += 16
            ins.ticket = self.dma_cnt[i]
            ins.signal = True
        fin = []
        for d in deps:
            if d.is_dma:
                fin.append(d)
            elif d.eng == eng and not dma:
                if SAME_ENGINE_SYNC and eng != "pe" and d in raw:
                    d.signal = True
                    fin.append(d)
            else:
                d.signal = True
                fin.append(d)
        ins.deps = fin
        self.lists[eng].append(ins)
        return ins

    def finalize(self):
        nc = self.nc
        for e in self.ENGS:
            c = 0
            for ins in self.lists[e]:
                if ins.is_dma:
                    continue
                if ins.signal:
                    c += 1
                    ins.ticket = c
        with contextlib.ExitStack() as st:
            esem = {e: st.enter_context(nc.semaphore("s_" + e)) for e in self.ENGS}
            dsem = [st.enter_context(nc.semaphore("d%d" % i)) for i in range(N_DMA_SEMS)]
            block = st.enter_context(nc.Block())
            lists = self.lists
            dma_cnt = self.dma_cnt

            def run(ename, eobj):
                waited = {}
                for ins in lists[ename]:
                    need = {}
                    for d in ins.deps:
                        key = ("d", d.sem_i) if d.is_dma else ("e", d.eng)
                        if d.ticket > need.get(key, 0):
                            need[key] = d.ticket
                    if ins.is_dma and ins.prev_dma is not None:
                        key = ("d", ins.sem_i)
                        if ins.prev_dma.ticket > need.get(key, 0):
                            need[key] = ins.prev_dma.ticket
                    for key, t in need.items():
                        if waited.get(key, 0) >= t:
                            continue
                        waited[key] = t
                        sem = dsem[key[1]] if key[0] == "d" else esem[key[1]]
                        eobj.wait_ge(sem, t)
                    bi = ins.emit(eobj)
                    if ins.is_dma:
                        bi.then_inc(dsem[ins.sem_i], 16)
                    elif ins.signal:
                        bi.then_inc(esem[ename], 1)
                if ename == "sp":
                    for i in range(N_DMA_SEMS):
                        if dma_cnt[i] > 0:
                            eobj.wait_ge(dsem[i], dma_cnt[i])

            @block.tensor
            def _(e):
                run("pe", e)

            @block.scalar
            def _(e):
                run("act", e)

            @block.vector
            def _(e):
                run("dve", e)

            @block.gpsimd
            def _(e):
                run("pool", e)

            @block.sync
            def _(e):
                run("sp", e)


def vec_layout():
    off = {}
    n = 0

    def add(name, ncols):
        nonlocal n
        off[name] = n
        n += ncols

    add("cv_b_in", 32)
    add("cv_w_dw", 16 * CW)
    add("cv_b_dw", 16)
    add("cv_ln_g", 16)
    add("cv_ln_b", 16)
    add("cv_b_out", 16)
    add("ml_b_out", 16)
    for l in range(2):
        add("ln_mix_g%d" % l, 16)
        add("ln_mix_b%d" % l, 16)
        add("ln_ffn_g%d" % l, 16)
        add("ln_ffn_b%d" % l, 16)
        add("ff_w_dw%d" % l, NFT * 3)
        add("ff_b_dw%d" % l, NFT)
        add("ff_b_down%d" % l, 16)
        add("pl_g%d" % l, 16)
    add("hmask", 1)
    add("ml_bg", 1)
    return off, n


VOFF, NV = vec_layout()


def fm(v):
    v = np.asarray(v, np.float32)
    return np.ascontiguousarray(v.reshape(-1, 128).T)


def pack_vecs(inp, hflag):
    V = np.zeros((128, NV), np.float32)

    def put(name, arr):
        V[:, VOFF[name]:VOFF[name] + arr.shape[1]] = arr

    put("cv_b_in", fm(inp["cv_b_in"][0]))
    w = inp["cv_w_dw"][0]
    put("cv_w_dw", np.ascontiguousarray(w.reshape(CW, 16, 128).transpose(2, 1, 0).reshape(128, 16 * CW)))
    put("cv_b_dw", fm(inp["cv_b_dw"][0]))
    put("cv_ln_g", fm(inp["cv_ln_g"][0]))
    put("cv_ln_b", fm(inp["cv_ln_b"][0]))
    put("cv_b_out", fm(inp["cv_b_out"][0]))
    put("ml_b_out", fm(inp["ml_b_out"][0]))
    for l in range(2):
        put("ln_mix_g%d" % l, fm(inp["ln_mix_g"][l]))
        put("ln_mix_b%d" % l, fm(inp["ln_mix_b"][l]))
        put("ln_ffn_g%d" % l, fm(inp["ln_ffn_g"][l]))
        put("ln_ffn_b%d" % l, fm(inp["ln_ffn_b"][l]))
        w = inp["ff_w_dw"][l]
        put("ff_w_dw%d" % l, np.ascontiguousarray(w.reshape(3, NFT, 128).transpose(2, 1, 0).reshape(128, NFT * 3)))
        put("ff_b_dw%d" % l, fm(inp["ff_b_dw"][l]))
        put("ff_b_down%d" % l, fm(inp["ff_b_down"][l]))
        put("pl_g%d" % l, fm(inp["pl_g"][l]))
    V[:, VOFF["hmask"]] = hflag
    V[0:16, VOFF["ml_bg"]] = np.asarray(inp["ml_b_gates"][0], np.float32).reshape(16)
    return V


class Prog:
    def __init__(self, n_layers=2):
        self.n_layers = n_layers
        nc = self.nc = bass.Bass("TRN2", target_bir_lowering=False)
        self.S = Sched(nc)
        dt = nc.dram_tensor
        self.xT2 = dt("xT", [2, D, NC], F32, kind="ExternalInput").ap()
        self.pT2 = dt("pT", [2, 2, 256, NC], F32, kind="ExternalInput").ap()
        self.vecs_d = dt("vecs", [128, NV], F32, kind="ExternalInput").ap()
        self.convst = dt("convst", [128, 16, NS, 30], F32, kind="ExternalInput").ap()
        self.ffnst = dt("ffnst", [2, 128, NFT, NS, 2], F32, kind="ExternalInput").ap()
        self.cv_w_in = dt("cv_w_in", [D, 2 * D], F32, kind="ExternalInput").ap()
        self.cv_w_out = dt("cv_w_out", [D, D], F32, kind="ExternalInput").ap()
        self.ff_w_gate = dt("ff_w_gate", [2, D, DFF], F32, kind="ExternalInput").ap()
        self.ff_w_up = dt("ff_w_up", [2, D, DFF], F32, kind="ExternalInput").ap()
        self.ff_w_down = dt("ff_w_down", [2, DFF, D], F32, kind="ExternalInput").ap()
        self.pl_w_proj = dt("pl_w_proj", [2, 256, D], F32, kind="ExternalInput").ap()
        self.pl_w_gate = dt("pl_w_gate", [2, D, D], F32, kind="ExternalInput").ap()
        self.ml_w_in = dt("ml_w_in", [D, MLP], F32, kind="ExternalInput").ap()
        self.ml_w_out = dt("ml_w_out", [D, D], F32, kind="ExternalInput").ap()
        self.consts_d = dt("consts", [128, 5 * 128], F32, kind="ExternalInput").ap()
        self.bg_rep = dt("bg_rep", [128, 16], F32, kind="ExternalInput").ap()
        self.hg_rep = dt("hg_rep", [128, D], F32, kind="ExternalInput").ap()
        self.xscr = dt("xscr", [128, 16, NC], F32).ap()
        self.cS = dt("cS", [NS, NH, DK, DV], F32, kind="ExternalInput").ap()
        self.nS = dt("nS", [NS, NH, DK], F32, kind="ExternalInput").ap()
        self.mS = dt("mS", [NS, NH], F32, kind="ExternalInput").ap()
        self.id16 = dt("id16", [128, NS * NS], F32, kind="ExternalInput").ap()
        self.cSo = dt("cSo", [NS, NH, DK, DV], F32, kind="ExternalOutput").ap()
        self.nSo = dt("nSo", [NS, NH, DK], F32, kind="ExternalOutput").ap()
        self.mSo = dt("mSo", [NS, NH], F32, kind="ExternalOutput").ap()
        self.cP = dt("cP", [128, NH, DV + 1], F32, kind="ExternalOutput").ap()
        self.mP = dt("mP", [128, NH], F32, kind="ExternalOutput").ap()
        self.yT = dt("yT", [128, 16, NC], F32, kind="ExternalOutput").ap()
        self.convP = dt("convP", [128, 16, 30], F32, kind="ExternalOutput").ap()
        self.convS = dt("convS", [128, 16, NS, 30], F32, kind="ExternalOutput").ap()
        self.ffnP = dt("ffnP", [2, 128, NFT, 2], F32, kind="ExternalOutput").ap()
        self.ffnS = dt("ffnS", [2, 128, NFT, NS, 2], F32, kind="ExternalOutput").ap()

        sb = nc.alloc_sbuf_tensor
        self.XB = sb("XB", [128, 16, NC], BF16)
        self.R = sb("R", [128, 16, NC], F32)
        self.bXB = [Buf("XB%d" % i) for i in range(16)]
        self.bR = [Buf("R%d" % i) for i in range(16)]
        self.WS = [sb("WS%d" % i, [128, WS_ELEMS], BF16) for i in range(N_WS)]
        self.bWS = [Buf("WS%d" % i) for i in range(N_WS)]
        self.ws_rr = 0
        self.VEC = sb("VEC", [128, NV], F32)
        self.bVEC = Buf("VEC")
        self.ONES = sb("ONES", [128, 128], BF16)
        self.bONES = Buf("ONES")
        self.bT1, self.bT2 = Buf("T1"), Buf("T2")
        self.bYB = [Buf() for _ in range(2)]
        self.bYQ = [Buf() for _ in range(2)]
        self.UH = sb("UH", [128, 16, 30], F32)
        self.bUH = [Buf() for _ in range(16)]
        self.XH = [sb("XH%d" % l, [128, 16, 2], BF16) for l in range(2)]
        self.bXH = [Buf() for _ in range(2)]
        self.CSTP = sb("CSTP", [128, NH, DV + 1], F32)
        self.bCSTP = [Buf() for _ in range(NH)]
        self.MPP = sb("MPP", [128, NH], F32)
        self.bMPP = Buf()
        self.tmp_base = None
        self.TMPB = 54 * 1024
        self.LNB = 2 * NC * 4 + 4 * NC * 2
        self.TMP = sb("TMP", [128, self.TMPB // 4], F32)
        lo = (self.TMPB - self.LNB) // 4
        self.T1 = self.TMP[:, lo:lo + NC]
        self.T2 = self.TMP[:, lo + NC:lo + 2 * NC]
        yb = self.TMP[:, lo + 2 * NC:lo + 4 * NC].bitcast(BF16)
        self.YB = [yb[:, i * NC:(i + 1) * NC] for i in range(2)]
        self.YQ = [yb[:, (2 + i) * NC:(3 + i) * NC] for i in range(2)]
        self.PG = [nc.alloc_psum_tensor("pg%d" % i, [128, 1536], F32) for i in range(2)]
        self.bPG = [Buf("pg%d" % i) for i in range(2)]
        self.PX = [nc.alloc_psum_tensor("px%d" % i, [128, 512], F32) for i in range(2)]
        self.bPX = [Buf("px%d" % i) for i in range(2)]
        for b_ in self.bPG + self.bPX:
            b_.excl = True
        self.FEN0 = sb("FEN0", [128, 8], F32)
        self.IDF = sb("IDF", [128, 128], F32)
        self.IDB16 = sb("IDB16", [128, 128], BF16)
        self.bIDB16 = Buf("IDB16")
        Buf.registry = []
        Buf.fence_ins = None

    def phase_fence(self, extra=(), clear=True):
        old = list(Buf.registry)
        if clear:
            Buf.registry = []
        ins = self.S.op("dve", lambda e: e.memset(self.FEN0[:], 0.0), writes=list(old) + list(extra))
        Buf.fence_ins = ins

    def V(self, name, col):
        c = VOFF[name] + col
        return self.VEC[:, c:c + 1]

    def tmp_alloc(self, base=None, cap=None):
        if base is None:
            base = self.TMP
            cap = self.TMPB - self.LNB if cap is None else cap
        state = {"off": 0}

        def alloc(shape, dtype):
            esz = 4 if dtype == F32 else 2
            n = int(np.prod(shape[1:]))
            nbytes = (n * esz + 31) // 32 * 32
            o = state["off"]
            assert o + nbytes <= cap, ("TMP overflow", o + nbytes)
            state["off"] = o + nbytes
            ap = base[:, o // 4:(o + nbytes) // 4]
            if dtype != F32:
                ap = ap.bitcast(BF16)
            ap = ap[:, 0:n]
            if len(shape) == 3:
                ap = ap.rearrange("p (a b) -> p a b", a=shape[1])
            elif len(shape) == 4:
                ap = ap.rearrange("p (a b c) -> p a b c", a=shape[1], b=shape[2])
            if shape[0] != 128:
                ap = ap[0:shape[0]]
            return ap
        return alloc

    def load_w(self, parts, nbytes_check=None):
        S = self.S
        slot = self.ws_rr
        self.ws_rr = (self.ws_rr + 1) % N_WS
        for dstf, src in parts:
            dst = dstf(self.WS[slot])
            S.op("pool", lambda e, dst=dst, src=src: e.dma_start(out=dst, in_=src),
                 writes=[self.bWS[slot]], dma=True)
        return slot

    def mm_group(self, pgk, lhsT_fn, rhs_fn, nk, reads, cbs=None, reads_k=None):
        S = self.S
        pg = self.PG[pgk]
        cbs = CBS if cbs is None else cbs
        for k in range(nk):
            for (c0, cn) in cbs:
                S.op("pe", lambda e, k=k, c0=c0, cn=cn: e.matmul(
                    pg[:, c0:c0 + cn], lhsT=lhsT_fn(k), rhs=rhs_fn(k, c0, cn),
                    start=(k == 0), stop=(k == nk - 1)),
                    reads=(reads if reads_k is None else list(reads) + [reads_k[k]]), writes=[self.bPG[pgk]])

    def setup(self):
        S = self.S
        ps = self.ps
        if ps == 0:
            S.op("sp", lambda e: e.dma_start(out=self.VEC[:], in_=self.vecs_d), writes=[self.bVEC], dma=True)
            S.op("dve", lambda e: e.memset(self.ONES[:], 1.0 / D), writes=[self.bONES])
            S.op("sp", lambda e: e.dma_start(out=self.IDF[:], in_=self.consts_d[:, 0:128]), writes=[self.bIDB16], dma=True)
            S.op("act", lambda e: e.activation(out=self.IDB16[:], in_=self.IDF[:], func=AF.Copy),
                 reads=[self.bIDB16], writes=[self.bIDB16])
        xv = self.xT2[ps].rearrange("(t p) n -> p t n", p=128)
        for i in range(0, 16, 4):
            S.op("pool", lambda e, i=i: e.dma_start(out=self.XB[:, i:i + 4, :], in_=xv[:, i:i + 4, :]),
                 writes=self.bXB[i:i + 4], dma=True)

    def layer_norm(self, gname, bname, outs, center=True, src_fn=None):
        S = self.S
        R, bR = self.R, self.bR
        if getattr(self, "ln_needs_fence", False):
            self.phase_fence(extra=[self.bT1, self.bT2] + self.bYB + self.bYQ, clear=False)
            self.ln_needs_fence = False
        for i in range(16):
            p = i % 2
            if center:
                S.op("dve", lambda e, i=i, p=p: e.tensor_copy(out=self.YB[p], in_=R[:, i, :]),
                     reads=[bR[i]], writes=[self.bYB[p]])
            S.op("act", lambda e, i=i, p=p: e.activation(out=self.YQ[p], in_=R[:, i, :], func=AF.Square),
                 reads=[bR[i]], writes=[self.bYQ[p]])
            for (c0, cn) in CBS:
                if center:
                    S.op("pe", lambda e, i=i, p=p, c0=c0, cn=cn: e.matmul(
                        self.PG[0][:, c0:c0 + cn], lhsT=self.ONES[:], rhs=self.YB[p][:, c0:c0 + cn],
                        start=(i == 0), stop=(i == 15)), reads=[self.bYB[p], self.bONES], writes=[self.bPG[0]])
                S.op("pe", lambda e, i=i, p=p, c0=c0, cn=cn: e.matmul(
                    self.PG[1][:, c0:c0 + cn], lhsT=self.ONES[:], rhs=self.YQ[p][:, c0:c0 + cn],
                    start=(i == 0), stop=(i == 15)), reads=[self.bYQ[p], self.bONES], writes=[self.bPG[1]])
        self.finish_stats(center)
        for i in range(16):
            S.op("dve", lambda e, i=i: e.tensor_tensor(out=R[:, i, :], in0=R[:, i, :], in1=self.T2, op=ALU.mult),
                 reads=[bR[i], self.bT2], writes=[bR[i]])
            if center:
                S.op("dve", lambda e, i=i: e.tensor_tensor(out=R[:, i, :], in0=R[:, i, :], in1=self.T1, op=ALU.add),
                     reads=[bR[i], self.bT1], writes=[bR[i]])
            for (dst_fn, buf_fn, func, is_R) in sorted(outs, key=lambda o: o[3]):
                S.op("act", lambda e, i=i, dst_fn=dst_fn, func=func: e.activation(
                    out=dst_fn(i), in_=R[:, i, :], func=func,
                    scale=self.V(gname, i), bias=self.V(bname, i)),
                    reads=[bR[i], self.bVEC], writes=[buf_fn(i)])

    def rsqrt_T2(self):
        S = self.S
        T2 = self.T2
        S.op("act", lambda e: e.activation(out=T2, in_=T2, func=AF.Sqrt), reads=[self.bT2], writes=[self.bT2])
        S.op("dve", lambda e: e.reciprocal(out=T2, in_=T2), reads=[self.bT2], writes=[self.bT2])

    def finish_stats(self, center):
        S = self.S
        T1, T2 = self.T1, self.T2
        if center:
            S.op("act", lambda e: e.activation(out=T1, in_=self.PG[0][:, 0:NC], func=AF.Copy),
                 reads=[self.bPG[0]], writes=[self.bT1])
            S.op("dve", lambda e: e.tensor_tensor(out=T2, in0=T1, in1=T1, op=ALU.mult),
                 reads=[self.bT1], writes=[self.bT2])
            S.op("dve", lambda e: e.tensor_tensor(out=T2, in0=self.PG[1][:, 0:NC], in1=T2, op=ALU.subtract),
                 reads=[self.bPG[1], self.bT2], writes=[self.bT2])
            S.op("dve", lambda e: e.tensor_scalar(out=T2, in0=T2, scalar1=LN_EPS, scalar2=None, op0=ALU.add),
                 reads=[self.bT2], writes=[self.bT2])
            self.rsqrt_T2()
            S.op("dve", lambda e: e.scalar_tensor_tensor(out=T1, in0=T1, scalar=-1.0, in1=T2,
                                                         op0=ALU.mult, op1=ALU.mult),
                 reads=[self.bT1, self.bT2], writes=[self.bT1])
        else:
            S.op("dve", lambda e: e.tensor_scalar(out=T2, in0=self.PG[1][:, 0:NC], scalar1=LN_EPS, scalar2=None,
                                                  op0=ALU.add),
                 reads=[self.bPG[1]], writes=[self.bT2])
            self.rsqrt_T2()

    def conformer(self):
        S = self.S
        self.phase_fence()
        R, bR, XB, bXB = self.R, self.bR, self.XB, self.bXB
        al = self.tmp_alloc()
        UT = [al([128, NC], F32) for _ in range(2)]
        UB = [al([128, NC], BF16) for _ in range(2)]
        SGT = [al([128, NC], F32) for _ in range(1)]
        XT = UT
        CS = [al([128, NS, 30], F32) for _ in range(1)]
        NCS = [al([128, NS, 30], F32) for _ in range(1)]
        RED = al([128, NS], F32)
        DGW = [al([128, CW, 128], BF16) for _ in range(2)]
        bUT, bUB, bSGT = [Buf() for _ in range(2)], [Buf() for _ in range(2)], [Buf() for _ in range(1)]
        bXT = bUT
        bCS, bNCS, bRED = [Buf() for _ in range(1)], [Buf() for _ in range(1)], Buf()
        bDGW = [Buf() for _ in range(2)]
        wv = self.cv_w_in.rearrange("(kt p) c -> p kt c", p=128)
        hm = self.V("hmask", 0)
        st = {}

        def proj(i):
            p = i % 2
            if i % 2 == 0:
                slot = self.load_w([
                    (lambda t: t[:, :].rearrange("p (kt g c) -> p kt g c", kt=16, g=2)[:, :, 0, :], wv[:, :, i * 128:i * 128 + 256]),
                    (lambda t: t[:, :].rearrange("p (kt g c) -> p kt g c", kt=16, g=2)[:, :, 1, :], wv[:, :, D + i * 128:D + i * 128 + 256]),
                ])
                st["wt"] = self.WS[slot][:, :].rearrange("p (kt g c) -> p kt g c", kt=16, g=2)
                st["bw"] = self.bWS[slot]
            wt, bw = st["wt"], st["bw"]
            j = i % 2
            self.mm_group(0, lambda k, wt=wt, j=j: wt[:, k, 0, j * 128:(j + 1) * 128],
                          lambda k, c0, cn: XB[:, k, c0:c0 + cn], 16, [bw], reads_k=bXB)
            self.mm_group(1, lambda k, wt=wt, j=j: wt[:, k, 1, j * 128:(j + 1) * 128],
                          lambda k, c0, cn: XB[:, k, c0:c0 + cn], 16, [bw], reads_k=bXB)
            S.op("act", lambda e, i=i: e.activation(out=SGT[0][:], in_=self.PG[1][:, 0:NC], func=AF.Sigmoid,
                                                    bias=self.V("cv_b_in", 16 + i), scale=1.0),
                 reads=[self.bPG[1], self.bVEC], writes=[bSGT[0]])
            S.op("dve", lambda e, i=i, p=p: e.scalar_tensor_tensor(
                out=UT[p][:], in0=self.PG[0][:, 0:NC], scalar=self.V("cv_b_in", i), in1=SGT[0][:],
                op0=ALU.add, op1=ALU.mult), reads=[self.bPG[0], bSGT[0], self.bVEC], writes=[bUT[p]])
            if self.ps == 0:
                S.op("dve", lambda e, p=p: e.memset(UT[p][:, 0:HALO], 0.0), writes=[bUT[p]])
            else:
                S.op("dve", lambda e, p=p: e.memset(UT[p][:, 0:2], 0.0), writes=[bUT[p]])
                S.op("dve", lambda e, p=p, i=i: e.tensor_scalar(out=UT[p][:, 2:HALO], in0=self.UH[:, i, :], scalar1=hm,
                                                              scalar2=None, op0=ALU.mult),
                     reads=[self.bUH[i], self.bVEC], writes=[bUT[p]])
            S.op("act", lambda e, p=p: e.activation(out=UB[p][:], in_=UT[p][:], func=AF.Copy), reads=[bUT[p]], writes=[bUB[p]])
            wc = VOFF["cv_w_dw"] + i * CW
            S.op("dve", lambda e, p=p, wc=wc: e.tensor_tensor(
                out=DGW[p][:], in0=self.IDB16[:][:, None, :].broadcast_to([128, CW, 128]),
                in1=self.VEC[:, wc:wc + CW][:, :, None].broadcast_to([128, CW, 128]), op=ALU.mult),
                reads=[self.bIDB16, self.bVEC], writes=[bDGW[p]])

        def conv(i):
            p = i % 2
            wc = VOFF["cv_w_dw"] + i * CW
            NDVE = 10
            for j2 in range(NDVE):
                wcol = self.VEC[:, wc + j2:wc + j2 + 1]
                if j2 == 0:
                    S.op("dve", lambda e, i=i, p=p, wcol=wcol: e.tensor_scalar(
                        out=R[:, i, C_P0:C_S0], in0=UT[p][:, C_P0 - 30:C_S0 - 30], scalar1=wcol,
                        scalar2=self.V("cv_b_dw", i), op0=ALU.mult, op1=ALU.add),
                        reads=[bUT[p], self.bVEC], writes=[bR[i]])
                else:
                    S.op("dve", lambda e, i=i, p=p, wcol=wcol, j2=j2: e.scalar_tensor_tensor(
                        out=R[:, i, C_P0:C_S0], in0=UT[p][:, C_P0 - 30 + j2:C_S0 - 30 + j2], scalar=wcol,
                        in1=R[:, i, C_P0:C_S0], op0=ALU.mult, op1=ALU.add),
                        reads=[bUT[p], bR[i], self.bVEC], writes=[bR[i]])
            for j2 in range(NDVE, CW):
                for q, c0 in enumerate((C_P0, C_P0 + 512)):
                    S.op("pe", lambda e, p=p, j2=j2, q=q, c0=c0: e.matmul(
                        self.PX[q][:, :], lhsT=DGW[p][:, j2, :], rhs=UB[p][:, c0 - 30 + j2:c0 - 30 + j2 + 512],
                        start=(j2 == NDVE), stop=(j2 == CW - 1)), reads=[bDGW[p], bUB[p]], writes=[self.bPX[q]])
            for q, c0 in enumerate((C_P0, C_P0 + 512)):
                S.op("dve", lambda e, i=i, q=q, c0=c0: e.tensor_tensor(out=R[:, i, c0:c0 + 512], in0=self.PX[q][:, :],
                                                                     in1=R[:, i, c0:c0 + 512], op=ALU.add),
                     reads=[self.bPX[q], bR[i]], writes=[bR[i]])
            S.op("dve", lambda e, i=i: e.memset(R[:, i, 0:C_P0], 0.0), writes=[bR[i]])
            S.op("sp", lambda e, i=i: e.dma_start(out=CS[0][:], in_=self.convst[:, i]), writes=[bCS[0]], dma=True)
            if self.ps == 1:
                S.op("act", lambda e: e.activation(out=NCS[0][:, :, 0:29], in_=CS[0][:, :, 1:30], func=AF.Copy),
                     reads=[bCS[0]], writes=[bNCS[0]])
            wrow = self.VEC[:, wc:wc + 30]
            S.op("dve", lambda e, wrow=wrow: e.tensor_tensor(
                out=CS[0][:], in0=CS[0][:], in1=wrow[:, None, :].broadcast_to([128, NS, 30]), op=ALU.mult),
                reads=[bCS[0], self.bVEC], writes=[bCS[0]])
            S.op("dve", lambda e: e.tensor_reduce(out=RED[:], in_=CS[0][:], axis=AX.X, op=ALU.add),
                 reads=[bCS[0]], writes=[bRED])
            S.op("dve", lambda e, i=i, p=p, wc=wc: e.scalar_tensor_tensor(
                out=R[:, i, C_S0:NC], in0=UT[p][:, C_S0:NC], scalar=self.VEC[:, wc + 30:wc + 31], in1=RED[:],
                op0=ALU.mult, op1=ALU.add), reads=[bUT[p], bRED, self.bVEC], writes=[bR[i]])
            S.op("dve", lambda e, i=i: e.tensor_scalar(
                out=R[:, i, C_S0:NC], in0=R[:, i, C_S0:NC], scalar1=self.V("cv_b_dw", i), scalar2=None, op0=ALU.add),
                reads=[bR[i], self.bVEC], writes=[bR[i]])
            if self.ps == 1:
                S.op("act", lambda e, p=p: e.activation(out=NCS[0][:, :, 29], in_=UT[p][:, C_S0:NC], func=AF.Copy),
                     reads=[bUT[p]], writes=[bNCS[0]])
                S.op("sp", lambda e, i=i: e.dma_start(out=self.convS[:, i], in_=NCS[0][:]), reads=[bNCS[0]], dma=True)
                S.op("sp", lambda e, i=i, p=p: e.dma_start(out=self.convP[:, i, :], in_=UT[p][:, C_S0 - 30:C_S0]),
                     reads=[bUT[p]], dma=True)
            else:
                S.op("act", lambda e, i=i, p=p: e.activation(out=self.UH[:, i, :], in_=UT[p][:, C_S0 - 30:C_S0], func=AF.Copy),
                     reads=[bUT[p]], writes=[self.bUH[i]])

        proj(0)
        for i in range(16):
            if i + 1 < 16:
                proj(i + 1)
            conv(i)
        self.layer_norm("cv_ln_g", "cv_ln_b", [(lambda i: XB[:, i, :], lambda i: bXB[i], AF.Silu, 0)])
        wo = self.cv_w_out.rearrange("(kt p) c -> p kt c", p=128)
        xv = self.xT2[self.ps].rearrange("(t p) n -> p t n", p=128)
        for m in range(16):
            p = m % 2
            if m % 2 == 0:
                slot = self.load_w([(lambda t: t[:, 0:4096].rearrange("p (kt c) -> p kt c", kt=16),
                                     wo[:, :, m * 128:m * 128 + 256])])
                wt = self.WS[slot][:, 0:4096].rearrange("p (kt c) -> p kt c", kt=16)
                bw = self.bWS[slot]
            j = m % 2
            S.op("sp", lambda e, m=m, p=p: e.dma_start(out=XT[p][:], in_=xv[:, m, :]), writes=[bXT[p]], dma=True)
            S.op("act", lambda e, m=m, p=p: e.activation(out=XT[p][:], in_=XT[p][:], func=AF.Identity,
                                                         scale=ALPHA, bias=self.V("cv_b_out", m)),
                 reads=[bXT[p], self.bVEC], writes=[bXT[p]])
            self.mm_group(p, lambda k, wt=wt, j=j: wt[:, k, j * 128:(j + 1) * 128],
                          lambda k, c0, cn: XB[:, k, c0:c0 + cn], 16, [bw], reads_k=bXB)
            S.op("dve", lambda e, m=m, p=p: e.tensor_tensor(out=R[:, m, :], in0=self.PG[p][:, 0:NC], in1=XT[p][:],
                                                            op=ALU.add),
                 reads=[self.bPG[p], bXT[p]], writes=[bR[m]])

    def post_mix_ln(self, l, mask_halo):
        S = self.S
        R, bR, XB, bXB = self.R, self.bR, self.XB, self.bXB
        self.layer_norm("ln_mix_g%d" % l, "ln_mix_b%d" % l,
                        [(lambda i: XB[:, i, :], lambda i: bXB[i], AF.Identity, 0),
                         (lambda i: R[:, i, :], lambda i: bR[i], AF.Identity, 1)])
        hm = self.V("hmask", 0)
        if self.ps == 0:
            S.op("act", lambda e: e.activation(out=self.XH[l][:], in_=XB[:, :, C_S0 - 2:C_S0], func=AF.Copy),
                 reads=list(bXB), writes=[self.bXH[l]])
            S.op("dve", lambda e: e.memset(XB[:, :, 0:HALO], 0.0), writes=list(bXB))
        else:
            S.op("dve", lambda e: e.tensor_scalar(out=XB[:, :, HALO - 2:HALO], in0=self.XH[l][:], scalar1=hm,
                                                  scalar2=None, op0=ALU.mult),
                 reads=[self.bXH[l], self.bVEC], writes=list(bXB))

    def ffn(self, l):
        S = self.S
        self.phase_fence()
        R, bR, XB, bXB = self.R, self.bR, self.XB, self.bXB
        al = self.tmp_alloc()
        GMAX = 8
        HB = al([128, GMAX, NC], BF16)
        GS = [al([128, NC], F32) for _ in range(2)]
        GC = [al([128, NC], F32) for _ in range(2)]
        FS = al([128, GMAX, NS, 2], F32)
        NFS = al([128, GMAX, NS, 2], F32)
        FP = al([128, GMAX, 2], F32)
        bHB = [Buf() for _ in range(GMAX)]
        bGS, bGC = [Buf() for _ in range(2)], [Buf() for _ in range(2)]
        bFS, bNFS, bFP = Buf(), Buf(), Buf()
        wg = self.ff_w_gate[l].rearrange("(kt p) c -> p kt c", p=128)
        wu = self.ff_w_up[l].rearrange("(kt p) c -> p kt c", p=128)
        wd = self.ff_w_down[l].rearrange("(kt p) c -> p kt c", p=128)
        for m in range(16):
            S.op("act", lambda e, m=m: e.activation(out=R[:, m, :], in_=R[:, m, :], func=AF.Identity, scale=ALPHA,
                                                    bias=self.V("ff_b_down%d" % l, m)),
                 reads=[bR[m], self.bVEC], writes=[bR[m]])
        S.op("dve", lambda e: e.memset(HB[:, :, 0:HALO], 0.0), writes=bHB)
        for p in range(2):
            S.op("dve", lambda e, p=p: e.memset(GC[p][:, 0:HALO], 0.0), writes=[bGC[p]])
        groups = [(0, 8), (8, 8), (16, 8), (24, 8), (32, 8), (40, 4)]
        wdw = VOFF["ff_w_dw%d" % l]
        for (f0, gn) in groups:
            S.op("sp", lambda e, f0=f0, gn=gn: e.dma_start(out=FS[:, 0:gn], in_=self.ffnst[l, :, f0:f0 + gn]),
                 writes=[bFS], dma=True)
            for fl in range(gn):
                f = f0 + fl
                p = f % 2
                if fl % 2 == 0:
                    slot = self.load_w([
                        (lambda t: t[:, :].rearrange("p (kt g c) -> p kt g c", kt=16, g=2)[:, :, 0, :], wg[:, :, f * 128:f * 128 + 256]),
                        (lambda t: t[:, :].rearrange("p (kt g c) -> p kt g c", kt=16, g=2)[:, :, 1, :], wu[:, :, f * 128:f * 128 + 256]),
                    ])
                    wt = self.WS[slot][:, :].rearrange("p (kt g c) -> p kt g c", kt=16, g=2)
                    bw = self.bWS[slot]
                j = fl % 2
                self.mm_group(0, lambda k, wt=wt, j=j: wt[:, k, 0, j * 128:(j + 1) * 128],
                              lambda k, c0, cn: XB[:, k, c0:c0 + cn], 16, [bw], reads_k=bXB)
                S.op("act", lambda e, p=p: e.activation(out=GS[p][:], in_=self.PG[0][:, 0:NC], func=AF.Copy),
                     reads=[self.bPG[0]], writes=[bGS[p]])
                self.mm_group(1, lambda k, wt=wt, j=j: wt[:, k, 1, j * 128:(j + 1) * 128],
                              lambda k, c0, cn: XB[:, k, c0:c0 + cn], 16, [bw], reads_k=bXB)
                w0 = self.VEC[:, wdw + f * 3 + 0:wdw + f * 3 + 1]
                w1 = self.VEC[:, wdw + f * 3 + 1:wdw + f * 3 + 2]
                w2 = self.VEC[:, wdw + f * 3 + 2:wdw + f * 3 + 3]
                bb = self.V("ff_b_dw%d" % l, f)
                S.op("dve", lambda e, p=p, w0=w0, bb=bb: e.tensor_scalar(
                    out=GC[p][:, C_P0:C_S0], in0=GS[p][:, C_P0 - 2:C_S0 - 2], scalar1=w0, scalar2=bb,
                    op0=ALU.mult, op1=ALU.add), reads=[bGS[p], self.bVEC], writes=[bGC[p]])
                S.op("dve", lambda e, p=p, w1=w1: e.scalar_tensor_tensor(
                    out=GC[p][:, C_P0:C_S0], in0=GS[p][:, C_P0 - 1:C_S0 - 1], scalar=w1, in1=GC[p][:, C_P0:C_S0],
                    op0=ALU.mult, op1=ALU.add), reads=[bGS[p], bGC[p], self.bVEC], writes=[bGC[p]])
                S.op("dve", lambda e, p=p, w2=w2: e.scalar_tensor_tensor(
                    out=GC[p][:, C_P0:C_S0], in0=GS[p][:, C_P0:C_S0], scalar=w2, in1=GC[p][:, C_P0:C_S0],
                    op0=ALU.mult, op1=ALU.add), reads=[bGS[p], bGC[p], self.bVEC], writes=[bGC[p]])
                S.op("dve", lambda e, p=p, fl=fl, w0=w0, bb=bb: e.tensor_scalar(
                    out=GC[p][:, C_S0:NC], in0=FS[:, fl, :, 0], scalar1=w0, scalar2=bb,
                    op0=ALU.mult, op1=ALU.add), reads=[bFS, self.bVEC], writes=[bGC[p]])
                S.op("dve", lambda e, p=p, fl=fl, w1=w1: e.scalar_tensor_tensor(
                    out=GC[p][:, C_S0:NC], in0=FS[:, fl, :, 1], scalar=w1, in1=GC[p][:, C_S0:NC],
                    op0=ALU.mult, op1=ALU.add), reads=[bFS, bGC[p], self.bVEC], writes=[bGC[p]])
                S.op("dve", lambda e, p=p, w2=w2: e.scalar_tensor_tensor(
                    out=GC[p][:, C_S0:NC], in0=GS[p][:, C_S0:NC], scalar=w2, in1=GC[p][:, C_S0:NC],
                    op0=ALU.mult, op1=ALU.add), reads=[bGS[p], bGC[p], self.bVEC], writes=[bGC[p]])
                S.op("act", lambda e, fl=fl: e.activation(out=NFS[:, fl, :, 0], in_=FS[:, fl, :, 1], func=AF.Copy),
                     reads=[bFS], writes=[bNFS])
                S.op("act", lambda e, p=p, fl=fl: e.activation(out=NFS[:, fl, :, 1], in_=GS[p][:, C_S0:NC], func=AF.Copy),
                     reads=[bGS[p]], writes=[bNFS])
                S.op("act", lambda e, p=p, fl=fl: e.activation(out=FP[:, fl, :], in_=GS[p][:, C_S0 - 2:C_S0], func=AF.Copy),
                     reads=[bGS[p]], writes=[bFP])
                S.op("act", lambda e, p=p: e.activation(out=GC[p][:, C_P0:NC], in_=GC[p][:, C_P0:NC], func=AF.Silu),
                     reads=[bGC[p]], writes=[bGC[p]])
                S.op("dve", lambda e, p=p, fl=fl: e.tensor_tensor(
                    out=HB[:, fl, C_P0:NC], in0=GC[p][:, C_P0:NC], in1=self.PG[1][:, C_P0:NC], op=ALU.mult),
                    reads=[bGC[p], self.bPG[1]], writes=[bHB[fl]])
            if self.ps == 1:
                S.op("sp", lambda e, f0=f0, gn=gn: e.dma_start(out=self.ffnS[l, :, f0:f0 + gn], in_=NFS[:, 0:gn]),
                     reads=[bNFS], dma=True)
                S.op("sp", lambda e, f0=f0, gn=gn: e.dma_start(out=self.ffnP[l, :, f0:f0 + gn], in_=FP[:, 0:gn]),
                     reads=[bFP], dma=True)
            for m in range(16):
                p = m % 2
                if m % 4 == 0:
                    slot = self.load_w([(lambda t, gn=gn: t[:, 0:gn * 512].rearrange("p (kt c) -> p kt c", kt=gn),
                                         wd[:, f0:f0 + gn, m * 128:m * 128 + 512])])
                    wt = self.WS[slot][:, 0:gn * 512].rearrange("p (kt c) -> p kt c", kt=gn)
                    bw = self.bWS[slot]
                j = m % 4
                self.mm_group(p, lambda k, wt=wt, j=j: wt[:, k, j * 128:(j + 1) * 128],
                              lambda k, c0, cn: HB[:, k, c0:c0 + cn], gn, [bw], reads_k=bHB)
                S.op("dve", lambda e, m=m, p=p: e.tensor_tensor(out=R[:, m, :], in0=self.PG[p][:, 0:NC], in1=R[:, m, :],
                                                                op=ALU.add),
                     reads=[self.bPG[p], bR[m]], writes=[bR[m]])

    def post_ffn_ln(self, l):
        R, bR, XB, bXB = self.R, self.bR, self.XB, self.bXB
        self.layer_norm("ln_ffn_g%d" % l, "ln_ffn_b%d" % l,
                        [(lambda i: XB[:, i, :], lambda i: bXB[i], AF.Identity, 0),
                         (lambda i: R[:, i, :], lambda i: bR[i], AF.Identity, 1)])

    def ple(self, l, last):
        S = self.S
        self.phase_fence()
        R, bR, XB, bXB = self.R, self.bR, self.XB, self.bXB
        al = self.tmp_alloc()
        PT = al([128, 2, NC], BF16)
        SG = [al([128, NC], F32) for _ in range(2)]
        TT = [al([128, NC], F32) for _ in range(2)]
        bPT, bSG, bTT = Buf(), [Buf() for _ in range(2)], [Buf() for _ in range(2)]
        pv = self.pT2[self.ps, l].rearrange("(kt p) n -> p kt n", p=128)
        S.op("pool", lambda e: e.dma_start(out=PT[:], in_=pv), writes=[bPT], dma=True)
        wp = self.pl_w_proj[l].rearrange("(kt p) c -> p kt c", p=128)
        wpt = al([128, 2, D], BF16)
        bwp = Buf()
        S.op("pool", lambda e: e.dma_start(out=wpt[:], in_=wp), writes=[bwp], dma=True)
        for m in range(16):
            p = m % 2
            self.mm_group(0, lambda k, m=m: wpt[:, k, m * 128:(m + 1) * 128],
                          lambda k, c0, cn: PT[:, k, c0:c0 + cn], 2, [bwp, bPT])
            S.op("act", lambda e, p=p: e.activation(out=self.YQ[p], in_=self.PG[0][:, 0:NC], func=AF.Square),
                 reads=[self.bPG[0]], writes=[self.bYQ[p]])
            for (c0, cn) in CBS:
                S.op("pe", lambda e, m=m, p=p, c0=c0, cn=cn: e.matmul(
                    self.PG[1][:, c0:c0 + cn], lhsT=self.ONES[:], rhs=self.YQ[p][:, c0:c0 + cn],
                    start=(m == 0), stop=(m == 15)), reads=[self.bYQ[p], self.bONES], writes=[self.bPG[1]])
        self.finish_stats(False)
        wgv = self.pl_w_gate[l].rearrange("(kt p) c -> p kt c", p=128)
        for m in range(16):
            p = m % 2
            if m % 2 == 0:
                slot = self.load_w([(lambda t: t[:, 0:4096].rearrange("p (kt c) -> p kt c", kt=16),
                                     wgv[:, :, m * 128:m * 128 + 256])])
                wt = self.WS[slot][:, 0:4096].rearrange("p (kt c) -> p kt c", kt=16)
                bw = self.bWS[slot]
            j = m % 2
            self.mm_group(0, lambda k, wt=wt, j=j: wt[:, k, j * 128:(j + 1) * 128], lambda k, c0, cn: XB[:, k, c0:c0 + cn], 16, [bw], reads_k=bXB)
            self.mm_group(1, lambda k, m=m: wpt[:, k, m * 128:(m + 1) * 128],
                          lambda k, c0, cn: PT[:, k, c0:c0 + cn], 2, [bwp, bPT])
            S.op("act", lambda e, p=p: e.activation(out=SG[p][:], in_=self.PG[0][:, 0:NC], func=AF.Sigmoid),
                 reads=[self.bPG[0]], writes=[bSG[p]])
            S.op("dve", lambda e, p=p: e.tensor_tensor(out=TT[p][:], in0=self.PG[1][:, 0:NC], in1=self.T2, op=ALU.mult),
                 reads=[self.bPG[1], self.bT2], writes=[bTT[p]])
            S.op("dve", lambda e, p=p, m=m: e.scalar_tensor_tensor(
                out=TT[p][:], in0=TT[p][:], scalar=self.V("pl_g%d" % l, m), in1=SG[p][:], op0=ALU.mult, op1=ALU.mult),
                reads=[bTT[p], bSG[p], self.bVEC], writes=[bTT[p]])
            S.op("dve", lambda e, p=p, m=m: e.tensor_tensor(out=R[:, m, :], in0=R[:, m, :], in1=TT[p][:], op=ALU.add),
                 reads=[bR[m], bTT[p]], writes=[bR[m]])
        if last:
            for m in range(0, 16, 4):
                S.op("sp", lambda e, m=m: e.dma_start(out=self.yT[:, m:m + 4, :], in_=R[:, m:m + 4, :]),
                     reads=bR[m:m + 4], dma=True)
        else:
            for m in range(16):
                S.op("act", lambda e, m=m: e.activation(out=XB[:, m, :], in_=R[:, m, :], func=AF.Copy),
                     reads=[bR[m]], writes=[bXB[m]])

    def build(self):
        for ps in (0, 1):
            self.ps = ps
            self.setup()
            self.conformer()
            self.post_mix_ln(0, True)
            self.ffn(0)
            self.post_ffn_ln(0)
            self.ple(0, last=(self.n_layers == 1 and ps == 1))
            if self.n_layers == 2:
                self.mlstm()
                self.ln_needs_fence = True
                self.post_mix_ln(1, False)
                if ps == 1:
                    self.ffn(1)
                    self.post_ffn_ln(1)
                    self.ple(1, last=True)
        self.S.finalize()
        return self.nc

    def mlstm(self):
        S = self.S
        self.phase_fence()
        op = S.op
        R, bR, XB, bXB = self.R, self.bR, self.XB, self.bXB
        b_scr = Buf("xscr")
        for m in range(0, 16, 4):
            op("sp", lambda e, m=m: e.dma_start(out=self.xscr[:, m:m + 4, :], in_=R[:, m:m + 4, :]),
               reads=bR[m:m + 4], writes=[b_scr], dma=True)
        ar_ = self.tmp_alloc(self.R[:, :, :].rearrange("p a b -> p (a b)"), 16 * NC * 4)
        al = self.tmp_alloc(cap=self.TMPB)
        arena_bufs = []

        def ar(shape, dtype):
            return ar_(shape, dtype)

        def AB(name=""):
            b = Buf(name)
            arena_bufs.append(b)
            return b
        HS = al([128, 16, NC], BF16)
        bHS = [Buf() for _ in range(16)]
        XT = [al([128, NC], F32) for _ in range(2)]
        bXT = [Buf() for _ in range(2)]
        QTs = [ar([128, NC], BF16) for _ in range(2)]
        KTs = [ar([128, NC], BF16) for _ in range(2)]
        VTs = [ar([128, 2, NC], BF16) for _ in range(2)]
        SOs = [ar([128, 2, NC], BF16) for _ in range(2)]
        bQTs, bKTs, bVTs, bSOs = [AB(), AB()], [AB(), AB()], [AB(), AB()], [AB(), AB()]
        CST = self.CSTP
        bCST = self.bCSTP
        CB = ar([128, DV + 1], BF16)
        bCB = AB()
        MP = self.MPP
        bMP = self.bMPP
        CON = ar([128, 5, 128], F32)
        bCON = AB()
        IDB = ar([128, 128], BF16)
        bIDB = AB()
        HGs = [ar([128, DV], F32) for _ in range(2)]
        bHGs = [AB(), AB()]
        BG = ar([128, 16], F32)
        bBG = AB()
        LI, LF, BH, AT = (ar([128, 8, NH], F32) for _ in range(4))
        bLI, bLF, bBH, bAT = AB(), AB(), AB(), AB()
        ZT = ar([128, NH], F32)
        bZT = AB()
        DG = ar([128, 128], F32); bDG = AB()
        DM = ar([128, 128], F32); bDM = AB()
        WI = ar([128, 128], F32); bWI = AB()
        PB = ar([128, 128], BF16); bPB = AB()
        PT_ = ar([128, 128], BF16); bPT_ = AB()
        VCH = ar([128, DV + 2], BF16); bVCH = AB()
        KG = ar([128, 128], BF16); bKG = AB()
        HN = ar([128, DV], F32); bHN = AB()
        HQ = ar([128, DV], F32); bHQ = AB()
        HNB = ar([128, DV], BF16); bHNB = AB()
        SC = ar([128, 16], F32)
        bSC = AB()
        bSCc = [AB() for _ in range(16)]
        BC = ar([128, 2], F32); bBC = AB()
        FEN = ar([128, 8], F32)
        op("dve", lambda e: e.memset(FEN[:], 0.0), writes=list(bR) + arena_bufs)
        IDENT, TRI, MASKNEG, ONESF, E127 = (CON[:, i, :] for i in range(5))
        PX0, PX1 = self.PX
        bS_ = bAB = bSM = self.bPX[0]
        bTR = self.bPX[1]
        bNI = bIN = bUP = self.bPG[0]
        PS_S = PX0[:, 0:128]
        PS_AB = PX0[:, 128:256]
        PS_SM = PX0[:, 256:272]
        PS_TR = PX1[:, 0:256].bitcast(BF16)
        PS_NI = self.PG[0][:, 0:DV]
        PS_IN = self.PG[0][:, 512:512 + DV + 1]
        PS_UP = self.PG[0][:, 1024:1024 + DV + 1]

        op("sp", lambda e: e.dma_start(out=CON[:], in_=self.consts_d.rearrange("p (a b) -> p a b", a=5)), writes=[bCON], dma=True)
        op("sp", lambda e: e.dma_start(out=BG[:], in_=self.bg_rep), writes=[bBG], dma=True)
        op("act", lambda e: e.activation(out=IDB[:], in_=IDENT, func=AF.Copy), reads=[bCON], writes=[bIDB])
        if self.ps == 0:
            op("dve", lambda e: e.memset(CST[:], 0.0), writes=bCST)
            op("dve", lambda e: e.memset(MP[:], 0.0), writes=[bMP])
        else:
            hm_ = self.V("hmask", 0)
            op("dve", lambda e: e.tensor_scalar(out=CST[:], in0=CST[:], scalar1=hm_, scalar2=None, op0=ALU.mult),
               reads=list(bCST) + [self.bVEC], writes=list(bCST))
            op("dve", lambda e: e.tensor_scalar(out=MP[:], in0=MP[:], scalar1=hm_, scalar2=None, op0=ALU.mult),
               reads=[bMP, self.bVEC], writes=[bMP])
        op("dve", lambda e: e.memset(VCH[:, DV:DV + 2], 1.0), writes=[bVCH])
        if self.ps == 0:
            for t_ in range(0, 16, 4):
                op("dve", lambda e, t_=t_: e.memset(HS[:, t_:t_ + 4, :], 0.0), writes=bHS[t_:t_ + 4])
        else:
            op("dve", lambda e: e.memset(HS[:, :, 0:C_P0], 0.0), writes=bHS)
            op("dve", lambda e: e.memset(HS[:, :, C_S0:NC], 0.0), writes=bHS)

        wv = self.ml_w_in.rearrange("(kt p) c -> p kt c", p=128)
        slot = self.load_w([(lambda t: t[:, 0:256].rearrange("p (kt c) -> p kt c", kt=16), wv[:, :, 6144:6160])])
        wg = self.WS[slot][:, 0:256].rearrange("p (kt c) -> p kt c", kt=16)
        bwg = self.bWS[slot]
        for c in range(8):
            c0 = C_P0 + 128 * c
            for kt in range(16):
                op("pe", lambda e, kt=kt, c0=c0: e.matmul(PS_SM, lhsT=XB[:, kt, c0:c0 + 128], rhs=wg[:, kt, :],
                                                          start=(kt == 0), stop=(kt == 15)),
                   reads=[bwg] + bXB, writes=[bSM])
            op("dve", lambda e, c=c: e.tensor_tensor(out=LI[:, c, :], in0=PS_SM[:, 0:8], in1=BG[:, 0:8], op=ALU.add),
               reads=[bSM, bBG], writes=[bLI])
            op("dve", lambda e: e.tensor_tensor(out=ZT[:], in0=PS_SM[:, 8:16], in1=BG[:, 8:16], op=ALU.add),
               reads=[bSM, bBG], writes=[bZT])
            op("act", lambda e: e.activation(out=ZT[:], in_=ZT[:], func=AF.Exp, scale=-1.0), reads=[bZT], writes=[bZT])
            op("dve", lambda e: e.tensor_scalar(out=ZT[:], in0=ZT[:], scalar1=1.0, scalar2=None, op0=ALU.add),
               reads=[bZT], writes=[bZT])
            op("act", lambda e: e.activation(out=ZT[:], in_=ZT[:], func=AF.Ln), reads=[bZT], writes=[bZT])
            op("dve", lambda e, c=c: e.tensor_scalar(out=LF[:, c, :], in0=ZT[:], scalar1=-1.0, scalar2=None, op0=ALU.mult),
               reads=[bZT], writes=[bLF])
            op("pe", lambda e, c=c: e.matmul(PS_SM[:, 0:8], lhsT=TRI, rhs=LF[:, c, :], start=True, stop=True),
               reads=[bCON, bLF], writes=[bSM])
            op("act", lambda e, c=c: e.activation(out=BH[:, c, :], in_=PS_SM[:, 0:8], func=AF.Copy),
               reads=[bSM], writes=[bBH])
            op("dve", lambda e, c=c: e.tensor_tensor(out=AT[:, c, :], in0=LI[:, c, :], in1=BH[:, c, :], op=ALU.subtract),
               reads=[bLI, bBH], writes=[bAT])

        ATT = ar([NH, 8, 128], F32); bATT = AB()
        AMAX = ar([NH, 8], F32); BHL = ar([NH, 8], F32); T2g = ar([NH, 8], F32); MN = ar([NH, 8], F32)
        MPV = ar([NH, 8], F32); GCOL = ar([NH, 8], F32); GST = ar([NH, 8], F32); T1g = ar([NH, 1], F32); M0 = ar([NH, 1], F32)
        bFM = AB()
        XG = ar([NH, 8, 8], F32); bXG = AB()
        GCOLB, GSTB, MPVB, MNB = (ar([128, NH, 8], F32) for _ in range(4))
        bGBC = AB()
        PSg = PX0[0:NH, 0:128]
        for c in range(8):
            op("pe", lambda e, c=c: e.transpose(out=PSg, in_=AT[:, c, :], identity=IDENT), reads=[bAT, bCON], writes=[self.bPX[0]])
            op("act", lambda e, c=c: e.activation(out=ATT[:, c, :], in_=PSg, func=AF.Copy), reads=[self.bPX[0]], writes=[bATT])
        op("dve", lambda e: e.tensor_reduce(out=AMAX[:], in_=ATT[:], axis=AX.X, op=ALU.max), reads=[bATT], writes=[bFM])
        for c in range(8):
            op("pe", lambda e, c=c: e.matmul(PX0[0:NH, 256 + c:257 + c], lhsT=LF[:, c, :], rhs=ONESF[:, 0:1], start=True, stop=True),
               reads=[bLF, bCON], writes=[self.bPX[0]])
        op("act", lambda e: e.activation(out=BHL[:], in_=PX0[0:NH, 256:264], func=AF.Copy), reads=[self.bPX[0]], writes=[bFM])
        op("pe", lambda e: e.transpose(out=PSg, in_=MP[:], identity=IDENT), reads=[bMP, bCON], writes=[self.bPX[0]])
        op("act", lambda e: e.activation(out=M0[:], in_=PX0[0:NH, 0:1], func=AF.Copy), reads=[self.bPX[0]], writes=[bFM])
        op("dve", lambda e: e.tensor_tensor(out=T2g[:], in0=BHL[:], in1=AMAX[:], op=ALU.add), reads=[bFM], writes=[bFM])
        for c in range(8):
            prev = M0[:] if c == 0 else MN[:, c - 1:c]
            op("dve", lambda e, c=c, prev=prev: e.tensor_tensor(out=T1g[:], in0=BHL[:, c:c + 1], in1=prev, op=ALU.add), reads=[bFM], writes=[bFM])
            op("dve", lambda e, c=c: e.tensor_tensor(out=MN[:, c:c + 1], in0=T1g[:], in1=T2g[:, c:c + 1], op=ALU.max), reads=[bFM], writes=[bFM])
        op("dve", lambda e: e.tensor_copy(out=MPV[:, 0:1], in_=M0[:]), reads=[bFM], writes=[bFM])
        op("dve", lambda e: e.tensor_copy(out=MPV[:, 1:8], in_=MN[:, 0:7]), reads=[bFM], writes=[bFM])
        op("dve", lambda e: e.tensor_tensor(out=GCOL[:], in0=BHL[:], in1=MN[:], op=ALU.subtract), reads=[bFM], writes=[bFM])
        op("dve", lambda e: e.tensor_tensor(out=GST[:], in0=MPV[:], in1=GCOL[:], op=ALU.add), reads=[bFM], writes=[bFM])
        op("act", lambda e: e.activation(out=GST[:], in_=GST[:], func=AF.Exp), reads=[bFM], writes=[bFM])
        for V_, VB_ in ((GCOL, GCOLB), (GST, GSTB), (MPV, MPVB), (MN, MNB)):
            op("dve", lambda e, V_=V_: e.tensor_tensor(out=XG[:], in0=V_[:][:, None, :].broadcast_to([NH, NH, 8]),
                                                       in1=IDENT[0:NH, 0:NH][:, :, None].broadcast_to([NH, NH, 8]), op=ALU.mult),
               reads=[bFM, bCON], writes=[bXG])
            op("pe", lambda e: e.matmul(PX0[:, 0:64], lhsT=ONESF[0:NH, :], rhs=XG[:].rearrange("k h c -> k (h c)"), start=True, stop=True),
               reads=[bXG, bCON], writes=[self.bPX[0]])
            op("act", lambda e, VB_=VB_: e.activation(out=VB_[:].rearrange("p h c -> p (h c)"), in_=PX0[:, 0:64], func=AF.Copy),
               reads=[self.bPX[0]], writes=[bGBC])
        op("dve", lambda e: e.tensor_copy(out=MP[:], in_=MNB[:, :, 7]), reads=[bGBC, bMP], writes=[bMP])

        if self.ps == 1:
            CSB = [ar([128, NS, DV], F32) for _ in range(1)]
            bCSB = [AB() for _ in range(1)]
            NSI = ar([NS, NH, DK], F32); bNSI = AB()
            NSO = NSI; bNSO = bNSI
            MS0 = ar([NS, NH], F32); bMS0 = AB()
            SG8 = ar([NS, 6, NH], F32); bSG8 = AB()
            ID16 = al([128, NS, NS], F32); bID16 = Buf()
            ZQ = al([128, NS, NS], F32); bZQ = Buf()
            QS = DG[0:NS, :]; bQS = bDG
            KS = WI[0:NS, :]; bKS = bWI
            VS = ar([NS, DV], F32); bVS = AB()
            KM = al([NS, NS, DK], F32); bKM = Buf()
            DW = ar([NS, NS], F32); bDW = AB()
            GB = ar([128, NS], F32); bGB = AB()
            SS = ar([NS, 16], F32); bSS = AB()
            TQ = DM[0:NS, :]; bTQ = bDM
            HNs = HN[0:NS, :]; bHNs = bHN
            HQs = HQ[0:NS, :]; bHQs = bHQ
            HNBs = HNB[0:NS, :]; bHNBs = bHNB
            op("sp", lambda e: e.dma_start(out=NSI[:], in_=self.nS), writes=[bNSI], dma=True)
            op("sp", lambda e: e.dma_start(out=MS0[:], in_=self.mS), writes=[bMS0], dma=True)
            op("sp", lambda e: e.dma_start(out=ID16[:], in_=self.id16.rearrange("p (a b) -> p a b", a=NS)), writes=[bID16], dma=True)
            for kt in range(16):
                op("pe", lambda e, kt=kt: e.matmul(PS_SM[0:NS, :], lhsT=XB[:, kt, C_S0:NC], rhs=wg[:, kt, :],
                                                   start=(kt == 0), stop=(kt == 15)), reads=[bwg] + bXB, writes=[bSM])
            op("dve", lambda e: e.tensor_tensor(out=SG8[:, 0, :], in0=PS_SM[0:NS, 0:8], in1=BG[0:NS, 0:8], op=ALU.add),
               reads=[bSM, bBG], writes=[bSG8])
            op("dve", lambda e: e.tensor_tensor(out=ZT[0:NS, :], in0=PS_SM[0:NS, 8:16], in1=BG[0:NS, 8:16], op=ALU.add),
               reads=[bSM, bBG], writes=[bZT])
            op("act", lambda e: e.activation(out=ZT[0:NS, :], in_=ZT[0:NS, :], func=AF.Exp, scale=-1.0), reads=[bZT], writes=[bZT])
            op("dve", lambda e: e.tensor_scalar(out=ZT[0:NS, :], in0=ZT[0:NS, :], scalar1=1.0, scalar2=None, op0=ALU.add),
               reads=[bZT], writes=[bZT])
            op("act", lambda e: e.activation(out=ZT[0:NS, :], in_=ZT[0:NS, :], func=AF.Ln), reads=[bZT], writes=[bZT])
            op("dve", lambda e: e.tensor_tensor(out=SG8[:, 1, :], in0=MS0[:], in1=ZT[0:NS, :], op=ALU.subtract),
               reads=[bMS0, bZT], writes=[bSG8])
            op("dve", lambda e: e.tensor_tensor(out=SG8[:, 2, :], in0=SG8[:, 1, :], in1=SG8[:, 0, :], op=ALU.max), reads=[bSG8], writes=[bSG8])
            op("dve", lambda e: e.tensor_tensor(out=SG8[:, 3, :], in0=SG8[:, 0, :], in1=SG8[:, 2, :], op=ALU.subtract), reads=[bSG8], writes=[bSG8])
            op("dve", lambda e: e.tensor_tensor(out=SG8[:, 4, :], in0=SG8[:, 1, :], in1=SG8[:, 2, :], op=ALU.subtract), reads=[bSG8], writes=[bSG8])
            op("act", lambda e: e.activation(out=SG8[:, 3:5, :], in_=SG8[:, 3:5, :], func=AF.Exp), reads=[bSG8], writes=[bSG8])
            op("act", lambda e: e.activation(out=SG8[:, 5, :], in_=SG8[:, 2, :], func=AF.Exp, scale=-1.0), reads=[bSG8], writes=[bSG8])
            op("sp", lambda e: e.dma_start(out=self.mSo, in_=SG8[:, 2, :]), reads=[bSG8], dma=True)

        scale_q = float(DK) ** -0.5

        def proj_items(h):
            hb = h % 2
            QT, KT, VTt, SO = QTs[hb], KTs[hb], VTs[hb], SOs[hb]
            bQT, bKT, bVT, bSO = bQTs[hb], bKTs[hb], bVTs[hb], bSOs[hb]
            items = []
            stt = {}

            def loads():
                s1 = self.load_w([
                    (lambda t: t[:, 0:8192].rearrange("p (kt c) -> p kt c", kt=16)[:, :, 0:128], wv[:, :, h * 128:(h + 1) * 128]),
                    (lambda t: t[:, 0:8192].rearrange("p (kt c) -> p kt c", kt=16)[:, :, 128:256], wv[:, :, 1024 + h * 128:1024 + (h + 1) * 128]),
                    (lambda t: t[:, 0:8192].rearrange("p (kt c) -> p kt c", kt=16)[:, :, 256:512], wv[:, :, 2048 + h * 256:2048 + (h + 1) * 256]),
                ])
                stt["w1"] = self.WS[s1][:, 0:8192].rearrange("p (kt c) -> p kt c", kt=16)
                stt["bw1"] = self.bWS[s1]
                s2 = self.load_w([(lambda t: t[:, 0:4096].rearrange("p (kt c) -> p kt c", kt=16),
                                   wv[:, :, 4096 + h * 256:4096 + (h + 1) * 256])])
                stt["w2"] = self.WS[s2][:, 0:4096].rearrange("p (kt c) -> p kt c", kt=16)
                stt["bw2"] = self.bWS[s2]
                op("sp", lambda e: e.dma_start(out=HGs[hb][:], in_=self.hg_rep[:, h * DV:(h + 1) * DV]), writes=[bHGs[hb]], dma=True)
            items.append(loads)
            cb_last = [(C_S0 - 128, 1024 - (C_S0 - 128)), (1024, C_S0 - 1024)] if self.ps == 0 else None

            def group(wkey, col0, evac, cbs):
                cbl = CBS if cbs is None else cbs
                for k in range(16):
                    for (c0, cn) in cbl:
                        def mm(k=k, c0=c0, cn=cn):
                            w = stt[wkey]
                            bw = stt["b" + wkey]
                            op("pe", lambda e: e.matmul(self.PG[1][:, c0:c0 + cn], lhsT=w[:, k, col0:col0 + 128],
                                                        rhs=XB[:, k, c0:c0 + cn], start=(k == 0), stop=(k == 15)),
                               reads=[bw, bXB[k]], writes=[self.bPG[1]])
                        items.append(mm)
                items.append(evac)
            group("w1", 0, lambda: op("act", lambda e: e.activation(out=QT[:], in_=self.PG[1][:, 0:NC], func=AF.Copy, scale=scale_q),
                                      reads=[self.bPG[1]], writes=[bQT]), cb_last)
            group("w1", 128, lambda: op("act", lambda e: e.activation(out=KT[:], in_=self.PG[1][:, 0:NC], func=AF.Copy),
                                        reads=[self.bPG[1]], writes=[bKT]), None)
            for j in range(2):
                group("w1", 256 + j * 128, lambda j=j: op("act", lambda e: e.activation(out=VTt[:, j, :], in_=self.PG[1][:, 0:NC], func=AF.Copy),
                                                         reads=[self.bPG[1]], writes=[bVT]), None)
            for j in range(2):
                group("w2", j * 128, lambda j=j: op("act", lambda e: e.activation(out=SO[:, j, :], in_=self.PG[1][:, 0:NC], func=AF.Sigmoid),
                                                    reads=[self.bPG[1]], writes=[bSO]), cb_last)
            return items

        def head_body(h, QT, KT, VTt, SO, bQT, bKT, bVT, bSO, HGc, bHGc, pending):
            def emit_pending(n):
                for _ in range(min(n, len(pending))):
                    pending.pop(0)()
            op("act", lambda e, h=h: e.activation(out=CB[:], in_=CST[:, h, :], func=AF.Copy), reads=[bCST[h]], writes=[bCB])
            for c in range(8):
                c0 = C_P0 + 128 * c
                mp = MPVB[:, h, c:c + 1]
                full = (self.ps == 1) or (c == 7)
                if full:
                    op("pe", lambda e, c0=c0: e.matmul(PS_S, lhsT=QT[:, c0:c0 + 128], rhs=KT[:, c0:c0 + 128], start=True, stop=True),
                       reads=[bQT, bKT], writes=[bS_])
                if full:
                    op("act", lambda e, c=c, h=h: e.activation(out=DG[:], in_=IDENT, func=AF.Copy, scale=AT[:, c, h:h + 1]),
                       reads=[bCON, bAT], writes=[bDG])
                    op("pe", lambda e: e.matmul(PS_AB, lhsT=ONESF, rhs=DG[:], start=True, stop=True),
                       reads=[bCON, bDG], writes=[bAB])
                emit_pending(10)
                if full:
                    op("dve", lambda e, c=c, h=h: e.scalar_tensor_tensor(out=DM[:], in0=PS_AB, scalar=BH[:, c, h:h + 1], in1=MASKNEG,
                                                                      op0=ALU.add, op1=ALU.add),
                       reads=[bAB, bBH, bCON], writes=[bDM])
                    op("dve", lambda e: e.tensor_reduce(out=SC[:, 0:1], in_=DM[:], axis=AX.X, op=ALU.max), reads=[bDM], writes=[bSCc[0]])
                    op("dve", lambda e, c=c, h=h, mp=mp: e.tensor_tensor(out=SC[:, 1:2], in0=BH[:, c, h:h + 1], in1=mp, op=ALU.add),
                       reads=[bBH, bGBC], writes=[bSCc[1]])
                    op("dve", lambda e: e.tensor_tensor(out=SC[:, 2:3], in0=SC[:, 0:1], in1=SC[:, 1:2], op=ALU.max), reads=[bSCc[0], bSCc[1]], writes=[bSCc[2]])
                    op("dve", lambda e: e.tensor_scalar(out=SC[:, 3:4], in0=SC[:, 2:3], scalar1=-1.0, scalar2=None, op0=ALU.mult),
                       reads=[bSCc[2]], writes=[bSCc[3]])
                if full:
                    op("act", lambda e: e.activation(out=WI[:], in_=DM[:], func=AF.Exp, bias=SC[:, 3:4], scale=1.0),
                       reads=[bDM, bSCc[3]], writes=[bWI])
                    op("act", lambda e: e.activation(out=SC[:, 4:5], in_=SC[:, 1:2], func=AF.Exp, bias=SC[:, 3:4], scale=1.0),
                       reads=[bSCc[1], bSCc[3]], writes=[bSCc[4]])
                    op("act", lambda e: e.activation(out=SC[:, 5:6], in_=SC[:, 3:4], func=AF.Exp), reads=[bSCc[3]], writes=[bSCc[5]])
                    op("dve", lambda e: e.tensor_tensor(out=PB[:], in0=PS_S, in1=WI[:], op=ALU.mult), reads=[bS_, bWI], writes=[bPB])
                    op("act", lambda e: e.activation(out=WI[:], in_=PB[:], func=AF.Copy, accum_out=SC[:, 6:7]), reads=[bPB], writes=[bWI, bSCc[6]])
                    op("pe", lambda e: e.transpose(out=PS_TR[:, 0:128], in_=PB[:], identity=IDB[:]), reads=[bPB, bIDB], writes=[bTR])
                    op("act", lambda e: e.activation(out=PT_[:], in_=PS_TR[:, 0:128], func=AF.Copy), reads=[bTR], writes=[bPT_])
                for j in range(2):
                    op("pe", lambda e, j=j, c0=c0: e.transpose(out=PS_TR[:, 128 + j * 128:256 + j * 128], in_=VTt[:, j, c0:c0 + 128], identity=IDB[:]),
                       reads=[bVT, bIDB], writes=[bTR])
                op("act", lambda e: e.activation(out=VCH[:, 0:DV], in_=PS_TR[:, 128:384], func=AF.Copy), reads=[bTR], writes=[bVCH])
                op("pe", lambda e, c0=c0: e.transpose(out=PS_TR[:, 384:512], in_=KT[:, c0:c0 + 128], identity=IDB[:]),
                   reads=[bKT, bIDB], writes=[bTR])
                if not full:
                    emit_pending(28)
                if full:
                    op("pe", lambda e: e.matmul(PS_NI, lhsT=PT_[:], rhs=VCH[:, 0:DV], start=True, stop=True),
                       reads=[bPT_, bVCH], writes=[bNI])
                    op("pe", lambda e, c0=c0: e.matmul(PS_IN, lhsT=QT[:, c0:c0 + 128], rhs=CB[:], start=True, stop=True),
                       reads=[bQT, bCB], writes=[bIN])
                    emit_pending(28)
                    op("dve", lambda e: e.tensor_scalar(out=HN[:], in0=PS_IN[:, 0:DV], scalar1=SC[:, 4:5], scalar2=None, op0=ALU.mult),
                       reads=[bIN, bSCc[4]], writes=[bHN])
                    op("dve", lambda e: e.tensor_tensor(out=HN[:], in0=HN[:], in1=PS_NI, op=ALU.add), reads=[bHN, bNI], writes=[bHN])
                    op("dve", lambda e: e.scalar_tensor_tensor(out=SC[:, 7:8], in0=PS_IN[:, DV:DV + 1], scalar=SC[:, 4:5], in1=SC[:, 6:7],
                                                               op0=ALU.mult, op1=ALU.add), reads=[bIN, bSCc[4], bSCc[6]], writes=[bSCc[7]])
                    op("dve", lambda e: e.tensor_scalar(out=SC[:, 14:15], in0=SC[:, 7:8], scalar1=-1.0, scalar2=None, op0=ALU.mult), reads=[bSCc[7]], writes=[bSCc[14]])
                    op("dve", lambda e: e.tensor_tensor(out=SC[:, 7:8], in0=SC[:, 7:8], in1=SC[:, 14:15], op=ALU.max), reads=[bSCc[7], bSCc[14]], writes=[bSCc[7]])
                    op("dve", lambda e: e.tensor_tensor(out=SC[:, 7:8], in0=SC[:, 7:8], in1=SC[:, 5:6], op=ALU.max), reads=[bSCc[5], bSCc[7]], writes=[bSCc[7]])
                    op("act", lambda e: e.activation(out=HQ[:], in_=HN[:], func=AF.Identity, accum_out=SC[:, 12:13]),
                       reads=[bHN], writes=[bHQ, bSCc[12]])
                    op("act", lambda e: e.activation(out=HQ[:], in_=HN[:], func=AF.Square, accum_out=SC[:, 13:14]),
                       reads=[bHN], writes=[bHQ, bSCc[13]])
                    op("dve", lambda e: e.tensor_scalar(out=SC[:, 12:13], in0=SC[:, 12:13], scalar1=1.0 / DV, scalar2=None, op0=ALU.mult),
                       reads=[bSCc[12]], writes=[bSCc[12]])
                    op("dve", lambda e: e.tensor_tensor(out=SC[:, 8:9], in0=SC[:, 12:13], in1=SC[:, 12:13], op=ALU.mult),
                       reads=[bSCc[12]], writes=[bSCc[8]])
                    op("dve", lambda e: e.scalar_tensor_tensor(out=SC[:, 13:14], in0=SC[:, 13:14], scalar=1.0 / DV, in1=SC[:, 8:9],
                                                               op0=ALU.mult, op1=ALU.subtract), reads=[bSCc[13], bSCc[8]], writes=[bSCc[13]])
                    op("dve", lambda e: e.tensor_tensor(out=SC[:, 8:9], in0=SC[:, 7:8], in1=SC[:, 7:8], op=ALU.mult),
                       reads=[bSCc[7]], writes=[bSCc[8]])
                    op("dve", lambda e: e.scalar_tensor_tensor(out=SC[:, 13:14], in0=SC[:, 8:9], scalar=LN_EPS, in1=SC[:, 13:14],
                                                               op0=ALU.mult, op1=ALU.add), reads=[bSCc[8], bSCc[13]], writes=[bSCc[13]])
                    op("act", lambda e: e.activation(out=SC[:, 13:14], in_=SC[:, 13:14], func=AF.Ln), reads=[bSCc[13]], writes=[bSCc[13]])
                    op("act", lambda e: e.activation(out=SC[:, 13:14], in_=SC[:, 13:14], func=AF.Exp, scale=-0.5), reads=[bSCc[13]], writes=[bSCc[13]])
                    op("dve", lambda e: e.tensor_scalar(out=HN[:], in0=HN[:], scalar1=SC[:, 12:13], scalar2=SC[:, 13:14],
                                                        op0=ALU.subtract, op1=ALU.mult), reads=[bHN, bSCc[12], bSCc[13]], writes=[bHN])
                    op("dve", lambda e, h=h: e.tensor_tensor(out=HNB[:], in0=HN[:], in1=HGc[:, :], op=ALU.mult),
                       reads=[bHN, bHGc], writes=[bHNB])
                op("act", lambda e, c=c, h=h: e.activation(out=SC[:, 10:11], in_=AT[:, c, h:h + 1], func=AF.Exp, bias=GCOLB[:, h, c:c + 1], scale=1.0),
                   reads=[bAT, bGBC], writes=[bSCc[10]])
                op("act", lambda e: e.activation(out=KG[:], in_=PS_TR[:, 384:512], func=AF.Copy, scale=SC[:, 10:11]),
                   reads=[bTR, bSCc[10]], writes=[bKG])
                op("pe", lambda e: e.matmul(PS_UP, lhsT=KG[:], rhs=VCH[:, 0:DV + 1], start=True, stop=True),
                   reads=[bKG, bVCH], writes=[bUP])
                op("dve", lambda e, h=h, c=c: e.scalar_tensor_tensor(out=CST[:, h, :], in0=CST[:, h, :], scalar=GSTB[:, h, c:c + 1], in1=PS_UP,
                                                                     op0=ALU.mult, op1=ALU.add), reads=[bCST[h], bUP, bGBC], writes=[bCST[h]])
                op("act", lambda e, h=h: e.activation(out=CB[:], in_=CST[:, h, :], func=AF.Copy), reads=[bCST[h]], writes=[bCB])
                if full:
                    for j in range(2):
                        op("pe", lambda e, j=j: e.transpose(out=PS_TR[:, j * 128:(j + 1) * 128], in_=HNB[:, j * 128:(j + 1) * 128], identity=IDB[:]),
                           reads=[bHNB, bIDB], writes=[bTR])
                        op("dve", lambda e, j=j, h=h, c0=c0: e.tensor_tensor(out=HS[:, 2 * h + j, c0:c0 + 128], in0=PS_TR[:, j * 128:(j + 1) * 128],
                                                                           in1=SO[:, j, c0:c0 + 128], op=ALU.mult),
                           reads=[bTR, bSO], writes=[bHS[2 * h + j]])

            emit_pending(len(pending))
            if self.ps == 0:
                return
            sb_ = 0
            CSh = CSB[sb_]
            op("sp", lambda e, h=h, CSh=CSh: e.dma_start(out=CSh[:], in_=self.cS[:, h].rearrange("i k v -> k i v")),
               writes=[bCSB[sb_]], dma=True)
            op("pe", lambda e: e.transpose(out=PS_TR[0:NS, 0:128], in_=QT[:, C_S0:NC], identity=IDB[:]), reads=[bQT, bIDB], writes=[bTR])
            op("pe", lambda e: e.transpose(out=PS_TR[0:NS, 128:256], in_=KT[:, C_S0:NC], identity=IDB[:]), reads=[bKT, bIDB], writes=[bTR])
            for j in range(2):
                op("pe", lambda e, j=j: e.transpose(out=PS_TR[0:NS, 256 + j * 128:384 + j * 128], in_=VTt[:, j, C_S0:NC], identity=IDB[:]),
                   reads=[bVT, bIDB], writes=[bTR])
            op("dve", lambda e: e.tensor_copy(out=QS[:], in_=PS_TR[0:NS, 0:128]), reads=[bTR], writes=[bQS])
            op("dve", lambda e: e.tensor_copy(out=VS[:], in_=PS_TR[0:NS, 256:512]), reads=[bTR], writes=[bVS])
            op("dve", lambda e, h=h: e.tensor_scalar(out=KS[:], in0=PS_TR[0:NS, 128:256], scalar1=SG8[:, 3, h:h + 1], scalar2=None, op0=ALU.mult),
               reads=[bTR, bSG8], writes=[bKS])
            op("dve", lambda e: e.tensor_tensor(out=TQ[:, 0:DK], in0=QS[:], in1=KS[:], op=ALU.mult), reads=[bQS, bKS], writes=[bTQ])
            op("dve", lambda e: e.tensor_reduce(out=SS[:, 0:1], in_=TQ[:, 0:DK], axis=AX.X, op=ALU.add), reads=[bTQ], writes=[bSS])
            op("dve", lambda e, h=h: e.tensor_tensor(out=TQ[:, 0:DK], in0=QS[:], in1=NSI[:, h, :], op=ALU.mult), reads=[bQS, bNSI], writes=[bTQ])
            op("dve", lambda e: e.tensor_reduce(out=SS[:, 1:2], in_=TQ[:, 0:DK], axis=AX.X, op=ALU.add), reads=[bTQ], writes=[bSS])
            op("dve", lambda e, h=h: e.scalar_tensor_tensor(out=NSO[:, h, :], in0=NSI[:, h, :], scalar=SG8[:, 4, h:h + 1], in1=KS[:],
                                                            op0=ALU.mult, op1=ALU.add), reads=[bNSI, bSG8, bKS], writes=[bNSO])
            op("dve", lambda e: e.tensor_tensor(out=ZQ[:], in0=QT[:, C_S0:NC][:, None, :].broadcast_to([128, NS, NS]), in1=ID16[:], op=ALU.mult),
               reads=[bQT, bID16], writes=[bZQ])
            for i in range(NS):
                op("pe", lambda e, i=i, CSh=CSh: e.matmul(PS_NI[0:NS, :], lhsT=ZQ[:, i, :], rhs=CSh[:, i, :], start=(i == 0), stop=(i == NS - 1)),
                   reads=[bZQ, bCSB[sb_]], writes=[bNI])
            op("dve", lambda e: e.tensor_scalar(out=HNs[:], in0=VS[:], scalar1=SS[:, 0:1], scalar2=None, op0=ALU.mult),
               reads=[bVS, bSS], writes=[bHNs])
            op("dve", lambda e, h=h: e.scalar_tensor_tensor(out=HNs[:], in0=PS_NI[0:NS, :], scalar=SG8[:, 4, h:h + 1], in1=HNs[:],
                                                            op0=ALU.mult, op1=ALU.add), reads=[bNI, bSG8, bHNs], writes=[bHNs])
            op("dve", lambda e, h=h: e.scalar_tensor_tensor(out=SS[:, 2:3], in0=SS[:, 1:2], scalar=SG8[:, 4, h:h + 1], in1=SS[:, 0:1],
                                                            op0=ALU.mult, op1=ALU.add), reads=[bSS, bSG8], writes=[bSS])
            op("dve", lambda e: e.tensor_scalar(out=SS[:, 3:4], in0=SS[:, 2:3], scalar1=-1.0, scalar2=None, op0=ALU.mult), reads=[bSS], writes=[bSS])
            op("dve", lambda e: e.tensor_tensor(out=SS[:, 2:3], in0=SS[:, 2:3], in1=SS[:, 3:4], op=ALU.max), reads=[bSS], writes=[bSS])
            op("dve", lambda e, h=h: e.tensor_tensor(out=SS[:, 2:3], in0=SS[:, 2:3], in1=SG8[:, 5, h:h + 1], op=ALU.max), reads=[bSS, bSG8], writes=[bSS])
            op("dve", lambda e: e.reciprocal(out=SS[:, 4:5], in_=SS[:, 2:3]), reads=[bSS], writes=[bSS])
            op("dve", lambda e: e.tensor_scalar(out=HNs[:], in0=HNs[:], scalar1=SS[:, 4:5], scalar2=None, op0=ALU.mult), reads=[bHNs, bSS], writes=[bHNs])
            op("dve", lambda e: e.tensor_reduce(out=SS[:, 5:6], in_=HNs[:], axis=AX.X, op=ALU.add), reads=[bHNs], writes=[bSS])
            op("dve", lambda e: e.tensor_scalar(out=SS[:, 5:6], in0=SS[:, 5:6], scalar1=1.0 / DV, scalar2=None, op0=ALU.mult), reads=[bSS], writes=[bSS])
            op("dve", lambda e: e.tensor_scalar(out=HNs[:], in0=HNs[:], scalar1=SS[:, 5:6], scalar2=None, op0=ALU.subtract), reads=[bHNs, bSS], writes=[bHNs])
            op("dve", lambda e: e.tensor_tensor(out=HQs[:], in0=HNs[:], in1=HNs[:], op=ALU.mult), reads=[bHNs], writes=[bHQs])
            op("dve", lambda e: e.tensor_reduce(out=SS[:, 6:7], in_=HQs[:], axis=AX.X, op=ALU.add), reads=[bHQs], writes=[bSS])
            op("dve", lambda e: e.tensor_scalar(out=SS[:, 6:7], in0=SS[:, 6:7], scalar1=1.0 / DV, scalar2=LN_EPS, op0=ALU.mult, op1=ALU.add),
               reads=[bSS], writes=[bSS])
            op("act", lambda e: e.activation(out=SS[:, 6:7], in_=SS[:, 6:7], func=AF.Ln), reads=[bSS], writes=[bSS])
            op("act", lambda e: e.activation(out=SS[:, 6:7], in_=SS[:, 6:7], func=AF.Exp, scale=-0.5), reads=[bSS], writes=[bSS])
            op("dve", lambda e, h=h: e.scalar_tensor_tensor(out=HNBs[:], in0=HNs[:], scalar=SS[:, 6:7], in1=HGc[0:NS, :],
                                                            op0=ALU.mult, op1=ALU.mult), reads=[bHNs, bSS, bHGc], writes=[bHNBs])
            for j in range(2):
                op("pe", lambda e, j=j: e.transpose(out=PS_TR[:, j * NS:(j + 1) * NS], in_=HNBs[:, j * 128:(j + 1) * 128], identity=IDB[0:NS, 0:NS]),
                   reads=[bHNBs, bIDB], writes=[bTR])
                op("dve", lambda e, j=j, h=h: e.tensor_tensor(out=HS[:, 2 * h + j, C_S0:NC], in0=PS_TR[:, j * NS:(j + 1) * NS],
                                                              in1=SO[:, j, C_S0:NC], op=ALU.mult),
                   reads=[bTR, bSO], writes=[bHS[2 * h + j]])
            op("dve", lambda e, h=h: e.tensor_scalar(out=DW[:], in0=IDENT[0:NS, 0:NS], scalar1=SG8[:, 4, h:h + 1], scalar2=None, op0=ALU.mult),
               reads=[bCON, bSG8], writes=[bDW])
            op("pe", lambda e: e.matmul(PS_SM[:, 0:NS], lhsT=ONESF[0:NS, :], rhs=DW[:], start=True, stop=True), reads=[bCON, bDW], writes=[bSM])
            op("act", lambda e: e.activation(out=GB[:], in_=PS_SM[:, 0:NS], func=AF.Copy), reads=[bSM], writes=[bGB])
            op("dve", lambda e: e.tensor_tensor(out=KM[:], in0=KS[:][:, None, :].broadcast_to([NS, NS, DK]),
                                                in1=IDENT[0:NS, 0:NS][:, :, None].broadcast_to([NS, NS, DK]), op=ALU.mult),
               reads=[bKS, bCON], writes=[bKM])
            bPSO = [Buf() for _ in range(3)]
            for b_ in bPSO:
                b_.excl = True
            for i in range(NS):
                slot_ = i % 3
                PS_O = self.PG[1][:, slot_ * 512:slot_ * 512 + DV]
                op("pe", lambda e, i=i, PS_O=PS_O: e.matmul(PS_O, lhsT=KM[:, i, :], rhs=VS[:], start=True, stop=True),
                   reads=[bKM, bVS], writes=[bPSO[slot_]] + ([self.bPG[1]] if i == 0 else []))
                op("dve", lambda e, i=i, PS_O=PS_O, CSh=CSh: e.scalar_tensor_tensor(out=CSh[:, i, :], in0=CSh[:, i, :], scalar=GB[:, i:i + 1], in1=PS_O,
                                                                                   op0=ALU.mult, op1=ALU.add),
                   reads=[bCSB[sb_], bGB, bPSO[slot_]], writes=[bCSB[sb_]])
            op("dve", lambda e: e.memset(FEN[:], 0.0), reads=bPSO, writes=[self.bPG[1]])
            op("sp", lambda e, h=h, CSh=CSh: e.dma_start(out=self.cSo[:, h].rearrange("i k v -> k i v"), in_=CSh[:]),
               reads=[bCSB[sb_]], dma=True)
        for it in proj_items(0):
            it()
        for h in range(NH):
            hb = h % 2
            pending = proj_items(h + 1) if h + 1 < NH else []
            head_body(h, QTs[hb], KTs[hb], VTs[hb], SOs[hb], bQTs[hb], bKTs[hb], bVTs[hb], bSOs[hb], HGs[hb], bHGs[hb], pending)
        if self.ps == 1:
            op("sp", lambda e: e.dma_start(out=self.nSo, in_=NSO[:]), reads=[bNSO], dma=True)
        if self.ps == 1:
            op("sp", lambda e: e.dma_start(out=self.cP, in_=CST[:]), reads=bCST, dma=True)
            op("sp", lambda e: e.dma_start(out=self.mP, in_=MP[:]), reads=[bMP], dma=True)
        wo = self.ml_w_out.rearrange("(kt p) c -> p kt c", p=128)
        for m in range(16):
            p = m % 2
            if m % 2 == 0:
                slot = self.load_w([(lambda t: t[:, 0:4096].rearrange("p (kt c) -> p kt c", kt=16),
                                     wo[:, :, m * 128:m * 128 + 256])])
                wt = self.WS[slot][:, 0:4096].rearrange("p (kt c) -> p kt c", kt=16)
                bw = self.bWS[slot]
            j = m % 2
            op("sp", lambda e, m=m, p=p: e.dma_start(out=XT[p][:], in_=self.xscr[:, m, :]), reads=[b_scr], writes=[bXT[p]], dma=True)
            op("act", lambda e, m=m, p=p: e.activation(out=XT[p][:], in_=XT[p][:], func=AF.Identity,
                                                       scale=ALPHA, bias=self.V("ml_b_out", m)),
               reads=[bXT[p], self.bVEC], writes=[bXT[p]])
            self.mm_group(p, lambda k, wt=wt, j=j: wt[:, k, j * 128:(j + 1) * 128],
                          lambda k, c0, cn: HS[:, k, c0:c0 + cn], 16, [bw], reads_k=bHS)
            op("dve", lambda e, m=m, p=p: e.tensor_tensor(out=R[:, m, :], in0=self.PG[p][:, 0:NC], in1=XT[p][:], op=ALU.add),
               reads=[self.bPG[p], bXT[p]], writes=[bR[m]] + (arena_bufs if m == 0 else []))


def make_in_maps(inp):
    xp = np.asarray(inp["x_prompt"], np.float32)
    xs = np.asarray(inp["x_sample"], np.float32)
    pp = np.asarray(inp["p_prompt"], np.float32)
    ps = np.asarray(inp["p_sample"], np.float32)
    sconv = np.asarray(inp["state_conv"], np.float32)
    sffn = np.asarray(inp["state_ffn_conv"], np.float32)
    shared = {k: np.ascontiguousarray(np.asarray(inp[k], np.float32)) for k in
              ("ff_w_gate", "ff_w_up", "ff_w_down", "pl_w_proj", "pl_w_gate")}
    shared["cv_w_in"] = np.ascontiguousarray(np.asarray(inp["cv_w_in"], np.float32)[0])
    shared["cv_w_out"] = np.ascontiguousarray(np.asarray(inp["cv_w_out"], np.float32)[0])
    shared["ml_w_in"] = np.ascontiguousarray(np.asarray(inp["ml_w_in"], np.float32)[0])
    shared["ml_w_out"] = np.ascontiguousarray(np.asarray(inp["ml_w_out"], np.float32)[0])
    ii = np.arange(128)
    consts = np.zeros((128, 5, 128), np.float32)
    consts[:, 0, :] = np.eye(128, dtype=np.float32)
    consts[:, 1, :] = (ii[:, None] <= ii[None, :]).astype(np.float32)
    consts[:, 2, :] = np.where(ii[None, :] <= ii[:, None], 0.0, -1.0e9)
    consts[:, 3, :] = 1.0
    consts[127, 4, :] = 1.0
    shared["consts"] = consts.reshape(128, 5 * 128)
    shared["id16"] = np.ascontiguousarray(np.broadcast_to(np.eye(NS, dtype=np.float32).reshape(1, NS * NS), (128, NS * NS)))
    shared["bg_rep"] = np.ascontiguousarray(np.broadcast_to(np.asarray(inp["ml_b_gates"], np.float32)[0].reshape(1, 16), (128, 16)))
    shared["hg_rep"] = np.ascontiguousarray(np.broadcast_to(np.asarray(inp["ml_hn_g"], np.float32)[0].reshape(1, D), (128, D)))
    maps = []
    for c in range(8):
        s, h = c // 2, c % 2
        t0 = h * NP
        xT = np.zeros((2, D, NC), np.float32)
        pT = np.zeros((2, 2, 256, NC), np.float32)
        if h == 1:
            xT[0][:, C_P0:C_S0] = xp[s, 0:NP].T
            for l in range(2):
                pT[0, l][:, C_P0:C_S0] = pp[l, s, 0:NP].T
        xT[1][:, C_P0:C_S0] = xp[s, t0:t0 + NP].T
        xT[1][:, C_S0:NC] = xs[c * NS:(c + 1) * NS, 0].T
        for l in range(2):
            pT[1, l][:, C_P0:C_S0] = pp[l, s, t0:t0 + NP].T
            pT[1, l][:, C_S0:NC] = ps[l, c * NS:(c + 1) * NS, 0].T
        cst = sconv[0, c * NS:(c + 1) * NS]
        convst = np.ascontiguousarray(cst.reshape(NS, 30, 16, 128).transpose(3, 2, 0, 1))
        fst = sffn[:, c * NS:(c + 1) * NS]
        ffnst = np.ascontiguousarray(fst.reshape(2, NS, 2, NFT, 128).transpose(0, 4, 3, 1, 2))
        m = dict(shared)
        m["cS"] = np.ascontiguousarray(np.asarray(inp["state_mlstm_c"], np.float32)[0, c * NS:(c + 1) * NS])
        m["nS"] = np.ascontiguousarray(np.asarray(inp["state_mlstm_n"], np.float32)[0, c * NS:(c + 1) * NS])
        m["mS"] = np.ascontiguousarray(np.asarray(inp["state_mlstm_m"], np.float32)[0, c * NS:(c + 1) * NS])
        m.update({"xT": xT, "pT": pT, "vecs": pack_vecs(inp, float(h)), "convst": convst, "ffnst": ffnst})
        maps.append(m)
    return maps


_PROG_CACHE = {}


def run_device(inp, n_layers=2):
    if n_layers not in _PROG_CACHE:
        _PROG_CACHE[n_layers] = Prog(n_layers).build()
    nc = _PROG_CACHE[n_layers]
    maps = make_in_maps(inp)
    if n_layers == 1:
        for m in maps:
            for k in ("ml_w_in", "ml_w_out", "consts", "bg_rep", "hg_rep", "id16", "cS", "nS", "mS"):
                m.pop(k, None)
    res = run_bass_kernel_spmd(nc, maps, core_ids=list(range(8)))
    return res.results


def assemble(results):
    y_prompt = np.zeros((4, 2048, D), np.float32)
    y_sample = np.zeros((128, 1, D), np.float32)
    conv_p = np.zeros((1, 4, 30, D), np.float32)
    conv_s = np.zeros((1, 128, 30, D), np.float32)
    ffn_p = np.zeros((2, 4, 2, DFF), np.float32)
    ffn_s = np.zeros((2, 128, 2, DFF), np.float32)
    for c in range(8):
        s, h = c // 2, c % 2
        r = results[c]
        yT = r["yT"].transpose(1, 0, 2).reshape(D, NC)
        y_prompt[s, h * NP:(h + 1) * NP] = yT[:, C_P0:C_S0].T
        y_sample[c * NS:(c + 1) * NS, 0] = yT[:, C_S0:NC].T
        conv_s[0, c * NS:(c + 1) * NS] = r["convS"].transpose(2, 3, 1, 0).reshape(NS, 30, D)
        ffn_s[:, c * NS:(c + 1) * NS] = r["ffnS"].transpose(0, 3, 4, 2, 1).reshape(2, NS, 2, DFF)
        if h == 1:
            conv_p[0, s] = r["convP"].transpose(2, 1, 0).reshape(30, D)
            ffn_p[:, s] = r["ffnP"].transpose(0, 3, 2, 1).reshape(2, 2, DFF)
    return y_prompt, y_sample, conv_p, conv_s, ffn_p, ffn_s


def kernel(**inputs):
    results = run_device(inputs, 2)
    y_prompt, y_sample, conv_p, conv_s, ffn_p, ffn_s = assemble(results)
    c_p = np.zeros((1, 4, NH, DK, DV), np.float32)
    n_p = np.zeros((1, 4, NH, DK), np.float32)
    m_p = np.zeros((1, 4, NH), np.float32)
    for c in range(8):
        s_, h = c // 2, c % 2
        if h == 1:
            cp = results[c]["cP"]
            c_p[0, s_] = cp[:, :, 0:DV].transpose(1, 0, 2)
            n_p[0, s_] = cp[:, :, DV].T
            m_p[0, s_] = results[c]["mP"][0]
    c_s = np.zeros((1, 128, NH, DK, DV), np.float32)
    n_s = np.zeros((1, 128, NH, DK), np.float32)
    m_s = np.zeros((1, 128, NH), np.float32)
    for c in range(8):
        c_s[0, c * NS:(c + 1) * NS] = results[c]["cSo"]
        n_s[0, c * NS:(c + 1) * NS] = results[c]["nSo"]
        m_s[0, c * NS:(c + 1) * NS] = results[c]["mSo"]
    return (y_prompt, y_sample, conv_p, conv_s, c_p, n_p, m_p, c_s, n_s, m_s, ffn_p, ffn_s)
```
